# Optimizing a Trainium2 kernel written in Bass

```python
import jax
import jax.numpy as jnp
from jax import lax
import numpy as np

D_MODEL = 1024
BATCH = 8
SEQ = 2048
DEPTH = 4
DEC_BATCH = 128
DEC_SEQ = 1
PAST_LEN = 16384
PAGE_SIZE = 128

N_MIXERS = 4
N_A = (DEPTH + 3) // 4
N_B = (DEPTH + 2) // 4
N_C = (DEPTH + 1) // 4
N_D = DEPTH // 4
D_FF = 4 * D_MODEL
NORM_EPS = 1e-6

RWKV_HEAD = 64
RWKV_H = D_MODEL // RWKV_HEAD
RWKV_LORA_W = 64
RWKV_LORA_A = 64
RWKV_LORA_G = 160
RWKV_GN_EPS = 64e-5

GLA_H = 4
GLA_DK_TOT = D_MODEL // 2
GLA_DV_TOT = D_MODEL
GLA_DK = GLA_DK_TOT // GLA_H
GLA_DV = GLA_DV_TOT // GLA_H
GLA_LR = 16
GLA_NORMALIZER = 16.0
GLA_CHUNK = 64
GLA_NORM_EPS = 1e-5

CONV_W = 3

POOL_WINDOWS = (2, 4, 8, 16)
POOL_G = D_MODEL // len(POOL_WINDOWS)
POOL_BUF = max(POOL_WINDOWS) - 1

kernel_name = 'hybrid_rwkv7_gla_conv_pool_decoder_step'


def rmsnorm(x, g, eps):
    xf = x.astype(jnp.float32)
    y = xf * lax.rsqrt(jnp.mean(xf * xf, axis=-1, keepdims=True) + eps)
    return (y * g.astype(jnp.float32)).astype(x.dtype)


def sqrelu_mlp(x, w_up, w_down):
    return jnp.square(jax.nn.relu(x @ w_up)) @ w_down


def rwkv7_mix(u, shift_buf, S0, mu, w_rkv, w0, w1, w2, a0, a1, a2, g1, g2, k_k, k_a, r_k, ln_w, ln_b, wo):
    B, L, D = u.shape
    f32 = jnp.float32
    prev = jnp.concatenate([shift_buf[:, None, :].astype(u.dtype), u[:, :-1]], axis=1)
    xm = u[None] + (prev - u)[None] * mu[:, None, None, :].astype(u.dtype)
    xr, xw, xk, xv, xa, xg = xm[0], xm[1], xm[2], xm[3], xm[4], xm[5]
    rkv = jnp.einsum('nbld,nde->nble', jnp.stack([xr, xk, xv]), w_rkv)
    r = rkv[0].astype(f32)
    k = rkv[1].astype(f32)
    v = rkv[2].astype(f32)
    w_log = -jax.nn.softplus(-(w0 + jnp.tanh(xw @ w1) @ w2).astype(f32)) - 0.5
    decay = jnp.exp(-jnp.exp(w_log))
    a = jax.nn.sigmoid((a0 + (xa @ a1) @ a2).astype(f32))
    g = (jax.nn.sigmoid(xg @ g1) @ g2).astype(f32)

    def heads(t):
        return t.reshape(B, L, RWKV_H, RWKV_HEAD)

    kk = heads(k * k_k.astype(f32))
    kk = kk / jnp.maximum(jnp.sqrt(jnp.sum(kk * kk, axis=-1, keepdims=True)), 1e-12)
    k = k * (1.0 + (a - 1.0) * k_a.astype(f32))
    r_h, k_h, v_h, w_h, a_h = heads(r), heads(k), heads(v), heads(decay), heads(a)
    b_h = kk * a_h
    seq = tuple(jnp.moveaxis(t, 1, 0) for t in (r_h, w_h, k_h, v_h, -kk, b_h))

    def step(S, inp):
        r_t, w_t, k_t, v_t, a_t, b_t = inp
        sa = jnp.einsum('bhvk,bhk->bhv', S, a_t)
        S = S * w_t[:, :, None, :] + sa[..., None] * b_t[:, :, None, :] + v_t[..., None] * k_t[:, :, None, :]
        return S, jnp.einsum('bhvk,bhk->bhv', S, r_t)

    S_fin, y = lax.scan(step, S0.astype(f32), seq)
    y = jnp.moveaxis(y, 0, 1)
    mean = jnp.mean(y, axis=-1, keepdims=True)
    var = jnp.mean(jnp.square(y - mean), axis=-1, keepdims=True)
    yn = ((y - mean) * lax.rsqrt(var + RWKV_GN_EPS)).reshape(B, L, D) * ln_w.astype(f32) + ln_b.astype(f32)
    bonus = (jnp.sum(r_h * k_h * r_k.astype(f32), axis=-1, keepdims=True) * v_h).reshape(B, L, D)
    out = ((yn + bonus) * g).astype(u.dtype) @ wo
    return out, S_fin.astype(S0.dtype), u[:, -1]


def gla_chunked(q, k, v, gk, S0):
    B, L, H, DK = q.shape
    DV = v.shape[-1]
    C = min(GLA_CHUNK, L)
    n = -(-L // C)
    pad = n * C - L
    f32 = jnp.float32

    def blocks(t):
        t = jnp.pad(t.astype(f32), ((0, 0), (0, pad), (0, 0), (0, 0)))
        return jnp.moveaxis(t.reshape(B, n, C, H, t.shape[-1]), 1, 0)

    mask = jnp.tril(jnp.ones((C, C), dtype=bool))[None, :, :, None, None]

    def step(S, inp):
        qc, kc, vc, gc = inp
        b = jnp.cumsum(gc, axis=1)
        o_inter = jnp.einsum('bchk,bhkv->bchv', qc * jnp.exp(b), S)
        diff = b[:, :, None] - b[:, None, :]
        dec = jnp.exp(jnp.where(mask, diff, -jnp.inf))
        att = jnp.einsum('btshk,bthk,bshk->bhts', dec, qc, kc)
        o_intra = jnp.einsum('bhts,bshv->bthv', att, vc)
        b_last = b[:, -1]
        S = jnp.exp(b_last)[..., None] * S + jnp.einsum('bshk,bshv->bhkv', kc * jnp.exp(b_last[:, None] - b), vc)
        return S, o_inter + o_intra

    S_fin, o = lax.scan(step, S0.astype(f32), (blocks(q), blocks(k), blocks(v), blocks(gk)))
    o = jnp.moveaxis(o, 0, 1).reshape(B, n * C, H, DV)[:, :L]
    return o, S_fin


def gla_mix(u, S0, w_in, w_gk2, b_gk, norm_w, wo):
    B, L, D = u.shape
    z = u @ w_in
    q, k, v, g, gl = jnp.split(z, [GLA_DK_TOT, 2 * GLA_DK_TOT, 2 * GLA_DK_TOT + GLA_DV_TOT, 2 * GLA_DK_TOT + 2 * GLA_DV_TOT], axis=-1)
    gk = jax.nn.log_sigmoid((gl @ w_gk2 + b_gk).astype(jnp.float32)) / GLA_NORMALIZER
    q = q.reshape(B, L, GLA_H, GLA_DK) * (GLA_DK ** -0.5)
    k = k.reshape(B, L, GLA_H, GLA_DK)
    v = v.reshape(B, L, GLA_H, GLA_DV)
    gk = gk.reshape(B, L, GLA_H, GLA_DK)
    o, S_fin = gla_chunked(q, k, v, gk, S0)
    o = rmsnorm(o, norm_w, GLA_NORM_EPS) * jax.nn.silu(g.reshape(B, L, GLA_H, GLA_DV).astype(jnp.float32))
    out = o.reshape(B, L, GLA_DV_TOT).astype(u.dtype) @ wo
    return out, S_fin.astype(S0.dtype)


def conv_mix(u, buf, w_in, w_conv, wo):
    L = u.shape[1]
    gB, gC, h = jnp.split(u @ w_in, 3, axis=-1)
    zc = jnp.concatenate([buf.astype(u.dtype), gC * h], axis=1)
    conv = sum(w_conv[j] * zc[:, j:j + L] for j in range(CONV_W))
    out = (gB * conv) @ wo
    return out, zc[:, -(CONV_W - 1):]


def pool_mix(u, buf, pos0, pool_w, scale):
    B, L, D = u.shape
    zc = jnp.concatenate([buf.astype(u.dtype), u], axis=1)
    zf = zc.astype(jnp.float32)
    cs = jnp.concatenate([jnp.zeros((B, 1, D), jnp.float32), jnp.cumsum(zf, axis=1)], axis=1)
    pos = pos0 + jnp.arange(L)
    uf = u.astype(jnp.float32)
    ys = []
    for gi, w in enumerate(POOL_WINDOWS):
        sl = slice(gi * POOL_G, (gi + 1) * POOL_G)
        s = cs[:, POOL_BUF + 1:POOL_BUF + 1 + L, sl] - cs[:, POOL_BUF + 1 - w:POOL_BUF + 1 - w + L, sl]
        cnt = jnp.minimum(w, pos + 1).astype(jnp.float32)
        d = s / cnt[None, :, None] - uf[:, :, sl]
        ys.append(d.astype(u.dtype) @ pool_w[gi])
    out = jnp.concatenate(ys, axis=-1) * scale
    return out, zc[:, -POOL_BUF:]


def setup_inputs(seed: int = 0) -> dict:
    key = jax.random.key(seed)
    ks = iter(jax.random.split(key, 64))
    D = D_MODEL

    def nrm(shape, scale):
        return scale * jax.random.normal(next(ks), shape, jnp.float32)

    def uni(shape, lo, hi):
        return jax.random.uniform(next(ks), shape, jnp.float32, lo, hi)

    inp = {}
    inp['x_prompt'] = nrm((BATCH, SEQ, D), 1.0)
    inp['x_sample'] = nrm((DEC_BATCH, DEC_SEQ, D), 1.0)
    inp['state_rwkv_wkv'] = nrm((N_A, DEC_BATCH, RWKV_H, RWKV_HEAD, RWKV_HEAD), 0.3)
    inp['state_rwkv_shift'] = nrm((N_A, DEC_BATCH, D), 1.0)
    inp['state_gla'] = nrm((N_B, DEC_BATCH, GLA_H, GLA_DK, GLA_DV), 0.3)
    inp['state_conv'] = nrm((N_C, DEC_BATCH, CONV_W - 1, D), 1.0)
    inp['state_pool'] = nrm((N_D, DEC_BATCH, POOL_BUF, D), 1.0)
    inp['norm_mix'] = 1.0 + nrm((DEPTH, D), 0.05)
    inp['norm_ffn'] = 1.0 + nrm((DEPTH, D), 0.05)
    inp['norm_final'] = 1.0 + nrm((D,), 0.05)
    inp['ffn_up'] = nrm((DEPTH, D, D_FF), D ** -0.5)
    inp['ffn_down'] = nrm((DEPTH, D_FF, D), 0.5 * D_FF ** -0.5)
    inp['rwkv_mu'] = uni((N_A, 6, D), 0.0, 1.0)
    inp['rwkv_w_rkv'] = nrm((N_A, 3, D, D), D ** -0.5)
    inp['rwkv_w0'] = uni((N_A, D), -6.0, 1.0)
    inp['rwkv_w1'] = nrm((N_A, D, RWKV_LORA_W), D ** -0.5)
    inp['rwkv_w2'] = nrm((N_A, RWKV_LORA_W, D), 0.1 * RWKV_LORA_W ** -0.5)
    inp['rwkv_a0'] = nrm((N_A, D), 0.1)
    inp['rwkv_a1'] = nrm((N_A, D, RWKV_LORA_A), D ** -0.5)
    inp['rwkv_a2'] = nrm((N_A, RWKV_LORA_A, D), 0.1 * RWKV_LORA_A ** -0.5)
    inp['rwkv_g1'] = nrm((N_A, D, RWKV_LORA_G), D ** -0.5)
    inp['rwkv_g2'] = nrm((N_A, RWKV_LORA_G, D), RWKV_LORA_G ** -0.5)
    inp['rwkv_k_k'] = 0.85 + nrm((N_A, D), 0.05)
    inp['rwkv_k_a'] = 1.0 + nrm((N_A, D), 0.05)
    inp['rwkv_r_k'] = nrm((N_A, RWKV_H, RWKV_HEAD), 0.1)
    inp['rwkv_ln_w'] = 1.0 + nrm((N_A, D), 0.05)
    inp['rwkv_ln_b'] = nrm((N_A, D), 0.02)
    inp['rwkv_wo'] = nrm((N_A, D, D), 0.5 * D ** -0.5)
    inp['gla_w_in'] = nrm((N_B, D, 2 * GLA_DK_TOT + 2 * GLA_DV_TOT + GLA_LR), D ** -0.5)
    inp['gla_w_gk2'] = nrm((N_B, GLA_LR, GLA_DK_TOT), GLA_LR ** -0.5)
    inp['gla_b_gk'] = nrm((N_B, GLA_DK_TOT), 0.1)
    inp['gla_norm'] = 1.0 + nrm((N_B, GLA_DV), 0.05)
    inp['gla_wo'] = nrm((N_B, GLA_DV_TOT, D), 0.5 * GLA_DV_TOT ** -0.5)
    inp['conv_w_in'] = nrm((N_C, D, 3 * D), D ** -0.5)
    inp['conv_w'] = nrm((N_C, CONV_W, D), 0.5)
    inp['conv_wo'] = nrm((N_C, D, D), 0.5 * D ** -0.5)
    inp['pool_w'] = nrm((N_D, len(POOL_WINDOWS), POOL_G, POOL_G), POOL_G ** -0.5)
    inp['pool_scale'] = uni((N_D, D), 0.5, 1.5)
    return inp


def reference(x_prompt, x_sample, state_rwkv_wkv, state_rwkv_shift, state_gla, state_conv, state_pool,
              norm_mix, norm_ffn, norm_final, ffn_up, ffn_down,
              rwkv_mu, rwkv_w_rkv, rwkv_w0, rwkv_w1, rwkv_w2, rwkv_a0, rwkv_a1, rwkv_a2, rwkv_g1, rwkv_g2,
              rwkv_k_k, rwkv_k_a, rwkv_r_k, rwkv_ln_w, rwkv_ln_b, rwkv_wo,
              gla_w_in, gla_w_gk2, gla_b_gk, gla_norm, gla_wo,
              conv_w_in, conv_w, conv_wo,
              pool_w, pool_scale):

    def trunk(h, st_wkv, st_shift, st_gla, st_conv, st_pool, pos0):
        n_wkv, n_shift, n_gla, n_conv, n_pool = [], [], [], [], []
        for i in range(DEPTH):
            j = i // N_MIXERS
            kind = i % N_MIXERS
            u = rmsnorm(h, norm_mix[i], NORM_EPS)
            if kind == 0:
                out, s_new, sh_new = rwkv7_mix(u, st_shift[j], st_wkv[j], rwkv_mu[j], rwkv_w_rkv[j], rwkv_w0[j],
                                               rwkv_w1[j], rwkv_w2[j], rwkv_a0[j], rwkv_a1[j], rwkv_a2[j],
                                               rwkv_g1[j], rwkv_g2[j], rwkv_k_k[j], rwkv_k_a[j], rwkv_r_k[j],
                                               rwkv_ln_w[j], rwkv_ln_b[j], rwkv_wo[j])
                n_wkv.append(s_new)
                n_shift.append(sh_new.astype(st_shift.dtype))
            elif kind == 1:
                out, s_new = gla_mix(u, st_gla[j], gla_w_in[j], gla_w_gk2[j], gla_b_gk[j], gla_norm[j], gla_wo[j])
                n_gla.append(s_new)
            elif kind == 2:
                out, c_new = conv_mix(u, st_conv[j], conv_w_in[j], conv_w[j], conv_wo[j])
                n_conv.append(c_new.astype(st_conv.dtype))
            else:
                out, p_new = pool_mix(u, st_pool[j], pos0, pool_w[j], pool_scale[j])
                n_pool.append(p_new.astype(st_pool.dtype))
            h = h + out.astype(h.dtype)
            h = h + sqrelu_mlp(rmsnorm(h, norm_ffn[i], NORM_EPS), ffn_up[i], ffn_down[i]).astype(h.dtype)
        y = rmsnorm(h, norm_final, NORM_EPS)
        return y, jnp.stack(n_wkv), jnp.stack(n_shift), jnp.stack(n_gla), jnp.stack(n_conv), jnp.stack(n_pool)

    dt = x_prompt.dtype
    y_p, wkv_p, sh_p, gla_p, conv_p, pool_p = trunk(
        x_prompt,
        jnp.zeros((N_A, BATCH, RWKV_H, RWKV_HEAD, RWKV_HEAD), dt),
        jnp.zeros((N_A, BATCH, D_MODEL), dt),
        jnp.zeros((N_B, BATCH, GLA_H, GLA_DK, GLA_DV), dt),
        jnp.zeros((N_C, BATCH, CONV_W - 1, D_MODEL), dt),
        jnp.zeros((N_D, BATCH, POOL_BUF, D_MODEL), dt),
        0)
    y_s, wkv_s, sh_s, gla_s, conv_s, pool_s = trunk(
        x_sample, state_rwkv_wkv, state_rwkv_shift, state_gla, state_conv, state_pool, PAST_LEN)
    return (y_p, y_s, wkv_p, wkv_s, sh_p, sh_s, gla_p, gla_s, conv_p, conv_s, pool_p, pool_s)
```

```python
import numpy as np
import contextlib
import concourse.bass as bass
import concourse.mybir as mybir
from concourse.bass_utils import run_bass_kernel_spmd

F32 = mybir.dt.float32
BF16 = mybir.dt.bfloat16
AF = mybir.ActivationFunctionType
ALU = mybir.AluOpType
AX = mybir.AxisListType

D = 1024
NCH = 8
DFF = 4096
NS = 16
NORM_EPS = 1e-6


class Sched:
    ENGS = ["tensor", "vector", "scalar", "gpsimd", "sync"]

    def __init__(self, nc, stack):
        self.nc = nc
        self.stack = stack
        self.ops = {e: [] for e in self.ENGS}
        self.semobj = {e: stack.enter_context(nc.semaphore("s_" + e)) for e in self.ENGS}
        self.cnt = {e: 0 for e in self.ENGS}
        self.waited = {e: {} for e in self.ENGS}
        self.W = {}
        self.R = {}
        self.dcnt = {}
        self.dkeys = {}

    def dma_sem(self, name):
        if name in self.dkeys:
            return self.dkeys[name]
        key = "d%d" % len(self.dkeys)
        self.semobj[key] = self.stack.enter_context(self.nc.semaphore(key))
        self.dcnt[key] = 0
        self.dkeys[name] = key
        return key

    def _deps(self, eng, reads, writes, acc=False):
        need = {}

        def add(k, v):
            if need.get(k, 0) < v:
                need[k] = v
        for r in reads:
            if r in self.W:
                add(*self.W[r])
        for w in writes:
            if w in self.W and not (acc and self.W[w][0] == eng):
                add(*self.W[w])
            for k, v in self.R.get(w, {}).items():
                add(k, v)
        out = []
        for k, v in need.items():
            if self.waited[eng].get(k, 0) >= v:
                continue
            self.waited[eng][k] = v
            out.append((k, v))
        return out

    def op(self, eng, meth, kw, reads=(), writes=(), acc=False):
        waits = self._deps(eng, reads, writes, acc)
        self.cnt[eng] += 1
        v = self.cnt[eng]
        self.ops[eng].append((waits, (meth, kw), (eng, 1)))
        for r in reads:
            self.R.setdefault(r, {})[eng] = v
        for w in writes:
            self.W[w] = (eng, v)
            self.R[w] = {}

    def dma(self, eng, semname, out, in_, reads=(), writes=()):
        semkey = self.dma_sem(semname)
        waits = self._deps(eng, reads, writes)
        self.dcnt[semkey] += 16
        v = self.dcnt[semkey]
        self.ops[eng].append((waits, ("dma_start", dict(out=out, in_=in_)), (semkey, 16)))
        for r in reads:
            self.R.setdefault(r, {})[semkey] = v
        for w in writes:
            self.W[w] = (semkey, v)
            self.R[w] = {}

    def barrier(self):
        tot = {}
        for e in self.ENGS:
            if self.cnt[e]:
                tot[e] = self.cnt[e]
        for k, v in self.dcnt.items():
            if v:
                tot[k] = v
        for e in self.ENGS:
            waits = []
            for k, v in tot.items():
                if self.waited[e].get(k, 0) >= v:
                    continue
                self.waited[e][k] = v
                waits.append((k, v))
            if waits:
                self.ops[e].append((waits, None, None))

    def replay(self, block):
        for e in self.ENGS:
            ops = self.ops[e]
            if not ops:
                continue

            def body(engine, ops=ops):
                for waits, fn, inc in ops:
                    for k, v in waits:
                        engine.wait_ge(self.semobj[k], v)
                    if fn is not None:
                        try:
                            ins = getattr(engine, fn[0])(**fn[1])
                        except Exception:
                            print("FAILED OP:", fn[0], {k: (getattr(v, "tensor", None) and v.tensor.name, getattr(v, "shape", v)) for k, v in fn[1].items()})
                            raise
                        ins.then_inc(self.semobj[inc[0]], inc[1])
            getattr(block, e)(body)


VEC_COLS = {}


def _vec_layout():
    names = []
    for i in range(4):
        names.append("norm_mix%d" % i)
    for i in range(4):
        names.append("norm_ffn%d" % i)
    names.append("norm_final")
    for j in range(6):
        names.append("mu%d" % j)
    for j in range(6):
        names.append("omu%d" % j)
    names += ["w0", "a0", "k_k", "k_a", "omk_a", "r_k", "ln_w", "ln_b",
              "conv_w0", "conv_w1", "conv_w2", "pool_scale"]
    col = 0
    for n in names:
        VEC_COLS[n] = col
        col += 8
    VEC_COLS["gla_b_gk"] = col
    col += 4
    VEC_COLS["gla_norm"] = col
    col += 2
    VEC_COLS["invc"] = col
    col += 16
    return col


NVEC = _vec_layout()


def make_cst():
    c = np.zeros((128, Builder.NCST), np.float32)
    c[:, Builder.C_ID:Builder.C_ID + 128] = np.eye(128, dtype=np.float32)
    c[:, Builder.C_TRIU:Builder.C_TRIU + 128] = np.triu(np.ones((128, 128), np.float32))
    su = np.triu(np.ones((64, 64), np.float32), 1)
    iu = np.triu(np.ones((64, 64), np.float32), 0)
    z = np.zeros((64, 64), np.float32)
    c[:, Builder.C_SUIU:Builder.C_SUIU + 128] = np.block([[su, z], [z, su]])
    c[:, Builder.C_SUIU + 128:Builder.C_SUIU + 256] = np.block([[iu, z], [z, iu]])
    c[:, Builder.C_SL:Builder.C_SL + 128] = np.block([[su.T, z], [z, su.T]])
    c[:, Builder.C_I2:Builder.C_I2 + 64] = np.concatenate([np.eye(64, dtype=np.float32)] * 2, 0)
    return c


class Builder:
    C_ID = 0
    C_TRIU = 128
    C_SUIU = 256
    C_SL = 512
    C_I2 = 640
    NCST = 704

    def __init__(self, L, layers=(1, 1, 1, 1), ffn=True, dbg=None):
        self.L = L
        self.NT = L + NS
        self.layers = layers
        self.ffn = ffn
        self.dbg = dbg
        tiles = []
        t = 0
        while t < L:
            n = min(512, L - t)
            tiles.append((t, n))
            t += n
        tiles.append((L, NS))
        self.tiles = tiles

    def sb(self, name, shape, dt, stack=None):
        self._uid = getattr(self, "_uid", 0) + 1
        return (stack or self.st).enter_context(self.nc.sbuf_tensor("%s_%d" % (name, self._uid), list(shape), dt))

    def dram_in(self, name, shape):
        return self.nc.dram_tensor(name, list(shape), F32, kind="ExternalInput").ap()

    def dram_out(self, name, shape):
        return self.nc.dram_tensor(name, list(shape), F32, kind="ExternalOutput").ap()

    def vcol(self, name, c=0):
        k = VEC_COLS[name] + c
        return self.vec[:, k:k + 1]

    def mm(self, out, lhsT, rhs, start, stop, reads, writes, nosw=False):
        self.S.op("tensor", "matmul", dict(out=out, lhsT=lhsT, rhs=rhs, start=start, stop=stop),
                  reads=reads, writes=writes, acc=(not start) or nosw)

    def act(self, out, in_, func, reads, writes, scale=1.0, bias=None, eng="scalar"):
        kw = dict(out=out, in_=in_, func=func, scale=scale)
        if bias is not None:
            kw["bias"] = bias
        self.S.op("scalar", "activation", kw, reads=reads, writes=writes)

    def tt(self, eng, out, in0, in1, op, reads, writes):
        self.S.op(eng, "tensor_tensor", dict(out=out, in0=in0, in1=in1, op=op), reads=reads, writes=writes)

    def ts(self, eng, out, in0, s1, op0, reads, writes, s2=None, op1=None):
        kw = dict(out=out, in0=in0, scalar1=s1, scalar2=s2, op0=op0)
        if op1 is not None:
            kw["op1"] = op1
        self.S.op(eng, "tensor_scalar", kw, reads=reads, writes=writes)

    def stt(self, out, in0, scalar, in1, op0, op1, reads, writes):
        self.S.op("vector", "scalar_tensor_tensor", dict(out=out, in0=in0, scalar=scalar, in1=in1, op0=op0, op1=op1),
                  reads=reads, writes=writes)

    def cp(self, eng, out, in_, reads, writes):
        self.S.op(eng, "tensor_copy", dict(out=out, in_=in_), reads=reads, writes=writes)

    def memset(self, eng, ap, val, writes):
        self.S.op(eng, "memset", dict(ap=ap, constant=val), writes=writes)

    def rms_rstd(self, ti, rstd, rstd_res):
        t0, n = self.tiles[ti]
        sq, pb = self.sq, self.ps[6]
        h = self.h
        self.act(sq[:, :, :n], h[:, :, t0:t0 + n], AF.Square, ["h%d" % ti], ["sq"])
        for c in range(NCH):
            self.mm(pb[:, :n], self.ones_bf[:, :], sq[:, c, :n], c == 0, c == NCH - 1, ["sq", "ones"], ["ps6"])
        self.act(rstd[:, :n], pb[:, :n], AF.Ln, ["ps6", "consts"], [rstd_res], scale=1.0 / D, bias=self.eps_t[:, 0:1])
        self.act(rstd[:, :n], rstd[:, :n], AF.Exp, [rstd_res], [rstd_res], scale=-0.5)

    def norm_tile(self, ti, gname, dst_fn, dst_res):
        t0, n = self.tiles[ti]
        rstd = self.rstd
        self.rms_rstd(ti, rstd, "rstd")
        h = self.h
        for c in range(NCH):
            self.stt(dst_fn(c), h[:, c, t0:t0 + n], self.vcol(gname, c), rstd[:, :n], ALU.mult, ALU.mult,
                     ["h%d" % ti, "rstd", "vec"], [dst_res])

    def load_w(self, dst, src, res, nsplit=1):
        k = dst.shape[1]
        srcv = src.rearrange("(k p) n -> p k n", p=128)
        step = max(1, k // nsplit)
        for a in range(0, k, step):
            b = min(k, a + step)
            self.S.dma("gpsimd", res, dst[:, a:b, :], srcv[:, a:b, :], writes=[res])

    def ffn_layer(self, i):
        S = self.S
        with contextlib.ExitStack() as ph:
            wup = [self.sb("wup%d" % b, [128, NCH, 1024], BF16, ph) for b in range(2)]
            wdn = [self.sb("wdn%d" % b, [128, NCH, 1024], BF16, ph) for b in range(2)]
            hid = self.sb("hid", [128, NCH, 512], BF16, ph)
            rtmp = [self.sb("rtmp%d" % b, [128, 512], BF16, ph) for b in range(2)]
            ub = self.ubuf
            h = self.h
            def ffn_load(q):
                b = q % 2
                self.load_w(wup[b][:, :, :], self.ffn_up[i, :, q * 1024:(q + 1) * 1024], "wup%d" % b, nsplit=2)
                self.load_w(wdn[b][:, :, :], self.ffn_down[i, q * 1024:(q + 1) * 1024, :], "wdn%d" % b, nsplit=2)
            ffn_load(0)
            for q in range(4):
                b = q % 2
                if q + 1 < 4:
                    ffn_load(q + 1)
                for ti, (t0, n) in enumerate(self.tiles):
                    if q == 0:
                        self.norm_tile(ti, "norm_ffn%d" % i, lambda c: ub[:, c, t0:t0 + n], "u%d" % ti)
                    for j in range(NCH):
                        pb = self.ps[j % 4]
                        pr = "ps%d" % (j % 4)
                        for c in range(NCH):
                            self.mm(pb[:, :n], wup[b][:, c, j * 128:(j + 1) * 128], ub[:, c, t0:t0 + n], c == 0, c == NCH - 1,
                                    ["wup%d" % b, "u%d" % ti], [pr])
                        rt = rtmp[j % 2]
                        self.act(rt[:, :n], pb[:, :n], AF.Relu, [pr], ["rtmp%d" % (j % 2)])
                        self.tt("gpsimd", hid[:, j, :n], rt[:, :n], rt[:, :n], ALU.mult, ["rtmp%d" % (j % 2)], ["hid%d" % j])
                    for d in range(NCH):
                        pb = self.ps[4 + d % 2]
                        pr = "ps%d" % (4 + d % 2)
                        for j in range(NCH):
                            self.mm(pb[:, :n], wdn[b][:, j, d * 128:(d + 1) * 128], hid[:, j, :n], j == 0, j == NCH - 1,
                                    ["wdn%d" % b, "hid%d" % j], [pr])
                        self.tt("vector", h[:, d, t0:t0 + n], h[:, d, t0:t0 + n], pb[:, :n], ALU.add, [pr, "h%d" % ti], ["h%d" % ti])
            S.barrier()

    def mix_norm_all(self, i, col0=0):
        ub = self.ubuf
        for ti, (t0, n) in enumerate(self.tiles):
            self.norm_tile(ti, "norm_mix%d" % i, lambda c: ub[:, c, col0 + t0:col0 + t0 + n], "u%d" % ti)

    def out_proj(self, xo, wo, wres, xres_fn):
        h = self.h
        for ti, (t0, n) in enumerate(self.tiles):
            for d in range(NCH):
                pb = self.ps[4 + d % 2]
                pr = "ps%d" % (4 + d % 2)
                for j in range(NCH):
                    self.mm(pb[:, :n], wo[:, j, d * 128:(d + 1) * 128], xo[:, j, t0:t0 + n], j == 0, j == NCH - 1,
                            [wres] + xres_fn(j, ti), [pr])
                self.tt("vector", h[:, d, t0:t0 + n], h[:, d, t0:t0 + n], pb[:, :n], ALU.add, [pr, "h%d" % ti], ["h%d" % ti])

    def rwkv_layer(self, i):
        S = self.S
        L, NT = self.L, self.NT
        h, ub, ps, vec = self.h, self.ubuf, self.ps, self.vec
        C = 64
        ntl = len(self.tiles)
        cst = self.cst
        V = VEC_COLS
        self.memset("vector", ub[:, :, 0:1], 0.0, ["ushift"])
        S.dma("gpsimd", "ushift", ub[:, :, NT + 1:NT + 1 + NS], self.shift_st.rearrange("(c p) r -> p c r", p=128), writes=["ushift"])
        with contextlib.ExitStack() as ph:
            sho = self.sb("sho", [128, NCH, 1 + NS], F32, ph)
            for ti, (t0, n) in enumerate(self.tiles):
                self.norm_tile(ti, "norm_mix%d" % i, lambda c: ub[:, c, 1 + t0:1 + t0 + n], "u%d" % ti)
                if t0 + n == L:
                    for c in range(NCH):
                        self.stt(sho[:, c, 0:1], h[:, c, L - 1:L], self.vcol("norm_mix%d" % i, c), self.rstd[:, n - 1:n], ALU.mult, ALU.mult,
                                 ["h%d" % ti, "rstd", "vec"], ["sho"])
                if t0 >= L:
                    for c in range(NCH):
                        self.stt(sho[:, c, 1:1 + NS], h[:, c, L:NT], self.vcol("norm_mix%d" % i, c), self.rstd[:, :NS], ALU.mult, ALU.mult,
                                 ["h%d" % ti, "rstd", "vec"], ["sho"])
            S.dma("sync", "sho", self.shift_o.rearrange("(c p) n -> p c n", p=128), sho[:, :, :], reads=["sho"])

            def u_ap(c, ti):
                t0, n = self.tiles[ti]
                return ub[:, c, 1 + t0:1 + t0 + n]

            def p_ap(c, ti):
                t0, n = self.tiles[ti]
                if t0 < L:
                    return ub[:, c, t0:t0 + n]
                return ub[:, c, NT + 1:NT + 1 + NS]

            def u_res(ti):
                return ["u%d" % ti, "ushift"] + (["u%d" % (ti - 1)] if ti > 0 else [])

            xv = self.sb("rxv", [128, 64], F32, ph)
            self.ts("vector", xv[:, 0:8], vec[:, V["w0"]:V["w0"] + 8], -1.0, ALU.mult, ["vec"], ["rxv"])
            self.ts("vector", xv[:, 8:16], vec[:, V["a0"]:V["a0"] + 8], -1.0, ALU.mult, ["vec"], ["rxv"])
            self.memset("vector", xv[:, 16:17], 1e-24, ["rxv"])
            for j in range(6):
                self.ts("vector", vec[:, V["omu%d" % j]:V["omu%d" % j] + 8], vec[:, V["mu%d" % j]:V["mu%d" % j] + 8], -1.0, ALU.mult,
                        ["vec"], ["vec"], s2=1.0, op1=ALU.add)
            self.ts("vector", vec[:, V["omk_a"]:V["omk_a"] + 8], vec[:, V["k_a"]:V["k_a"] + 8], -1.0, ALU.mult, ["vec"], ["vec"], s2=1.0, op1=ALU.add)
            onesblk = self.sb("onesblk", [128, 128], BF16, ph)
            self.memset("vector", onesblk[:, :], 0.0, ["onesblk"])
            self.memset("vector", onesblk[0:64, 0:64], 1.0, ["onesblk"])
            self.memset("vector", onesblk[64:128, 64:128], 1.0, ["onesblk"])
            i2b = self.sb("i2b", [128, 64], BF16, ph)
            self.cp("vector", i2b[:, :], cst[:, self.C_I2:self.C_I2 + 64], ["cst"], ["i2b"])
            cm64 = self.sb("cm64", [128, 512], F32, ph)
            self.memset("gpsimd", cm64[:, :], 1.0, ["cm64"])
            self.memset("gpsimd", cm64[:, :].rearrange("p (n k) -> p n k", k=64)[:, :, 0:1], 0.0, ["cm64"])

            stage = self.sb("rstage", [128, NCH, 160], F32, ph)
            self._eng_rr = 0

            def prep_w(src, m, muj, dA, dB, res):
                S.dma("sync", "rstage", stage[:, :, :m], src.rearrange("(c p) n -> p c n", p=128), writes=["rstage"])
                for c in range(NCH):
                    e = "vector"
                    self.ts(e, dA[:, c, :], stage[:, c, :m], vec[:, V["omu%d" % muj] + c:V["omu%d" % muj] + c + 1], ALU.mult, ["rstage", "vec"], [res])
                    self.ts(e, dB[:, c, :], stage[:, c, :m], vec[:, V["mu%d" % muj] + c:V["mu%d" % muj] + c + 1], ALU.mult, ["rstage", "vec"], [res])

            tw = self.sb("rtw", [64, NT], BF16, ph)
            ta = self.sb("rta", [64, NT], BF16, ph)
            tg0 = self.sb("rtg0", [128, NT], BF16, ph)
            tg1 = self.sb("rtg1", [32, NT], BF16, ph)
            tmpA = self.rstd
            with contextlib.ExitStack() as ph2:
                l1 = [self.sb("rl1_%d" % k, [128, NCH, m], BF16, ph2) for k, m in enumerate([64, 64, 64, 64, 160, 160])]
                prep_w(self.rwkv_w1, 64, 1, l1[0], l1[1], "rl1w")
                prep_w(self.rwkv_a1, 64, 4, l1[2], l1[3], "rl1a")
                prep_w(self.rwkv_g1, 160, 5, l1[4], l1[5], "rl1g")
                for ti, (t0, n) in enumerate(self.tiles):
                    specs = [(l1[0][:, :, :], l1[1][:, :, :], 64, "rl1w"), (l1[2][:, :, :], l1[3][:, :, :], 64, "rl1a"),
                             (l1[4][:, :, 0:128], l1[5][:, :, 0:128], 128, "rl1g"), (l1[4][:, :, 128:160], l1[5][:, :, 128:160], 32, "rl1g")]
                    for k, (wa, wb, m, res) in enumerate(specs):
                        for c in range(NCH):
                            self.mm(ps[k][:m, :n], wa[:, c, :], u_ap(c, ti), c == 0, False, [res] + u_res(ti), ["ps%d" % k])
                        for c in range(NCH):
                            self.mm(ps[k][:m, :n], wb[:, c, :], p_ap(c, ti), False, c == NCH - 1, [res] + u_res(ti), ["ps%d" % k])
                    self.act(tmpA[:64, :n], ps[0][:64, :n], AF.Exp, ["ps0"], ["rstd"], scale=-2.0)
                    self.ts("vector", tmpA[:64, :n], tmpA[:64, :n], 1.0, ALU.add, ["rstd"], ["rstd"])
                    S.op("vector", "reciprocal", dict(out=tmpA[:64, :n], in_=tmpA[:64, :n]), reads=["rstd"], writes=["rstd"])
                    self.ts("vector", tw[:, t0:t0 + n], tmpA[:64, :n], 2.0, ALU.mult, ["rstd"], ["rtw"], s2=-1.0, op1=ALU.add)
                    self.act(ta[:, t0:t0 + n], ps[1][:64, :n], AF.Copy, ["ps1"], ["rta"])
                    for k, (dst, m) in ((2, (tg0, 128)), (3, (tg1, 32))):
                        self.act(tmpA[:m, :n], ps[k][:m, :n], AF.Exp, ["ps%d" % k], ["rstd"], scale=-1.0)
                        self.ts("vector", tmpA[:m, :n], tmpA[:m, :n], 1.0, ALU.add, ["rstd"], ["rstd"])
                        S.op("vector", "reciprocal", dict(out=tmpA[:m, :n], in_=tmpA[:m, :n]), reads=["rstd"], writes=["rstd"])
                        self.cp("vector", dst[:, t0:t0 + n], tmpA[:m, :n], ["rstd"], ["rtg%d" % (k - 2)])
                S.barrier()

            F = [self.sb("rF%d" % k, [128, 512], F32, ph) for k in range(12)]
            r_f, k_f, v_f, lw, a_f, g_f, kkn, k2, bon, cl, E1, E3 = F
            E2 = k_f
            BB = [bon, self.rstd]
            GF = [self.sq[:, 2, :], self.sq[:, 3, :]]
            yfm_sb = g_f
            gam = self.sb("rgam", [128, 2, 16], F32, ph)
            sqb = self.sq[:, 0, :]
            wpr = [self.sb("rwp%d" % k, [128, NCH, 128], BF16, ph) for k in range(6)]
            w2c = self.sb("rw2c", [64, 128], BF16, ph)
            a2c = self.sb("ra2c", [64, 128], BF16, ph)
            g2c0 = self.sb("rg2c0", [128, 128], BF16, ph)
            g2c1 = self.sb("rg2c1", [32, 128], BF16, ph)
            woc = self.sb("rwoc", [128, 1024], BF16, ph)
            xot = self.sq[:, 1, :]
            NK = 8
            pads = {}
            for kind in ("p",):
                pads[kind] = dict(
                    AR=self.sb("rAR" + kind, [128, NK, 4, C], BF16, ph),
                    B=self.sb("rB" + kind, [128, NK, 2, C], BF16, ph),
                    K=self.sb("rK" + kind, [128, NK, 2, C], BF16, ph),
                    V=self.sb("rV" + kind, [128, NK, 2, C], BF16, ph),
                    BH=self.sb("rBH" + kind, [128, NK, 2, C], BF16, ph),
                    KH=self.sb("rKH" + kind, [128, NK, 2, C], BF16, ph))
                for nm, t_ in pads[kind].items():
                    self.memset("gpsimd", t_[:, :, :, :], 0.0, ["pad" + kind + nm])
            import os as _os
            WSL = int(_os.environ.get("RW_WSL", "2"))
            Gs = self.sb("rGs", [128, NK, 128], BF16, ph)
            Zb = self.sb("rZb", [128, NK, 64], BF16, ph)
            Rqs = self.sb("rRqs", [128, NK, 128], BF16, ph)
            Yz = self.sb("rYz", [128, NK, 64], BF16, ph)
            SL = []
            for s_ in range(WSL):
                SL.append(dict(
                    XQ=[self.sb("rXQ%d_%d" % (s_, b_), [128, 256], BF16, ph) for b_ in range(2)],
                    Xt=[self.sb("rXt%d_%d" % (s_, b_), [128, 128], BF16, ph) for b_ in range(2)],
                    MM2=self.sb("rMM2_%d" % s_, [128, 256], BF16, ph),
                    MRB=self.sb("rMRB_%d" % s_, [128, 128], BF16, ph),
                    BhT=self.sb("rBhT_%d" % s_, [128, 128], BF16, ph),
                    KhT=self.sb("rKhT_%d" % s_, [128, 128], BF16, ph),
                    AT=self.sb("rAT_%d" % s_, [128, 128], BF16, ph),
                    AqT=self.sb("rAqT_%d" % s_, [128, 128], BF16, ph),
                    Vst=self.sb("rVst_%d" % s_, [128, 64], BF16, ph),
                    MakV=self.sb("rMakV_%d" % s_, [128, 64], BF16, ph),
                    Vq=self.sb("rVq_%d" % s_, [128, 64], BF16, ph)))
            Tfs = [self.sb("rTf%d" % b_, [128, 64], F32, ph) for b_ in range(2)]
            Tbs = [self.sb("rTb%d" % b_, [128, 64], BF16, ph) for b_ in range(2)]
            Tf, Tb = Tfs[0], Tbs[0]
            NYB = 4
            Ypad = [self.sb("rYpad%d" % b_, [128, 2, 64], BF16, ph) for b_ in range(NYB)]
            for b_ in range(NYB):
                self.memset("gpsimd", Ypad[b_][:, :, :], 0.0, ["rYpad%d" % b_])
            bsts = [self.sb("rbst%d" % b_, [128, 6], F32, ph) for b_ in range(NYB)]
            bags = [self.sb("rbag%d" % b_, [128, 4], F32, ph) for b_ in range(NYB)]
            Ysb = [self.sb("rYsb%d" % b_, [128, 64], F32, ph) for b_ in range(NYB)]
            ysmp = self.sb("rysmp", [128, NS], F32, ph)
            MSUIU = cst[:, self.C_SUIU:self.C_SUIU + 256]
            MSU = cst[:, self.C_SUIU:self.C_SUIU + 128]
            MIU = cst[:, self.C_SUIU + 128:self.C_SUIU + 256]
            MSL = cst[:, self.C_SL:self.C_SL + 128]
            IDN = cst[:, self.C_ID:self.C_ID + 128]
            wkv_st = self.wkv_st

            for cc in range(NCH):
                cs = slice(cc * 128, (cc + 1) * 128)
                for k, (widx, muj) in enumerate(((0, 0), (1, 2), (2, 3))):
                    prep_w(self.rwkv_w_rkv[widx][:, cs], 128, muj, wpr[2 * k], wpr[2 * k + 1], "rwp%d" % k)
                S.dma("gpsimd", "rw2c", w2c[:, :], self.rwkv_w2[:, cs], writes=["rw2c"])
                S.dma("gpsimd", "ra2c", a2c[:, :], self.rwkv_a2[:, cs], writes=["ra2c"])
                S.dma("gpsimd", "rg2c0", g2c0[:, :], self.rwkv_g2[0:128, cs], writes=["rg2c"])
                S.dma("gpsimd", "rg2c1", g2c1[:, :], self.rwkv_g2[128:160, cs], writes=["rg2c"])
                S.dma("gpsimd", "rwoc", woc[:, :], self.rwkv_wo[cs, :], writes=["rwoc"])
                self.memset("vector", Tf[:, :], 0.0, ["rTf0"])
                self.memset("vector", Tb[:, :], 0.0, ["rTb0"])
                def p1a(ti, info):
                    t0, n = self.tiles[ti]
                    par = ti % 2
                    samp = t0 >= L
                    if False:
                        yield
                    for k, (dst_, dres_) in enumerate(((r_f, "r_f"), (k_f, "k_f"), (v_f, "v_f"))):
                        for c in range(NCH):
                            self.mm(ps[6][:, :n], wpr[2 * k][:, c, :], u_ap(c, ti), c == 0, False, ["rwp%d" % k] + u_res(ti), ["ps6"])
                            if c % 4 == 3:
                                yield
                        for c in range(NCH):
                            self.mm(ps[6][:, :n], wpr[2 * k + 1][:, c, :], p_ap(c, ti), False, c == NCH - 1, ["rwp%d" % k] + u_res(ti), ["ps6"])
                            if c % 4 == 3:
                                yield
                        self.act(dst_[:, :n], ps[6][:, :n], AF.Copy, ["ps6"], [dres_])
                        yield
                    self.mm(ps[6][:, :n], w2c[:, :], tw[:, t0:t0 + n], True, True, ["rw2c", "rtw"], ["ps6"])
                    self.act(lw[:, :n], ps[6][:, :n], AF.Exp, ["ps6", "rxv"], ["lw"], scale=-1.0, bias=xv[:, cc:cc + 1])
                    yield
                    self.mm(ps[6][:, :n], a2c[:, :], ta[:, t0:t0 + n], True, True, ["ra2c", "rta"], ["ps6"])
                    self.ts("vector", lw[:, :n], lw[:, :n], 1.0, ALU.add, ["lw"], ["lw"])
                    S.op("vector", "reciprocal", dict(out=lw[:, :n], in_=lw[:, :n]), reads=["lw"], writes=["lw"])
                    self.ts("vector", lw[:, :n], lw[:, :n], -0.6065306597126334, ALU.mult, ["lw"], ["lw"])
                    yield
                    yield
                    self.act(a_f[:, :n], ps[6][:, :n], AF.Exp, ["ps6", "rxv"], ["a_f"], scale=-1.0, bias=xv[:, 8 + cc:9 + cc])
                    yield
                    self.mm(ps[6][:, :n], g2c0[:, :], tg0[:, t0:t0 + n], True, False, ["rg2c", "rtg0"], ["ps6"])
                    self.mm(ps[6][:, :n], g2c1[:, :], tg1[:, t0:t0 + n], False, True, ["rg2c", "rtg1"], ["ps6"])
                    self.ts("vector", a_f[:, :n], a_f[:, :n], 1.0, ALU.add, ["a_f"], ["a_f"])
                    S.op("vector", "reciprocal", dict(out=a_f[:, :n], in_=a_f[:, :n]), reads=["a_f"], writes=["a_f"])
                    yield
                    self.act(GF[par][:, :n], ps[6][:, :n], AF.Copy, ["ps6"], ["GF%d" % par])
                    yield
                    self.ts("vector", kkn[:, :n], k_f[:, :n], self.vcol("k_k", cc), ALU.mult, ["k_f", "vec"], ["kkn"])
                    self.tt("gpsimd", sqb[:, :n], kkn[:, :n], kkn[:, :n], ALU.mult, ["kkn"], ["rsqb"])
                    yield
                    self.mm(ps[6][:, :n], onesblk[:, :], sqb[:, :n], True, True, ["onesblk", "rsqb"], ["ps6"])
                    yield
                    self.act(E1[:, :n], ps[6][:, :n], AF.Ln, ["ps6", "rxv"], ["E1"], bias=xv[:, 16:17])
                    self.act(E1[:, :n], E1[:, :n], AF.Exp, ["E1"], ["E1"], scale=-0.5)
                    self.tt("vector", kkn[:, :n], kkn[:, :n], E1[:, :n], ALU.mult, ["kkn", "E1"], ["kkn"])
                    self.ts("vector", k2[:, :n], a_f[:, :n], self.vcol("k_a", cc), ALU.mult, ["a_f", "vec"], ["k2"], s2=self.vcol("omk_a", cc), op1=ALU.add)
                    self.tt("vector", k2[:, :n], k2[:, :n], k_f[:, :n], ALU.mult, ["k2", "k_f"], ["k2"])
                    yield
                    self.stt(sqb[:, :n], r_f[:, :n], self.vcol("r_k", cc), k2[:, :n], ALU.mult, ALU.mult, ["r_f", "k2", "vec", "rsqb"], ["rsqb"])
                    yield
                    self.mm(ps[6][:, :n], onesblk[:, :], sqb[:, :n], True, True, ["onesblk", "rsqb"], ["ps6"])
                    yield
                    self.tt("vector", BB[par][:, :n], ps[6][:, :n], v_f[:, :n], ALU.mult, ["ps6", "v_f"], ["BB%d" % par])
                    self.ts("vector", BB[par][:, :n], BB[par][:, :n], self.vcol("ln_b", cc), ALU.add, ["BB%d" % par, "vec"], ["BB%d" % par])
                    yield
                    self.tt("gpsimd", a_f[:, :n], a_f[:, :n], kkn[:, :n], ALU.mult, ["a_f", "kkn"], ["a_f"])
                    bv = a_f
                    if not samp:
                        S.op("vector", "tensor_tensor_scan", dict(out=cl[:, :n], data0=cm64[:, :n], data1=lw[:, :n], initial=0.0,
                                                                  op0=ALU.mult, op1=ALU.add), reads=["lw", "cm64"], writes=["cl"])
                    else:
                        self.cp("vector", cl[:, :n], lw[:, :n], ["lw"], ["cl"])
                    yield
                    self.act(E1[:, :n], cl[:, :n], AF.Exp, ["cl"], ["E1"])
                    self.act(E2[:, :n], cl[:, :n], AF.Exp, ["cl", "k_f"], ["k_f"], scale=-1.0)
                    self.tt("gpsimd", E3[:, :n], cl[:, :n], lw[:, :n], ALU.subtract, ["cl", "lw"], ["E3"])
                    self.act(E3[:, :n], E3[:, :n], AF.Exp, ["E3"], ["E3"])
                    yield
                    E4 = lw
                    if not samp:
                        nck = n // C
                        for q in range(nck):
                            last = q * C + C - 1
                            self.ts("vector", E4[:, q * C:(q + 1) * C], cl[:, q * C:(q + 1) * C], -1.0, ALU.mult,
                                    ["cl", "lw", "E3"], ["lw"], s2=cl[:, last:last + 1], op1=ALU.add)
                        self.act(E4[:, :n], E4[:, :n], AF.Exp, ["lw"], ["lw"])
                        yield
                        self.cp("vector", gam[:, par, 0:nck], E1[:, 0:n].rearrange("p (q k) -> p q k", k=C)[:, :, C - 1], ["E1"], ["gam%d" % par])
                        groups = [("p", 0, nck)]
                    else:
                        self.memset("vector", E4[:, :n], 1.0, ["lw"])
                        self.cp("vector", gam[:, par, 0:NS], E1[:, 0:NS], ["E1"], ["gam%d" % par])
                        groups = [("s", 0, NK), ("s", NK, NK)]
                    info["groups"] = groups
                    yield

                infos = [dict() for _ in range(ntl)]
                g_first = p1a(0, infos[0])
                for _ in g_first:
                    pass
                for ti, (t0, n) in enumerate(self.tiles):
                    samp = t0 >= L
                    par = ti % 2
                    bv = a_f
                    E4 = lw
                    groups = infos[ti]["groups"]
                    nxt_gen = p1a(ti + 1, infos[ti + 1]) if ti + 1 < ntl else None
                    if samp:
                        nxt_gen = None
                    for (kind, g0, gn) in groups:
                        P = pads["p"]
                        if kind == "s":
                            for nm, t_ in P.items():
                                nb = 4 if nm == "AR" else 2
                                for hh in range(2):
                                    hs = slice(64 * hh, 64 * hh + 64)
                                    blks = [hh, 2 + hh] if nm == "AR" else [hh]
                                    for blk in blks:
                                        self.memset("gpsimd" if hh else "vector", t_[hs, :, blk, :], 0.0, ["padp" + nm])

                        def pv(t_, hh, blk):
                            hs = slice(64 * hh, 64 * hh + 64)
                            if kind == "p":
                                return t_[hs, 0:gn, blk, :]
                            return t_[hs, 0:gn, blk, 0]

                        def fv(t_, hh):
                            hs = slice(64 * hh, 64 * hh + 64)
                            if kind == "p":
                                return t_[hs, 0:n].rearrange("p (q k) -> p q k", k=C)
                            return t_[hs, g0:g0 + gn]
                        for hh in range(2):
                            e1 = "vector" if hh == 0 else "gpsimd"
                            self.stt(pv(P["AR"], hh, hh), fv(kkn, hh), -1.0, fv(E3, hh), ALU.mult, ALU.mult, ["kkn", "E3"], ["padpAR"])
                            self.tt(e1, pv(P["AR"], hh, 2 + hh), fv(r_f, hh), fv(E1, hh), ALU.mult, ["r_f", "E1"], ["padpAR"])
                            self.tt(e1, pv(P["B"], hh, hh), fv(bv, hh), fv(E2, hh), ALU.mult, ["a_f", "k_f"], ["padpB"])
                            self.tt(e1, pv(P["K"], hh, hh), fv(k2, hh), fv(E2, hh), ALU.mult, ["k2", "k_f"], ["padpK"])
                            self.cp(e1, pv(P["V"], hh, hh), fv(v_f, hh), ["v_f"], ["padpV"])
                            self.tt(e1, pv(P["BH"], hh, hh), fv(bv, hh), fv(E4, hh), ALU.mult, ["a_f", "lw"], ["padpBH"])
                            self.tt(e1, pv(P["KH"], hh, hh), fv(k2, hh), fv(E4, hh), ALU.mult, ["k2", "lw"], ["padpKH"])
                        pr = lambda nm: "padp" + nm
                        identb = self.identb

                        def indep(q, sidx):
                            sl = SL[sidx]
                            bA, bB = ps[2 * sidx], ps[2 * sidx + 1]
                            rA, rB = "ps%d" % (2 * sidx), "ps%d" % (2 * sidx + 1)
                            sr = lambda nm: "sl%d_%s" % (sidx, nm)
                            ARq = P["AR"][:, q, :, :]
                            Aq = P["AR"][:, q, 0:2, :]
                            Rq_ = P["AR"][:, q, 2:4, :]
                            Bq = P["B"][:, q, :, :]
                            Kq = P["K"][:, q, :, :]
                            XQ, Xt = sl["XQ"], sl["Xt"]
                            tb0 = sidx * 256
                            psT = self.psT
                            if "all" in _os.environ.get("RW_SKIP", ""):
                                return
                            yield

                            def tr(dst, src, res):
                                S.op("tensor", "transpose", dict(out=dst, in_=src, identity=identb[:, :]), reads=[res, "identb"], writes=["psT"])
                            if kind == "p":
                                self.mm(bA[:, 0:256], Bq, ARq, True, True, [pr("B"), pr("AR")], [rA])
                                self.mm(bB[:, 0:128], Aq, Bq, True, True, [pr("B"), pr("AR")], [rB])
                                tr(psT[:, tb0:tb0 + 128], P["BH"][:, q, :, :], pr("BH"))
                                tr(psT[:, tb0 + 128:tb0 + 256], P["KH"][:, q, :, :], pr("KH"))
                                yield
                                self.tt("vector", XQ[0][:, 0:128], bA[:, 0:128], MSU, ALU.mult, [rA, "cst"], [sr("XQ0")])
                                self.tt("vector", sl["MRB"][:, :], bA[:, 128:256], MIU, ALU.mult, [rA, "cst"], [sr("MRB")])
                                self.tt("vector", Xt[0][:, :], bB[:, 0:128], MSL, ALU.mult, [rB, "cst"], [sr("Xt0")])
                                self.act(sl["BhT"][:, :], psT[:, tb0:tb0 + 128], AF.Copy, ["psT"], [sr("BhT")])
                                self.act(sl["KhT"][:, :], psT[:, tb0 + 128:tb0 + 256], AF.Copy, ["psT"], [sr("KhT")])
                                yield
                                self.tt("vector", XQ[0][:, 128:256], XQ[0][:, 0:128], IDN, ALU.add, [sr("XQ0"), "cst"], [sr("XQ0")])
                                self.mm(bA[:, 0:256], Kq, ARq, True, True, [pr("K"), pr("AR")], [rA])
                                self.mm(bB[:, 0:64], P["V"][:, q, :, :], i2b[:, :], True, True, [pr("V"), "i2b"], [rB])
                                tr(psT[:, tb0:tb0 + 128], Aq, pr("AR"))
                                yield
                                self.tt("vector", sl["MM2"][:, :], bA[:, 0:256], MSUIU, ALU.mult, [rA, "cst"], [sr("MM2")])
                                self.act(sl["Vst"][:, :], bB[:, 0:64], AF.Copy, [rB], [sr("Vst")])
                                self.act(sl["AT"][:, :], psT[:, tb0:tb0 + 128], AF.Copy, ["psT"], [sr("AT")])
                                yield
                                cur = 0
                                for lev in range(6):
                                    nxt = 1 - cur
                                    xc, xn = sr("XQ%d" % cur), sr("XQ%d" % nxt)
                                    tc, tn = sr("Xt%d" % cur), sr("Xt%d" % nxt)
                                    if lev == 0:
                                        self.mm(bA[:, 0:128], Xt[cur][:, :], XQ[cur][:, 0:128], True, True, [tc, xc], [rA])
                                        self.mm(bB[:, 0:128], XQ[cur][:, 0:128], Xt[cur][:, :], True, True, [tc, xc], [rB])
                                        yield
                                        self.act(XQ[nxt][:, 0:128], bA[:, 0:128], AF.Copy, [rA], [xn])
                                        self.cp("vector", XQ[nxt][:, 128:256], XQ[cur][:, 128:256], [xc], [xn])
                                        self.cp("vector", Xt[nxt][:, :], bB[:, 0:128], [rB], [tn])
                                        yield
                                    elif lev < 5:
                                        self.mm(bA[:, 0:256], Xt[cur][:, :], XQ[cur][:, :], True, True, [tc, xc], [rA])
                                        self.mm(bB[:, 0:128], XQ[cur][:, 0:128], Xt[cur][:, :], True, True, [tc, xc], [rB])
                                        yield
                                        self.act(XQ[nxt][:, 0:128], bA[:, 0:128], AF.Copy, [rA], [xn])
                                        self.tt("vector", XQ[nxt][:, 128:256], bA[:, 128:256], XQ[cur][:, 128:256], ALU.add, [rA, xc], [xn])
                                        self.cp("vector", Xt[nxt][:, :], bB[:, 0:128], [rB], [tn])
                                        yield
                                    else:
                                        self.mm(bA[:, 0:128], Xt[cur][:, :], XQ[cur][:, 128:256], True, True, [tc, xc], [rA])
                                        self.mm(bB[:, 0:64], sl["MM2"][:, 0:128], sl["Vst"][:, :], True, True, [sr("MM2"), sr("Vst")], [rB])
                                        yield
                                        self.tt("vector", XQ[nxt][:, 128:256], bA[:, 0:128], XQ[cur][:, 128:256], ALU.add, [rA, xc], [xn])
                                        self.act(sl["MakV"][:, :], bB[:, 0:64], AF.Copy, [rB], [sr("MakV")])
                                        yield
                                    cur = nxt
                                Q = XQ[cur][:, 128:256]
                                qres = sr("XQ%d" % cur)
                                self.mm(bA[:, 0:128], Q, sl["AT"][:, :], True, True, [qres, sr("AT")], [rA])
                                self.mm(bB[:, 0:64], Q, sl["MakV"][:, :], True, True, [qres, sr("MakV")], [rB])
                                yield
                                self.act(sl["AqT"][:, :], bA[:, 0:128], AF.Copy, [rA], [sr("AqT")])
                                self.cp("vector", sl["Vq"][:, :], bB[:, 0:64], [rB], [sr("Vq")])
                                yield
                                AqT, aqres = sl["AqT"], sr("AqT")
                            else:
                                self.mm(bA[:, 0:128], Bq, Rq_, True, True, [pr("B"), pr("AR")], [rA])
                                self.mm(bB[:, 0:128], Kq, Rq_, True, True, [pr("K"), pr("AR")], [rB])
                                tr(psT[:, tb0:tb0 + 128], P["BH"][:, q, :, :], pr("BH"))
                                tr(psT[:, tb0 + 128:tb0 + 256], P["KH"][:, q, :, :], pr("KH"))
                                yield
                                self.tt("vector", sl["MRB"][:, :], bA[:, 0:128], MIU, ALU.mult, [rA, "cst"], [sr("MRB")])
                                self.tt("vector", sl["MM2"][:, 128:256], bB[:, 0:128], MIU, ALU.mult, [rB, "cst"], [sr("MM2")])
                                self.act(sl["BhT"][:, :], psT[:, tb0:tb0 + 128], AF.Copy, ["psT"], [sr("BhT")])
                                self.act(sl["KhT"][:, :], psT[:, tb0 + 128:tb0 + 256], AF.Copy, ["psT"], [sr("KhT")])
                                yield
                                self.mm(bB[:, 0:64], P["V"][:, q, :, :], i2b[:, :], True, True, [pr("V"), "i2b"], [rB])
                                tr(psT[:, tb0:tb0 + 128], Aq, pr("AR"))
                                yield
                                self.act(sl["Vst"][:, :], bB[:, 0:64], AF.Copy, [rB], [sr("Vst")])
                                self.cp("vector", sl["AT"][:, :], psT[:, tb0:tb0 + 128], ["psT"], [sr("AT")])
                                yield
                                AqT, aqres = sl["AT"], sr("AT")
                            self.mm(bA[:, 0:128], AqT[:, :], sl["BhT"][:, :], True, True, [aqres, sr("BhT")], [rA])
                            self.mm(bB[:, 0:128], AqT[:, :], sl["MRB"][:, :], True, True, [aqres, sr("MRB")], [rB])
                            yield
                            self.act(Gs[:, q, :], bA[:, 0:128], AF.Copy, [rA], ["rGs%d" % q])
                            self.tt("vector", Rqs[:, q, :], bB[:, 0:128], Rq_, ALU.add, [rB, pr("AR")], ["rRqs%d" % q])
                            yield
                            if kind == "p":
                                self.mm(bA[:, 0:64], sl["BhT"][:, :], sl["Vq"][:, :], True, False, [sr("BhT"), sr("Vq")], [rA])
                                self.mm(bA[:, 0:64], sl["KhT"][:, :], sl["Vst"][:, :], False, True, [sr("KhT"), sr("Vst")], [rA])
                                self.mm(bB[:, 0:64], sl["MRB"][:, :], sl["Vq"][:, :], True, False, [sr("MRB"), sr("Vq")], [rB])
                                self.mm(bB[:, 0:64], sl["MM2"][:, 128:256], sl["Vst"][:, :], False, True, [sr("MM2"), sr("Vst")], [rB])
                            else:
                                self.mm(bA[:, 0:64], sl["KhT"][:, :], sl["Vst"][:, :], True, True, [sr("KhT"), sr("Vst")], [rA])
                                self.mm(bB[:, 0:64], sl["MM2"][:, 128:256], sl["Vst"][:, :], True, True, [sr("MM2"), sr("Vst")], [rB])
                            yield
                            self.act(Zb[:, q, :], bA[:, 0:64], AF.Copy, [rA], ["rZb%d" % q])
                            self.cp("vector", Yz[:, q, :], bB[:, 0:64], [rB], ["rYz%d" % q])
                            yield

                        def sidegen(q, yb, row):
                            bst, bag, ysb, ypad = bsts[yb], bags[yb], Ysb[yb], Ypad[yb]
                            yield
                            self.act(bag[:, 2:3], bag[:, 1:2], AF.Ln, ["rbag%d" % yb, "consts"], ["rbag%d" % yb], bias=self.eps_t[:, 3:4])
                            self.act(bag[:, 2:3], bag[:, 2:3], AF.Exp, ["rbag%d" % yb], ["rbag%d" % yb], scale=-0.5)
                            yield
                            for hh in range(2):
                                hs = slice(64 * hh, 64 * hh + 64)
                                self.ts("vector", ypad[hs, hh, :], ysb[hs, :], bag[hs, 0:1], ALU.subtract,
                                        ["rYsb%d" % yb, "rbag%d" % yb], ["rYpad%d" % yb], s2=bag[hs, 2:3], op1=ALU.mult)
                            yield
                            self.mm(ps[5][:, 64:128], ypad[:, :, :], i2b[:, :], True, True, ["rYpad%d" % yb, "i2b"], ["ps5"])
                            yield
                            if kind == "p":
                                self.act(yfm_sb[:, q * C:(q + 1) * C], ps[5][:, 64:128], AF.Copy, ["ps5"], ["yfm"])
                            else:
                                self.cp("vector", yfm_sb[:, row:row + 1], ps[5][:, 64:65], ["ps5"], ["yfm"])
                            yield

                        def seqgen(done):
                            if "seq" in _os.environ.get("RW_SKIP", ""):
                                return
                            sides = []

                            def adv_sides():
                                for sd in list(sides):
                                    try:
                                        next(sd)
                                    except StopIteration:
                                        sides.remove(sd)
                            for q in range(gn):
                                while q not in done:
                                    adv_sides()
                                    yield
                                row = g0 + q
                                if kind == "s":
                                    tb_i = q % 2
                                    S.dma("sync", "rTf%d" % tb_i, Tfs[tb_i][:, :], wkv_st[row, cc], writes=["rTf%d" % tb_i])
                                    self.cp("vector", Tbs[tb_i][:, :], Tfs[tb_i][:, :], ["rTf%d" % tb_i], ["rTb%d" % tb_i])
                                    gcol = gam[:, par, row:row + 1]
                                else:
                                    tb_i = 0
                                    gcol = gam[:, par, q:q + 1]
                                tf_, tb_ = Tfs[tb_i], Tbs[tb_i]
                                tfr, tbr = "rTf%d" % tb_i, "rTb%d" % tb_i
                                yb = q % NYB
                                self.mm(ps[4][:, 0:64], Gs[:, q, :], tb_[:, :], True, False, ["rGs%d" % q, tbr], ["ps4"])
                                self.mm(ps[4][:, 0:64], identb[:, :], Zb[:, q, :], False, True, ["identb", "rZb%d" % q], ["ps4"])
                                self.mm(ps[5][:, 0:64], Rqs[:, q, :], tb_[:, :], True, False, ["rRqs%d" % q, tbr], ["ps5"])
                                self.mm(ps[5][:, 0:64], identb[:, :], Yz[:, q, :], False, True, ["identb", "rYz%d" % q], ["ps5"])
                                adv_sides()
                                yield
                                self.stt(tb_[:, :], tf_[:, :], gcol, ps[4][:, 0:64], ALU.mult, ALU.add, [tfr, "gam%d" % par, "ps4"], [tbr])
                                self.cp("vector", Ysb[yb][:, :], ps[5][:, 0:64], ["ps5"], ["rYsb%d" % yb])
                                self.stt(tf_[:, :], tf_[:, :], gcol, ps[4][:, 0:64], ALU.mult, ALU.add, [tfr, "gam%d" % par, "ps4"], [tfr])
                                if kind == "s":
                                    S.dma("sync", "rTfo%d" % tb_i, self.wkv_so[row, cc], tf_[:, :], reads=[tfr])
                                S.op("vector", "bn_stats", dict(out=bsts[yb][:, :], in_=Ysb[yb][:, :]), reads=["rYsb%d" % yb], writes=["rbst%d" % yb])
                                S.op("vector", "bn_aggr", dict(out=bags[yb][:, 0:2], in_=bsts[yb][:, :]), reads=["rbst%d" % yb], writes=["rbag%d" % yb])
                                sd = sidegen(q, yb, row)
                                next(sd)
                                sides.append(sd)
                                yield
                            while sides:
                                adv_sides()
                                yield

                        from collections import deque
                        pend = deque(range(gn))
                        slots = [None] * WSL
                        done = set()
                        sg = seqgen(done)
                        seq_alive = True
                        nx_alive = nxt_gen is not None and (kind == "p" or g0 > 0 or True)
                        while pend or any(x is not None for x in slots) or seq_alive or nx_alive:
                            if nx_alive:
                                try:
                                    next(nxt_gen)
                                except StopIteration:
                                    nx_alive = False
                            for sidx in range(WSL):
                                if slots[sidx] is None and pend:
                                    q_ = pend.popleft()
                                    slots[sidx] = (q_, indep(q_, sidx))
                                if slots[sidx] is not None:
                                    q_, g_ = slots[sidx]
                                    try:
                                        next(g_)
                                    except StopIteration:
                                        done.add(q_)
                                        slots[sidx] = None
                            if seq_alive:
                                try:
                                    next(sg)
                                except StopIteration:
                                    seq_alive = False
                    if not samp and t0 + n == L:
                        S.dma("sync", "rTfo0", self.wkv_po[cc], Tf[:, :], reads=["rTf0"])
                    self.stt(BB[par][:, :n], yfm_sb[:, :n], self.vcol("ln_w", cc), BB[par][:, :n], ALU.mult, ALU.add, ["yfm", "BB%d" % par, "vec"], ["BB%d" % par])
                    self.tt("vector", xot[:, :n], BB[par][:, :n], GF[par][:, :n], ALU.mult, ["BB%d" % par, "GF%d" % par], ["rxot"])
                    for d in range(NCH):
                        pb = ps[d % 2]
                        prr = "ps%d" % (d % 2)
                        self.mm(pb[:, :n], woc[:, d * 128:(d + 1) * 128], xot[:, :n], True, True, ["rwoc", "rxot"], [prr])
                        self.tt("vector", h[:, d, t0:t0 + n], h[:, d, t0:t0 + n], pb[:, :n], ALU.add, [prr, "h%d" % ti], ["h%d" % ti])
            S.barrier()

    def gla_layer(self, i):
        S = self.S
        L, NT = self.L, self.NT
        h, ub, ps = self.h, self.ubuf, self.ps
        CG = 128
        NCK = L // CG
        ntl = len(self.tiles)
        allu = ["u%d" % ti for ti in range(ntl)]
        self.mix_norm_all(i)
        w_in = self.gla_w_in
        with contextlib.ExitStack() as ph:
            self.cm128 = self.sb("cm128", [128, L], F32, ph)
            self.memset("gpsimd", self.cm128[:, :], 1.0, ["cm"])
            self.memset("gpsimd", self.cm128[:, :].rearrange("p (n k) -> p n k", k=128)[:, :, 0:1], 0.0, ["cm"])
            vtok = self.sb("vtok", [128, NCK, 256], BF16, ph)
            vtok_s = self.sb("vtok_s", [NS, 256], F32, ph)
            wvh = self.sb("gwvh", [128, NCH, 256], BF16, ph)
            glb = self.sb("glb", [16, NT], BF16, ph)
            wgl = self.sb("wgl", [128, NCH, 16], BF16, ph)
            negb = self.sb("negb", [128, 4], F32, ph)
            gb = VEC_COLS["gla_b_gk"]
            self.ts("vector", negb[:, :], self.vec[:, gb:gb + 4], -1.0, ALU.mult, ["vec"], ["negb"])
            self.load_w(wgl[:, :, :], w_in[:, 3072:3088], "wgl")
            if True:
                for ti, (t0, n) in enumerate(self.tiles):
                    for c in range(NCH):
                        self.mm(ps[2][:16, :n], wgl[:, c, :], ub[:, c, t0:t0 + n], c == 0, c == NCH - 1, ["wgl", "u%d" % ti], ["ps2"])
                    self.act(glb[:, t0:t0 + n], ps[2][:16, :n], AF.Copy, ["ps2"], ["glb"])
                S.barrier()
            bA = self.sb("gA", [128, NT], F32, ph)
            bB = self.sb("gB", [128, NT], F32, ph)
            qt = self.sb("gqt", [128, NT], BF16, ph)
            kt = self.sb("gkt", [128, NT], BF16, ph)
            kh = self.sb("gkh", [128, L], BF16, ph)
            oh = self.sb("goh", [128, 2, NT], BF16, ph)
            xoh = self.sb("gxo", [128, 2, 512], BF16, ph)
            wq = self.sb("gwq", [128, NCH, 128], BF16, ph)
            wk = self.sb("gwk", [128, NCH, 128], BF16, ph)
            wg = self.sb("gwg", [128, NCH, 256], BF16, ph)
            woh = self.sb("gwo", [128, 2, 1024], BF16, ph)
            wgk2 = self.sb("gwgk2", [16, 128], BF16, ph)
            Sf = self.sb("gSf", [128, 256], F32, ph)
            Sb = self.sb("gSb", [128, 256], BF16, ph)
            attb2 = [self.sb("gatt%d" % b_, [128, 128], BF16, ph) for b_ in range(2)]
            khT2 = [self.sb("gkhT%d" % b_, [128, 128], BF16, ph) for b_ in range(2)]
            ktok_s = self.sb("gktoks", [NS, 128], F32, ph)
            ksel = self.sb("gksel", [NS, 128], F32, ph)
            qs = self.sb("gqs", [128, NS], F32, ph)
            S0 = [self.sb("gS0_%d" % b, [128, 256], F32, ph) for b in range(2)]
            S1 = [self.sb("gS1_%d" % b, [128, 256], F32, ph) for b in range(2)]
            tmp = self.sb("gtmp", [128, 512], F32, ph)
            tmp2 = self.sb("gtmp2", [128, 512], F32, ph)
            rs = self.rstd
            sqh = self.sq
            for hd in range(4):
                self.load_w(wq[:, :, :], w_in[:, hd * 128:(hd + 1) * 128], "gwq")
                self.load_w(wk[:, :, :], w_in[:, 512 + hd * 128:512 + (hd + 1) * 128], "gwk")
                self.load_w(wg[:, :, :], w_in[:, 2048 + hd * 256:2048 + (hd + 1) * 256], "gwg")
                self.load_w(woh[:, :, :], self.gla_wo[hd * 256:(hd + 1) * 256, :], "gwo")
                S.dma("gpsimd", "gwgk2", wgk2[:, :], self.gla_w_gk2[:, hd * 128:(hd + 1) * 128], writes=["gwgk2"])
                self.load_w(wvh[:, :, :], w_in[:, 1024 + hd * 256:1024 + (hd + 1) * 256], "gwvh")
                for n in range(NCK + 1):
                    pb = ps[n % 2]
                    if n < NCK:
                        lo, cnt = n * CG, CG
                    else:
                        lo, cnt = L, NS
                    for c in range(NCH):
                        self.mm(pb[:cnt, :256], ub[:, c, lo:lo + cnt], wvh[:, c, :], c == 0, c == NCH - 1, ["gwvh"] + allu, ["ps%d" % (n % 2)])
                    if n < NCK:
                        self.act(vtok[:, n, :], pb[:, :256], AF.Copy, ["ps%d" % (n % 2)], ["vtok"])
                    else:
                        self.act(vtok_s[:, :], pb[:NS, :256], AF.Copy, ["ps%d" % (n % 2)], ["vtok_s"])
                for ti, (t0, n) in enumerate(self.tiles):
                    self.mm(ps[0][:, :n], wgk2[:, :], glb[:, t0:t0 + n], True, True, ["gwgk2", "glb"], ["ps0"])
                    self.act(bA[:, t0:t0 + n], ps[0][:, :n], AF.Exp, ["ps0", "negb"], ["gA"], scale=-1.0, bias=negb[:, hd:hd + 1])
                self.act(bA[:, :], bA[:, :], AF.Ln, ["gA", "consts"], ["gA"], bias=self.eps_t[:, 1:2])
                S.op("vector", "tensor_tensor_scan", dict(out=bB[:, 0:L], data0=self.cm128[:, 0:L], data1=bA[:, 0:L], initial=0.0,
                                                          op0=ALU.mult, op1=ALU.add), reads=["gA", "cm"], writes=["gB"])
                self.cp("vector", bB[:, L:NT], bA[:, L:NT], ["gA"], ["gB"])
                self.act(bA[:, :], bB[:, :], AF.Exp, ["gB"], ["gA"], scale=-1.0 / 16)
                self.act(bB[:, :], bB[:, :], AF.Exp, ["gB"], ["gB"], scale=1.0 / 16)
                for ti, (t0, n) in enumerate(self.tiles):
                    for c in range(NCH):
                        self.mm(ps[0][:, :n], wq[:, c, :], ub[:, c, t0:t0 + n], c == 0, c == NCH - 1, ["gwq", "u%d" % ti], ["ps0"])
                    for c in range(NCH):
                        self.mm(ps[1][:, :n], wk[:, c, :], ub[:, c, t0:t0 + n], c == 0, c == NCH - 1, ["gwk", "u%d" % ti], ["ps1"])
                    self.stt(qt[:, t0:t0 + n], ps[0][:, :n], 128.0 ** -0.5, bA[:, t0:t0 + n], ALU.mult, ALU.mult, ["ps0", "gA"], ["gqt"])
                    self.tt("vector", kt[:, t0:t0 + n], ps[1][:, :n], bB[:, t0:t0 + n], ALU.mult, ["ps1", "gB"], ["gkt"])
                    if t0 >= L:
                        self.ts("vector", qs[:, :], ps[0][:, :NS], 128.0 ** -0.5, ALU.mult, ["ps0"], ["gqs"])
                        for c in range(NCH):
                            self.mm(ps[2][:NS, :128], ub[:, c, L:NT], wk[:, c, :], c == 0, c == NCH - 1, ["gwk", "u%d" % ti], ["ps2"])
                        self.act(ktok_s[:, :], ps[2][:NS, :128], AF.Copy, ["ps2"], ["gktoks"])
                for n in range(NCK):
                    last = n * CG + CG - 1
                    self.ts("vector", kh[:, n * CG:(n + 1) * CG], kt[:, n * CG:(n + 1) * CG], bA[:, last:last + 1], ALU.mult, ["gkt", "gA"], ["gkh"])
                self.memset("vector", Sf[:, :], 0.0, ["gSf"])
                self.memset("vector", Sb[:, :], 0.0, ["gSb"])
                for n in range(NCK):
                    sl = slice(n * CG, (n + 1) * CG)
                    pp = n % 2
                    pa, pra = ps[pp], "ps%d" % pp
                    tbo = pp * 128
                    self.mm(pa[:, :128], kt[:, sl], qt[:, sl], True, True, ["gkt", "gqt"], [pra])
                    self.tt("vector", attb2[pp][:, :], pa[:, :128], self.cst[:, self.C_TRIU:self.C_TRIU + 128], ALU.mult, [pra, "cst"], ["gatt%d" % pp])
                    S.op("tensor", "transpose", dict(out=self.psT[:, tbo:tbo + 128], in_=kh[:, sl], identity=self.identb[:, :]),
                         reads=["gkh", "identb"], writes=["psT"])
                    self.act(khT2[pp][:, :], self.psT[:, tbo:tbo + 128], AF.Copy, ["psT"], ["gkhT%d" % pp])
                    for m in range(2):
                        pb = ps[2 + 2 * pp + m]
                        pr = "ps%d" % (2 + 2 * pp + m)
                        self.mm(pb[:, :128], Sb[:, m * 128:(m + 1) * 128], qt[:, sl], True, False, ["gSb", "gqt"], [pr])
                        self.mm(pb[:, :128], vtok[:, n, m * 128:(m + 1) * 128], attb2[pp][:, :], False, True,
                                ["vtok", "gatt%d" % pp], [pr])
                        self.act(oh[:, m, sl], pb[:, :128], AF.Copy, [pr], ["goh"])
                    self.mm(ps[6][:, :256], khT2[pp][:, :], vtok[:, n, :], True, True, ["gkhT%d" % pp, "vtok"], ["ps6"])
                    last = n * CG + CG - 1
                    self.stt(Sf[:, :], Sf[:, :], bA[:, last:last + 1], ps[6][:, :256], ALU.mult, ALU.add, ["gSf", "gA", "ps6"], ["gSf"])
                    if n < NCK - 1:
                        self.cp("vector", Sb[:, :], Sf[:, :], ["gSf"], ["gSb"])
                S.dma("sync", "gSf", self.gla_po[hd], Sf[:, :], reads=["gSf"])
                for r in range(NS):
                    b = r % 2
                    S.dma("sync", "gS0_%d" % b, S0[b][:, :], self.gla_st[r, hd], writes=["gS0_%d" % b])
                    self.ts("vector", ksel[:, :], ktok_s[:, :], self.cst[:NS, self.C_ID + r:self.C_ID + r + 1], ALU.mult, ["gktoks", "cst"], ["gksel"])
                    self.mm(ps[5][:, :256], ksel[:, :], vtok_s[:, :], True, True, ["gksel", "vtok_s"], ["ps5"])
                    self.stt(S1[b][:, :], S0[b][:, :], bA[:, L + r:L + r + 1], ps[5][:, :256], ALU.mult, ALU.add,
                             ["gS0_%d" % b, "gA", "ps5"], ["gS1_%d" % b])
                    S.dma("sync", "gS1_%d" % b, self.gla_so[r, hd], S1[b][:, :], reads=["gS1_%d" % b])
                    for m in range(2):
                        self.mm(ps[3][:, m:m + 1], S1[b][:, m * 128:(m + 1) * 128], qs[:, r:r + 1], True, True, ["gS1_%d" % b, "gqs"], ["ps3"])
                    self.cp("vector", oh[:, :, L + r], ps[3][:, 0:2], ["ps3"], ["goh"])
                for ti, (t0, n) in enumerate(self.tiles):
                    self.act(sqh[:, 0:2, :n], oh[:, :, t0:t0 + n], AF.Square, ["goh"], ["sq"])
                    for m in range(2):
                        self.mm(ps[6][:, :n], self.ones_bf[:, :], sqh[:, m, :n], m == 0, m == 1, ["sq", "ones"], ["ps6"])
                    self.act(rs[:, :n], ps[6][:, :n], AF.Ln, ["ps6", "consts"], ["rstd"], scale=1.0 / 256, bias=self.eps_t[:, 2:3])
                    self.act(rs[:, :n], rs[:, :n], AF.Exp, ["rstd"], ["rstd"], scale=-0.5)
                    for m in range(2):
                        pb = ps[m]
                        pr = "ps%d" % m
                        for c in range(NCH):
                            self.mm(pb[:, :n], wg[:, c, m * 128:(m + 1) * 128], ub[:, c, t0:t0 + n], c == 0, c == NCH - 1, ["gwg", "u%d" % ti], [pr])
                        self.act(tmp[:, :n], pb[:, :n], AF.Exp, [pr], ["gtmp"], scale=-1.0)
                        self.ts("vector", tmp[:, :n], tmp[:, :n], 1.0, ALU.add, ["gtmp"], ["gtmp"])
                        S.op("vector", "reciprocal", dict(out=tmp[:, :n], in_=tmp[:, :n]), reads=["gtmp"], writes=["gtmp"])
                        self.tt("vector", tmp[:, :n], tmp[:, :n], pb[:, :n], ALU.mult, ["gtmp", pr], ["gtmp"])
                        self.stt(tmp2[:, :n], oh[:, m, t0:t0 + n], self.vec[:, VEC_COLS["gla_norm"] + m:VEC_COLS["gla_norm"] + m + 1], rs[:, :n],
                                 ALU.mult, ALU.mult, ["goh", "vec", "rstd"], ["gtmp2"])
                        self.tt("vector", xoh[:, m, :n], tmp2[:, :n], tmp[:, :n], ALU.mult, ["gtmp", "gtmp2"], ["gxo"])
                    for d in range(NCH):
                        pb = ps[4 + d % 2]
                        pr = "ps%d" % (4 + d % 2)
                        for m in range(2):
                            self.mm(pb[:, :n], woh[:, m, d * 128:(d + 1) * 128], xoh[:, m, :n], m == 0, m == 1, ["gwo", "gxo"], [pr])
                        self.tt("vector", h[:, d, t0:t0 + n], h[:, d, t0:t0 + n], pb[:, :n], ALU.add, [pr, "h%d" % ti], ["h%d" % ti])
            S.barrier()

    def conv_layer(self, i):
        S = self.S
        L, NT = self.L, self.NT
        self.mix_norm_all(i)
        ub = self.ubuf
        with contextlib.ExitStack() as ph:
            xo = self.sb("cxo", [128, NCH, NT], BF16, ph)
            wo = self.sb("cwo", [128, NCH, 1024], BF16, ph)
            ws = [[self.sb("cw%d_%d" % (k, b), [128, NCH, 128], BF16, ph) for k in range(3)] for b in range(2)]
            zcb = self.sb("zcb", [128, 2 + L + 3 * NS], F32, ph)
            gbt = self.sb("gbt", [128, NT], F32, ph)
            cv = self.sb("cv", [128, NT], F32, ph)
            tmp = self.sb("ctmp", [128, 512], F32, ph)
            cst = self.sb("cst", [128, NCH, 2 + 2 * NS], F32, ph)
            self.load_w(wo[:, :, :], self.conv_wo[:, :], "cwo", nsplit=2)
            SB = 2 + L
            zs = zcb[:, SB:SB + 3 * NS].rearrange("p (r k) -> p r k", k=3)
            stv = self.conv_st.rearrange("(c p) r k -> p c r k", p=128)
            self.memset("gpsimd", zcb[:, 0:2], 0.0, ["zcb_halo"])
            def conv_load(cc):
                b = cc % 2
                for k in range(3):
                    self.load_w(ws[b][k][:, :, :], self.conv_w_in[:, k * 1024 + cc * 128:k * 1024 + (cc + 1) * 128], "cw%d_%d" % (k, b))
            conv_load(0)
            for cc in range(NCH):
                b = cc % 2
                if cc + 1 < NCH:
                    conv_load(cc + 1)
                S.dma("sync", "zcb_st", zs[:, :, 0:2], stv[:, cc, :, :], writes=["zcb_st"])
                for ti, (t0, n) in enumerate(self.tiles):
                    pbs = []
                    for k in range(3):
                        pb = self.ps[k]
                        for c in range(NCH):
                            self.mm(pb[:, :n], ws[b][k][:, c, :], ub[:, c, t0:t0 + n], c == 0, c == NCH - 1,
                                    ["cw%d_%d" % (k, b), "u%d" % ti], ["ps%d" % k])
                    self.act(gbt[:, t0:t0 + n], self.ps[0][:, :n], AF.Copy, ["ps0"], ["gbt%d" % ti])
                    self.act(tmp[:, :n], self.ps[1][:, :n], AF.Copy, ["ps1"], ["ctmp"])
                    if t0 < L:
                        dst = zcb[:, 2 + t0:2 + t0 + n]
                    else:
                        dst = zs[:, :, 2]
                    self.tt("vector", dst, tmp[:, :n], self.ps[2][:, :n], ALU.mult, ["ctmp", "ps2"], ["zcb%d" % ti])
                allz = ["zcb%d" % ti for ti in range(len(self.tiles))] + ["zcb_halo", "zcb_st"]
                self.ts("vector", cv[:, 0:L], zcb[:, 2:2 + L], self.vcol("conv_w2", cc), ALU.mult, allz + ["vec"], ["cv"])
                self.stt(cv[:, 0:L], zcb[:, 1:1 + L], self.vcol("conv_w1", cc), cv[:, 0:L], ALU.mult, ALU.add, allz + ["cv", "vec"], ["cv"])
                self.stt(cv[:, 0:L], zcb[:, 0:L], self.vcol("conv_w0", cc), cv[:, 0:L], ALU.mult, ALU.add, allz + ["cv", "vec"], ["cv"])
                self.ts("vector", cv[:, L:NT], zs[:, :, 2], self.vcol("conv_w2", cc), ALU.mult, allz + ["vec", "cv"], ["cv"])
                self.stt(cv[:, L:NT], zs[:, :, 1], self.vcol("conv_w1", cc), cv[:, L:NT], ALU.mult, ALU.add, allz + ["cv", "vec"], ["cv"])
                self.stt(cv[:, L:NT], zs[:, :, 0], self.vcol("conv_w0", cc), cv[:, L:NT], ALU.mult, ALU.add, allz + ["cv", "vec"], ["cv"])
                gres = ["gbt%d" % ti for ti in range(len(self.tiles))]
                self.tt("vector", xo[:, cc, :], gbt[:, :], cv[:, :], ALU.mult, gres + ["cv"], ["cxo%d" % cc])
                self.cp("vector", cst[:, cc, 0:2], zcb[:, 2 + L - 2:2 + L], allz, ["cst"])
                self.cp("vector", cst[:, cc, 2:2 + 2 * NS].rearrange("p (r k) -> p r k", k=2), zs[:, :, 1:3], allz, ["cst"])
            S.dma("sync", "cst", self.conv_o.rearrange("(c p) n -> p c n", p=128), cst[:, :, :], reads=["cst"])
            self.out_proj(xo, wo, "cwo", lambda j, ti: ["cxo%d" % j])
            S.barrier()

    def pool_layer(self, i):
        S = self.S
        L, NT = self.L, self.NT
        h = self.h
        ub = self.ubuf
        NB = 15 + L + 16 * NS
        with contextlib.ExitStack() as ph:
            rall = self.sb("rall", [128, NT], F32, ph)
            pw = self.sb("pw", [128, 4, 2, 256], BF16, ph)
            bufs = [self.sb("pbuf%d" % b, [128, NB], F32, ph) for b in range(3)]
            pst = self.sb("pst", [128, NCH, 15 + 15 * NS], F32, ph)
            S.dma("gpsimd", "pw", pw[:, :, :, :], self.pool_w.rearrange("g (k p) n -> p g k n", p=128), writes=["pw"])
            for ti, (t0, n) in enumerate(self.tiles):
                self.rms_rstd(ti, rall[:, t0:t0 + n], "rall")
            stv = self.pool_st.rearrange("(c p) r k -> p c r k", p=128)
            SB = 15 + L
            for b in range(3):
                self.memset("gpsimd", bufs[b][:, 0:15], 0.0, ["pbuf%d" % b])
            for c in range(NCH):
                gi = c // 2
                w = 2 << gi
                A = bufs[0]
                As = A[:, SB:SB + 16 * NS].rearrange("p (r k) -> p r k", k=16)
                S.dma("sync", "pbuf0", As[:, :, 0:15], stv[:, c, :, :], writes=["pbuf0"])
                allh = ["h%d" % ti for ti in range(len(self.tiles))]
                self.stt(A[:, 15:15 + L], h[:, c, 0:L], self.vcol("norm_mix%d" % i, c), rall[:, 0:L], ALU.mult, ALU.mult,
                         allh + ["rall", "vec"], ["pbuf0"])
                self.stt(As[:, :, 15], h[:, c, L:NT], self.vcol("norm_mix%d" % i, c), rall[:, L:NT], ALU.mult, ALU.mult,
                         allh + ["rall", "vec"], ["pbuf0"])
                self.cp("gpsimd", pst[:, c, 0:15], A[:, L:L + 15], ["pbuf0"], ["pst"])
                self.cp("gpsimd", pst[:, c, 15:15 + 15 * NS].rearrange("p (r k) -> p r k", k=15), As[:, :, 1:16], ["pbuf0"], ["pst"])
                src, si = A, 0
                lo = 0
                for k in range(gi + 1):
                    sh = 1 << k
                    lo += sh
                    di = 1 if si != 1 else 2
                    dst = bufs[di]
                    ss = src[:, SB:SB + 16 * NS].rearrange("p (r k) -> p r k", k=16)
                    ds = dst[:, SB:SB + 16 * NS].rearrange("p (r k) -> p r k", k=16)
                    self.tt("vector", dst[:, lo:15 + L], src[:, lo:15 + L], src[:, lo - sh:15 + L - sh], ALU.add,
                            ["pbuf%d" % si], ["pbuf%d" % di])
                    self.tt("gpsimd", ds[:, :, lo:16], ss[:, :, lo:16], ss[:, :, lo - sh:16 - sh], ALU.add,
                            ["pbuf%d" % si], ["pbuf%d" % di])
                    src, si = dst, di
                ss = src[:, SB:SB + 16 * NS].rearrange("p (r k) -> p r k", k=16)
                self.stt(ub[:, c, 0:L], src[:, 15:15 + L], 1.0 / w, A[:, 15:15 + L], ALU.mult, ALU.subtract,
                         ["pbuf%d" % si, "pbuf0"], ["pd%d" % c])
                self.stt(ub[:, c, L:NT], ss[:, :, 15], 1.0 / w, As[:, :, 15], ALU.mult, ALU.subtract,
                         ["pbuf%d" % si, "pbuf0"], ["pd%d" % c])
                nf = w - 1
                ic = VEC_COLS["invc"]
                ftmp = self.rstd
                self.tt("vector", ftmp[:, 0:nf], src[:, 15:15 + nf], self.vec[:, ic:ic + nf], ALU.mult, ["pbuf%d" % si, "vec"], ["rstd"])
                self.tt("vector", ub[:, c, 0:nf], ftmp[:, 0:nf], A[:, 15:15 + nf], ALU.subtract, ["rstd", "pbuf0"], ["pd%d" % c])
            S.dma("sync", "pst", self.pool_o.rearrange("(c p) n -> p c n", p=128), pst[:, :, :], reads=["pst"])
            for ti, (t0, n) in enumerate(self.tiles):
                for gi in range(4):
                    for m in range(2):
                        d = 2 * gi + m
                        pb = self.ps[4 + d % 2]
                        pr = "ps%d" % (4 + d % 2)
                        for k in range(2):
                            self.mm(pb[:, :n], pw[:, gi, k, m * 128:(m + 1) * 128], ub[:, 2 * gi + k, t0:t0 + n], k == 0, k == 1,
                                    ["pw", "pd%d" % (2 * gi + k)], [pr])
                        self.stt(h[:, d, t0:t0 + n], pb[:, :n], self.vcol("pool_scale", d), h[:, d, t0:t0 + n], ALU.mult, ALU.add,
                                 [pr, "h%d" % ti, "vec"], ["h%d" % ti])
            S.barrier()

    def build(self):
        L, NT = self.L, self.NT
        nc = bass.Bass("TRN2", target_bir_lowering=False)
        self.nc = nc
        self.xT = self.dram_in("xT", [D, NT])
        self.vec_d = self.dram_in("vec", [128, NVEC])
        self.ffn_up = self.dram_in("ffn_up", [4, D, DFF])
        self.ffn_down = self.dram_in("ffn_down", [4, DFF, D])
        self.yT = self.dram_out("yT", [D, NT])
        self.conv_w_in = self.dram_in("conv_w_in", [D, 3 * D])
        self.conv_wo = self.dram_in("conv_wo", [D, D])
        self.conv_st = self.dram_in("conv_st", [D, NS, 2])
        self.conv_o = self.dram_out("conv_o", [D, 2 + 2 * NS])
        self.pool_w = self.dram_in("pool_w", [4, 256, 256])
        self.pool_st = self.dram_in("pool_st", [D, NS, 15])
        self.pool_o = self.dram_out("pool_o", [D, 15 + 15 * NS])
        self.cst_d = self.dram_in("cst", [128, self.NCST])
        self.gla_w_in = self.dram_in("gla_w_in", [D, 3088])
        self.gla_w_gk2 = self.dram_in("gla_w_gk2", [16, 512])
        self.gla_wo = self.dram_in("gla_wo", [D, D])
        self.gla_st = self.dram_in("gla_st", [NS, 4, 128, 256])
        self.gla_po = self.dram_out("gla_po", [4, 128, 256])
        self.gla_so = self.dram_out("gla_so", [NS, 4, 128, 256])
        self.rwkv_w_rkv = [self.dram_in("rwkv_w_%s" % k, [D, D]) for k in "rkv"]
        self.rwkv_w1 = self.dram_in("rwkv_w1", [D, 64])
        self.rwkv_w2 = self.dram_in("rwkv_w2", [64, D])
        self.rwkv_a1 = self.dram_in("rwkv_a1", [D, 64])
        self.rwkv_a2 = self.dram_in("rwkv_a2", [64, D])
        self.rwkv_g1 = self.dram_in("rwkv_g1", [D, 160])
        self.rwkv_g2 = self.dram_in("rwkv_g2", [160, D])
        self.rwkv_wo = self.dram_in("rwkv_wo", [D, D])
        self.shift_st = self.dram_in("shift_st", [D, NS])
        self.wkv_st = self.dram_in("wkv_st", [NS, 8, 128, 64])
        self.shift_o = self.dram_out("shift_o", [D, 1 + NS])
        self.wkv_po = self.dram_out("wkv_po", [8, 128, 64])
        self.wkv_so = self.dram_out("wkv_so", [NS, 8, 128, 64])
        with contextlib.ExitStack() as st:
            self.st = st
            S = Sched(nc, st)
            self.S = S
            self.h = self.sb("h", [128, NCH, NT], F32)
            self.ubuf = self.sb("ubuf", [128, NCH, NT + 1 + NS], BF16)
            self.vec = self.sb("vecs", [128, NVEC], F32)
            self.ones_bf = self.sb("ones_bf", [128, 128], BF16)
            self.eps_t = self.sb("eps_t", [128, 4], F32)
            self.sq = self.sb("sq", [128, NCH, 512], BF16)
            self.rstd = self.sb("rstd", [128, 512], F32)
            self.ps = [st.enter_context(nc.psum_tensor("psb%d" % i, [128, 512], F32)) for i in range(7)]
            self.psT = st.enter_context(nc.psum_tensor("psT", [128, 1024], BF16))
            self.cst = self.sb("cst", [128, self.NCST], F32)
            self.identb = self.sb("identb", [128, 128], BF16)

            self.memset("vector", self.ones_bf[:], 1.0, ["ones"])
            self.memset("vector", self.eps_t[:, 0:1], NORM_EPS, ["consts"])
            self.memset("vector", self.eps_t[:, 1:2], 1.0, ["consts"])
            self.memset("vector", self.eps_t[:, 2:3], 1e-5, ["consts"])
            self.memset("vector", self.eps_t[:, 3:4], 64e-5, ["consts"])
            S.dma("sync", "cst", self.cst[:], self.cst_d, writes=["cst"])
            self.cp("vector", self.identb[:, :], self.cst[:, self.C_ID:self.C_ID + 128], ["cst"], ["identb"])

            S.dma("sync", "vec", self.vec[:], self.vec_d, writes=["vec"])
            xv = self.xT.rearrange("(c p) t -> p c t", p=128)
            for ti, (t0, n) in enumerate(self.tiles):
                S.dma("sync", "h%d" % ti, self.h[:, :, t0:t0 + n], xv[:, :, t0:t0 + n], writes=["h%d" % ti])
            for i in range(4):
                if self.layers[i]:
                    getattr(self, ["rwkv_layer", "gla_layer", "conv_layer", "pool_layer"][i])(i)
                if self.ffn:
                    self.ffn_layer(i)
            yv = self.yT.rearrange("(c p) t -> p c t", p=128)
            with contextlib.ExitStack() as ph:
                yo = [self.sb("yo%d" % b, [128, NCH, 512], F32, ph) for b in range(2)]
                for ti, (t0, n) in enumerate(self.tiles):
                    b = ti % 2
                    self.norm_tile(ti, "norm_final", lambda c: yo[b][:, c, :n], "yo%d" % b)
                    S.dma("sync", "yo%d" % b, yv[:, :, t0:t0 + n], yo[b][:, :, :n], reads=["yo%d" % b])
                S.barrier()
            with nc.Block() as block:
                S.replay(block)
        return nc


def pack_vec(inp):
    v = np.zeros((128, NVEC), np.float32)

    def put(name, arr):
        a = np.asarray(arr, np.float32).reshape(-1)
        k = a.size // 128
        c0 = VEC_COLS[name]
        v[:, c0:c0 + k] = a.reshape(k, 128).T
    for i in range(4):
        put("norm_mix%d" % i, inp["norm_mix"][i])
        put("norm_ffn%d" % i, inp["norm_ffn"][i])
    put("norm_final", inp["norm_final"])
    for j in range(6):
        put("mu%d" % j, inp["rwkv_mu"][0, j])
    put("w0", inp["rwkv_w0"][0]); put("a0", inp["rwkv_a0"][0]); put("k_k", inp["rwkv_k_k"][0]); put("k_a", inp["rwkv_k_a"][0])
    put("r_k", inp["rwkv_r_k"][0]); put("ln_w", inp["rwkv_ln_w"][0]); put("ln_b", inp["rwkv_ln_b"][0])
    for j in range(3):
        put("conv_w%d" % j, inp["conv_w"][0, j])
    put("pool_scale", inp["pool_scale"][0])
    put("gla_b_gk", inp["gla_b_gk"][0]); put("gla_norm", inp["gla_norm"][0])
    v[:, VEC_COLS["invc"]:VEC_COLS["invc"] + 16] = (1.0 / np.arange(1, 17, dtype=np.float64)).astype(np.float32)[None, :]
    return v


def make_in_maps(inp, L, ncores=8):
    vec = pack_vec(inp)
    maps = []
    for core in range(ncores):
        xp = np.asarray(inp["x_prompt"][core, :L], np.float32)
        xs = np.asarray(inp["x_sample"][core * NS:(core + 1) * NS, 0], np.float32)
        xT = np.ascontiguousarray(np.concatenate([xp, xs], axis=0).T)
        rs = slice(core * NS, (core + 1) * NS)
        m = {"xT": xT, "vec": vec,
             "ffn_up": np.asarray(inp["ffn_up"], np.float32), "ffn_down": np.asarray(inp["ffn_down"], np.float32),
             "conv_w_in": np.asarray(inp["conv_w_in"][0], np.float32), "conv_wo": np.asarray(inp["conv_wo"][0], np.float32),
             "conv_st": np.ascontiguousarray(np.transpose(inp["state_conv"][0, rs], (2, 0, 1))),
             "pool_w": np.asarray(inp["pool_w"][0], np.float32), "cst": make_cst(),
             "gla_w_in": np.asarray(inp["gla_w_in"][0], np.float32), "gla_w_gk2": np.asarray(inp["gla_w_gk2"][0], np.float32),
             "gla_wo": np.asarray(inp["gla_wo"][0], np.float32), "gla_st": np.ascontiguousarray(inp["state_gla"][0, rs]),
             "rwkv_w_r": np.asarray(inp["rwkv_w_rkv"][0, 0], np.float32), "rwkv_w_k": np.asarray(inp["rwkv_w_rkv"][0, 1], np.float32),
             "rwkv_w_v": np.asarray(inp["rwkv_w_rkv"][0, 2], np.float32),
             "rwkv_w1": np.asarray(inp["rwkv_w1"][0], np.float32), "rwkv_w2": np.asarray(inp["rwkv_w2"][0], np.float32),
             "rwkv_a1": np.asarray(inp["rwkv_a1"][0], np.float32), "rwkv_a2": np.asarray(inp["rwkv_a2"][0], np.float32),
             "rwkv_g1": np.asarray(inp["rwkv_g1"][0], np.float32), "rwkv_g2": np.asarray(inp["rwkv_g2"][0], np.float32),
             "rwkv_wo": np.asarray(inp["rwkv_wo"][0], np.float32),
             "shift_st": np.ascontiguousarray(np.asarray(inp["state_rwkv_shift"][0, rs], np.float32).T),
             "wkv_st": np.ascontiguousarray(np.transpose(inp["state_rwkv_wkv"][0, rs], (0, 1, 3, 2)).reshape(NS, 8, 128, 64)),
             "pool_st": np.ascontiguousarray(np.transpose(inp["state_pool"][0, rs], (2, 0, 1)))}
        maps.append(m)
    return maps


_CACHE = {}


def run(inp, L, ncores=8, **kw):
    key = (L, tuple(sorted(kw.items())))
    if key not in _CACHE:
        _CACHE[key] = Builder(L, **kw).build()
    nc = _CACHE[key]
    maps = make_in_maps(inp, L, ncores)
    res = run_bass_kernel_spmd(nc, maps, core_ids=list(range(ncores)))
    return res.results


def gather(res, L):
    outs = {}
    outs["y"] = (np.stack([r["yT"][:, :L].T for r in res], 0),
                 np.concatenate([r["yT"][:, L:].T for r in res], 0)[:, None, :])
    outs["conv"] = (np.stack([r["conv_o"][:, 0:2].T for r in res], 0)[None],
                    np.concatenate([np.transpose(r["conv_o"][:, 2:].reshape(D, NS, 2), (1, 2, 0)) for r in res], 0)[None])
    outs["wkv"] = (np.stack([np.transpose(r["wkv_po"].reshape(16, 64, 64), (0, 2, 1)) for r in res], 0)[None],
                   np.concatenate([np.transpose(r["wkv_so"].reshape(NS, 16, 64, 64), (0, 1, 3, 2)) for r in res], 0)[None])
    outs["shift"] = (np.stack([r["shift_o"][:, 0] for r in res], 0)[None],
                     np.concatenate([r["shift_o"][:, 1:].T for r in res], 0)[None])
    outs["gla"] = (np.stack([r["gla_po"] for r in res], 0)[None], np.concatenate([r["gla_so"] for r in res], 0)[None])
    outs["pool"] = (np.stack([r["pool_o"][:, 0:15].T for r in res], 0)[None],
                    np.concatenate([np.transpose(r["pool_o"][:, 15:].reshape(D, NS, 15), (1, 2, 0)) for r in res], 0)[None])
    return outs


def kernel(**inputs):
    L = 2048
    res = run(inputs, L)
    o = gather(res, L)
    f = lambda a: np.ascontiguousarray(a, dtype=np.float32)
    return (f(o["y"][0]), f(o["y"][1]), f(o["wkv"][0]), f(o["wkv"][1]), f(o["shift"][0]), f(o["shift"][1]),
            f(o["gla"][0]), f(o["gla"][1]), f(o["conv"][0]), f(o["conv"][1]), f(o["pool"][0]), f(o["pool"][1]))
```

```python
import numpy as np
import contextlib
import concourse.bass as bass
import concourse.mybir as mybir
from concourse.bass_utils import run_bass_kernel_spmd

F32 = mybir.dt.float32
BF16 = mybir.dt.bfloat16
AF = mybir.ActivationFunctionType
ALU = mybir.AluOpType
AX = mybir.AxisListType

D = 1024
NCH = 8
DFF = 4096
NS = 16
NORM_EPS = 1e-6


import os as _os_mod
INLINE_WAIT = _os_mod.environ.get("SCHED_INLINE", "1") == "1"


class Sched:
    ENGS = ["tensor", "vector", "scalar", "gpsimd", "sync"]

    def __init__(self, nc, stack):
        self.nc = nc
        self.stack = stack
        self.ops = {e: [] for e in self.ENGS}
        self.semobj = {e: stack.enter_context(nc.semaphore("s_" + e)) for e in self.ENGS}
        self.cnt = {e: 0 for e in self.ENGS}
        self.waited = {e: {} for e in self.ENGS}
        self.W = {}
        self.R = {}
        self.dcnt = {}
        self.dkeys = {}

    def dma_sem(self, name):
        if name in self.dkeys:
            return self.dkeys[name]
        key = "d%d" % len(self.dkeys)
        self.semobj[key] = self.stack.enter_context(self.nc.semaphore(key))
        self.dcnt[key] = 0
        self.dkeys[name] = key
        return key

    def _deps(self, eng, reads, writes, acc=False):
        need = {}

        def add(k, v):
            if need.get(k, 0) < v:
                need[k] = v
        for r in reads:
            if r in self.W:
                add(*self.W[r])
        for w in writes:
            if w in self.W and not (acc and self.W[w][0] == eng):
                add(*self.W[w])
            for k, v in self.R.get(w, {}).items():
                add(k, v)
        out = []
        for k, v in need.items():
            if self.waited[eng].get(k, 0) >= v:
                continue
            self.waited[eng][k] = v
            out.append((k, v))
        return out

    def op(self, eng, meth, kw, reads=(), writes=(), acc=False):
        waits = self._deps(eng, reads, writes, acc)
        self.cnt[eng] += 1
        v = self.cnt[eng]
        self.ops[eng].append((waits, (meth, kw), (eng, 1)))
        for r in reads:
            self.R.setdefault(r, {})[eng] = v
        for w in writes:
            self.W[w] = (eng, v)
            self.R[w] = {}

    def dma(self, eng, semname, out, in_, reads=(), writes=()):
        semkey = self.dma_sem(semname)
        waits = self._deps(eng, reads, writes)
        self.dcnt[semkey] += 16
        v = self.dcnt[semkey]
        self.ops[eng].append((waits, ("dma_start", dict(out=out, in_=in_)), (semkey, 16)))
        for r in reads:
            self.R.setdefault(r, {})[semkey] = v
        for w in writes:
            self.W[w] = (semkey, v)
            self.R[w] = {}

    def barrier(self):
        tot = {}
        for e in self.ENGS:
            if self.cnt[e]:
                tot[e] = self.cnt[e]
        for k, v in self.dcnt.items():
            if v:
                tot[k] = v
        for e in self.ENGS:
            waits = []
            for k, v in tot.items():
                if self.waited[e].get(k, 0) >= v:
                    continue
                self.waited[e][k] = v
                waits.append((k, v))
            if waits:
                self.ops[e].append((waits, None, None))

    def replay(self, block):
        for e in self.ENGS:
            ops = self.ops[e]
            if not ops:
                continue

            def body(engine, ops=ops):
                for waits, fn, inc in ops:
                    inl = None
                    if INLINE_WAIT and fn is not None and waits and fn[0] != "dma_start":
                        inl = waits[-1]
                        waits = waits[:-1]
                    for k, v in waits:
                        engine.wait_ge(self.semobj[k], v)
                    if fn is not None:
                        try:
                            ins = getattr(engine, fn[0])(**fn[1])
                        except Exception:
                            print("FAILED OP:", fn[0], {k: (getattr(v, "tensor", None) and v.tensor.name, getattr(v, "shape", v)) for k, v in fn[1].items()})
                            raise
                        if inl is not None:
                            ins._wait_ge(self.semobj[inl[0]], inl[1])
                        ins.then_inc(self.semobj[inc[0]], inc[1])
            getattr(block, e)(body)


VEC_COLS = {}


def _vec_layout():
    names = []
    for i in range(4):
        names.append("norm_mix%d" % i)
    for i in range(4):
        names.append("norm_ffn%d" % i)
    names.append("norm_final")
    for j in range(6):
        names.append("mu%d" % j)
    for j in range(6):
        names.append("omu%d" % j)
    names += ["w0", "a0", "k_k", "k_a", "omk_a", "r_k", "ln_w", "ln_b",
              "conv_w0", "conv_w1", "conv_w2", "pool_scale"]
    col = 0
    for n in names:
        VEC_COLS[n] = col
        col += 8
    VEC_COLS["gla_b_gk"] = col
    col += 4
    VEC_COLS["gla_norm"] = col
    col += 2
    VEC_COLS["invc"] = col
    col += 16
    return col


NVEC = _vec_layout()


def make_cst():
    c = np.zeros((128, Builder.NCST), np.float32)
    c[:, Builder.C_ID:Builder.C_ID + 128] = np.eye(128, dtype=np.float32)
    c[:, Builder.C_TRIU:Builder.C_TRIU + 128] = np.triu(np.ones((128, 128), np.float32))
    su = np.triu(np.ones((64, 64), np.float32), 1)
    iu = np.triu(np.ones((64, 64), np.float32), 0)
    z = np.zeros((64, 64), np.float32)
    c[:, Builder.C_SUIU:Builder.C_SUIU + 128] = np.block([[su, z], [z, su]])
    c[:, Builder.C_SUIU + 128:Builder.C_SUIU + 256] = np.block([[iu, z], [z, iu]])
    c[:, Builder.C_SL:Builder.C_SL + 128] = np.block([[su.T, z], [z, su.T]])
    c[:, Builder.C_I2:Builder.C_I2 + 64] = np.concatenate([np.eye(64, dtype=np.float32)] * 2, 0)
    return c


class Builder:
    C_ID = 0
    C_TRIU = 128
    C_SUIU = 256
    C_SL = 512
    C_I2 = 640
    NCST = 704

    def __init__(self, L, layers=(1, 1, 1, 1), ffn=True, dbg=None):
        self.L = L
        self.NT = L + NS
        self.layers = layers
        self.ffn = ffn
        self.dbg = dbg
        tiles = []
        t = 0
        while t < L:
            n = min(512, L - t)
            tiles.append((t, n))
            t += n
        tiles.append((L, NS))
        self.tiles = tiles

    def sb(self, name, shape, dt, stack=None):
        self._uid = getattr(self, "_uid", 0) + 1
        return (stack or self.st).enter_context(self.nc.sbuf_tensor("%s_%d" % (name, self._uid), list(shape), dt))

    def dram_in(self, name, shape):
        return self.nc.dram_tensor(name, list(shape), F32, kind="ExternalInput").ap()

    def dram_out(self, name, shape):
        return self.nc.dram_tensor(name, list(shape), F32, kind="ExternalOutput").ap()

    def vcol(self, name, c=0):
        k = VEC_COLS[name] + c
        return self.vec[:, k:k + 1]

    def mm(self, out, lhsT, rhs, start, stop, reads, writes, nosw=False):
        self.S.op("tensor", "matmul", dict(out=out, lhsT=lhsT, rhs=rhs, start=start, stop=stop),
                  reads=reads, writes=writes, acc=(not start) or nosw)

    def act(self, out, in_, func, reads, writes, scale=1.0, bias=None, eng="scalar"):
        kw = dict(out=out, in_=in_, func=func, scale=scale)
        if bias is not None:
            kw["bias"] = bias
        self.S.op("scalar", "activation", kw, reads=reads, writes=writes)

    def tt(self, eng, out, in0, in1, op, reads, writes):
        self.S.op(eng, "tensor_tensor", dict(out=out, in0=in0, in1=in1, op=op), reads=reads, writes=writes)

    def ts(self, eng, out, in0, s1, op0, reads, writes, s2=None, op1=None):
        kw = dict(out=out, in0=in0, scalar1=s1, scalar2=s2, op0=op0)
        if op1 is not None:
            kw["op1"] = op1
        self.S.op(eng, "tensor_scalar", kw, reads=reads, writes=writes)

    def stt(self, out, in0, scalar, in1, op0, op1, reads, writes):
        self.S.op("vector", "scalar_tensor_tensor", dict(out=out, in0=in0, scalar=scalar, in1=in1, op0=op0, op1=op1),
                  reads=reads, writes=writes)

    def cp(self, eng, out, in_, reads, writes):
        self.S.op(eng, "tensor_copy", dict(out=out, in_=in_), reads=reads, writes=writes)

    def memset(self, eng, ap, val, writes):
        self.S.op(eng, "memset", dict(ap=ap, constant=val), writes=writes)

    def rms_rstd(self, ti, rstd, rstd_res):
        t0, n = self.tiles[ti]
        sq, pb = self.sq, self.ps[6]
        h = self.h
        self.act(sq[:, :, :n], h[:, :, t0:t0 + n], AF.Square, ["h%d" % ti], ["sq"])
        for c in range(NCH):
            self.mm(pb[:, :n], self.ones_bf[:, :], sq[:, c, :n], c == 0, c == NCH - 1, ["sq", "ones"], ["ps6"])
        self.act(rstd[:, :n], pb[:, :n], AF.Ln, ["ps6", "consts"], [rstd_res], scale=1.0 / D, bias=self.eps_t[:, 0:1])
        self.act(rstd[:, :n], rstd[:, :n], AF.Exp, [rstd_res], [rstd_res], scale=-0.5)

    def norm_tile(self, ti, gname, dst_fn, dst_res):
        t0, n = self.tiles[ti]
        rstd = self.rstd
        self.rms_rstd(ti, rstd, "rstd")
        h = self.h
        for c in range(NCH):
            self.stt(dst_fn(c), h[:, c, t0:t0 + n], self.vcol(gname, c), rstd[:, :n], ALU.mult, ALU.mult,
                     ["h%d" % ti, "rstd", "vec"], [dst_res])

    def load_w(self, dst, src, res, nsplit=1):
        k = dst.shape[1]
        srcv = src.rearrange("(k p) n -> p k n", p=128)
        step = max(1, k // nsplit)
        for a in range(0, k, step):
            b = min(k, a + step)
            self.S.dma("gpsimd", res, dst[:, a:b, :], srcv[:, a:b, :], writes=[res])

    def ffn_layer(self, i):
        saved_tiles = self.tiles
        nt_ = -(-self.NT // 512)
        base_, rem_ = divmod(self.NT, nt_)
        ft, t_ = [], 0
        for k_ in range(nt_):
            sz = base_ + (1 if k_ < rem_ else 0)
            ft.append((t_, sz))
            t_ += sz
        self.tiles = ft
        try:
            self._ffn_layer(i)
        finally:
            self.tiles = saved_tiles

    def _ffn_layer(self, i):
        S = self.S
        with contextlib.ExitStack() as ph:
            wup = [self.sb("wup%d" % b, [128, NCH, 1024], BF16, ph) for b in range(2)]
            wdn = [self.sb("wdn%d" % b, [128, NCH, 1024], BF16, ph) for b in range(2)]
            hid = self.sb("hid", [128, NCH, 512], BF16, ph)
            rtmp = [self.sb("rtmp%d" % b, [128, 512], BF16, ph) for b in range(2)]
            ub = self.ubuf
            h = self.h
            def ffn_load(q):
                b = q % 2
                self.load_w(wup[b][:, :, :], self.ffn_up[i, :, q * 1024:(q + 1) * 1024], "wup%d" % b, nsplit=2)
                self.load_w(wdn[b][:, :, :], self.ffn_down[i, q * 1024:(q + 1) * 1024, :], "wdn%d" % b, nsplit=2)
            ffn_load(0)
            for q in range(4):
                b = q % 2
                if q + 1 < 4:
                    ffn_load(q + 1)
                for ti, (t0, n) in enumerate(self.tiles):
                    if q == 0:
                        self.norm_tile(ti, "norm_ffn%d" % i, lambda c: ub[:, c, t0:t0 + n], "u%d" % ti)
                    for j in range(NCH):
                        pb = self.ps[j % 4]
                        pr = "ps%d" % (j % 4)
                        for c in range(NCH):
                            self.mm(pb[:, :n], wup[b][:, c, j * 128:(j + 1) * 128], ub[:, c, t0:t0 + n], c == 0, c == NCH - 1,
                                    ["wup%d" % b, "u%d" % ti], [pr])
                        rt = rtmp[j % 2]
                        self.act(rt[:, :n], pb[:, :n], AF.Relu, [pr], ["rtmp%d" % (j % 2)])
                        self.tt("gpsimd", hid[:, j, :n], rt[:, :n], rt[:, :n], ALU.mult, ["rtmp%d" % (j % 2)], ["hid%d" % j])
                    for d in range(NCH):
                        pb = self.ps[4 + d % 2]
                        pr = "ps%d" % (4 + d % 2)
                        for j in range(NCH):
                            self.mm(pb[:, :n], wdn[b][:, j, d * 128:(d + 1) * 128], hid[:, j, :n], j == 0, j == NCH - 1,
                                    ["wdn%d" % b, "hid%d" % j], [pr])
                        self.tt("vector", h[:, d, t0:t0 + n], h[:, d, t0:t0 + n], pb[:, :n], ALU.add, [pr, "h%d" % ti], ["h%d" % ti])
            S.barrier()

    def mix_norm_all(self, i, col0=0):
        ub = self.ubuf
        for ti, (t0, n) in enumerate(self.tiles):
            self.norm_tile(ti, "norm_mix%d" % i, lambda c: ub[:, c, col0 + t0:col0 + t0 + n], "u%d" % ti)

    def out_proj(self, xo, wo, wres, xres_fn):
        h = self.h
        for ti, (t0, n) in enumerate(self.tiles):
            for d in range(NCH):
                pb = self.ps[4 + d % 2]
                pr = "ps%d" % (4 + d % 2)
                for j in range(NCH):
                    self.mm(pb[:, :n], wo[:, j, d * 128:(d + 1) * 128], xo[:, j, t0:t0 + n], j == 0, j == NCH - 1,
                            [wres] + xres_fn(j, ti), [pr])
                self.tt("vector", h[:, d, t0:t0 + n], h[:, d, t0:t0 + n], pb[:, :n], ALU.add, [pr, "h%d" % ti], ["h%d" % ti])

    def rwkv_layer(self, i):
        S = self.S
        L, NT = self.L, self.NT
        h, ub, ps, vec = self.h, self.ubuf, self.ps, self.vec
        C = 64
        ntl = len(self.tiles)
        cst = self.cst
        V = VEC_COLS
        self.memset("vector", ub[:, :, 0:1], 0.0, ["ushift"])
        S.dma("gpsimd", "ushift", ub[:, :, NT + 1:NT + 1 + NS], self.shift_st.rearrange("(c p) r -> p c r", p=128), writes=["ushift"])
        with contextlib.ExitStack() as ph:
            sho = self.sb("sho", [128, NCH, 1 + NS], F32, ph)
            for ti, (t0, n) in enumerate(self.tiles):
                self.norm_tile(ti, "norm_mix%d" % i, lambda c: ub[:, c, 1 + t0:1 + t0 + n], "u%d" % ti)
                if t0 + n == L:
                    for c in range(NCH):
                        self.stt(sho[:, c, 0:1], h[:, c, L - 1:L], self.vcol("norm_mix%d" % i, c), self.rstd[:, n - 1:n], ALU.mult, ALU.mult,
                                 ["h%d" % ti, "rstd", "vec"], ["sho"])
                if t0 >= L:
                    for c in range(NCH):
                        self.stt(sho[:, c, 1:1 + NS], h[:, c, L:NT], self.vcol("norm_mix%d" % i, c), self.rstd[:, :NS], ALU.mult, ALU.mult,
                                 ["h%d" % ti, "rstd", "vec"], ["sho"])
            S.dma("sync", "sho", self.shift_o.rearrange("(c p) n -> p c n", p=128), sho[:, :, :], reads=["sho"])

            def u_ap(c, ti):
                t0, n = self.tiles[ti]
                return ub[:, c, 1 + t0:1 + t0 + n]

            def p_ap(c, ti):
                t0, n = self.tiles[ti]
                if t0 < L:
                    return ub[:, c, t0:t0 + n]
                return ub[:, c, NT + 1:NT + 1 + NS]

            def u_res(ti):
                return ["u%d" % ti, "ushift"] + (["u%d" % (ti - 1)] if ti > 0 else [])

            xv = self.sb("rxv", [128, 64], F32, ph)
            self.ts("vector", xv[:, 0:8], vec[:, V["w0"]:V["w0"] + 8], -1.0, ALU.mult, ["vec"], ["rxv"])
            self.ts("vector", xv[:, 8:16], vec[:, V["a0"]:V["a0"] + 8], -1.0, ALU.mult, ["vec"], ["rxv"])
            self.memset("vector", xv[:, 16:17], 1e-24, ["rxv"])
            for j in range(6):
                self.ts("vector", vec[:, V["omu%d" % j]:V["omu%d" % j] + 8], vec[:, V["mu%d" % j]:V["mu%d" % j] + 8], -1.0, ALU.mult,
                        ["vec"], ["vec"], s2=1.0, op1=ALU.add)
            self.ts("vector", vec[:, V["omk_a"]:V["omk_a"] + 8], vec[:, V["k_a"]:V["k_a"] + 8], -1.0, ALU.mult, ["vec"], ["vec"], s2=1.0, op1=ALU.add)
            onesblk = self.sb("onesblk", [128, 128], BF16, ph)
            self.memset("vector", onesblk[:, :], 0.0, ["onesblk"])
            self.memset("vector", onesblk[0:64, 0:64], 1.0, ["onesblk"])
            self.memset("vector", onesblk[64:128, 64:128], 1.0, ["onesblk"])
            i2b = self.sb("i2b", [128, 64], BF16, ph)
            self.cp("vector", i2b[:, :], cst[:, self.C_I2:self.C_I2 + 64], ["cst"], ["i2b"])
            cm64 = self.sb("cm64", [128, 512], F32, ph)
            self.memset("gpsimd", cm64[:, :], 1.0, ["cm64"])
            self.memset("gpsimd", cm64[:, :].rearrange("p (n k) -> p n k", k=64)[:, :, 0:1], 0.0, ["cm64"])

            stage = self.sb("rstage", [128, NCH, 160], F32, ph)
            self._eng_rr = 0

            def prep_w(src, m, muj, dA, dB, res):
                S.dma("sync", "rstage", stage[:, :, :m], src.rearrange("(c p) n -> p c n", p=128), writes=["rstage"])
                for c in range(NCH):
                    e = "vector"
                    self.ts(e, dA[:, c, :], stage[:, c, :m], vec[:, V["omu%d" % muj] + c:V["omu%d" % muj] + c + 1], ALU.mult, ["rstage", "vec"], [res])
                    self.ts(e, dB[:, c, :], stage[:, c, :m], vec[:, V["mu%d" % muj] + c:V["mu%d" % muj] + c + 1], ALU.mult, ["rstage", "vec"], [res])

            tw = self.sb("rtw", [64, NT], BF16, ph)
            ta = self.sb("rta", [64, NT], BF16, ph)
            tg0 = self.sb("rtg0", [128, NT], BF16, ph)
            tg1 = self.sb("rtg1", [32, NT], BF16, ph)
            tmpA = self.rstd
            with contextlib.ExitStack() as ph2:
                l1 = [self.sb("rl1_%d" % k, [128, NCH, m], BF16, ph2) for k, m in enumerate([64, 64, 64, 64, 160, 160])]
                prep_w(self.rwkv_w1, 64, 1, l1[0], l1[1], "rl1w")
                prep_w(self.rwkv_a1, 64, 4, l1[2], l1[3], "rl1a")
                prep_w(self.rwkv_g1, 160, 5, l1[4], l1[5], "rl1g")
                for ti, (t0, n) in enumerate(self.tiles):
                    specs = [(l1[0][:, :, :], l1[1][:, :, :], 64, "rl1w"), (l1[2][:, :, :], l1[3][:, :, :], 64, "rl1a"),
                             (l1[4][:, :, 0:128], l1[5][:, :, 0:128], 128, "rl1g"), (l1[4][:, :, 128:160], l1[5][:, :, 128:160], 32, "rl1g")]
                    for k, (wa, wb, m, res) in enumerate(specs):
                        for c in range(NCH):
                            self.mm(ps[k][:m, :n], wa[:, c, :], u_ap(c, ti), c == 0, False, [res] + u_res(ti), ["ps%d" % k])
                        for c in range(NCH):
                            self.mm(ps[k][:m, :n], wb[:, c, :], p_ap(c, ti), False, c == NCH - 1, [res] + u_res(ti), ["ps%d" % k])
                    self.act(tmpA[:64, :n], ps[0][:64, :n], AF.Exp, ["ps0"], ["rstd"], scale=-2.0)
                    self.ts("vector", tmpA[:64, :n], tmpA[:64, :n], 1.0, ALU.add, ["rstd"], ["rstd"])
                    S.op("vector", "reciprocal", dict(out=tmpA[:64, :n], in_=tmpA[:64, :n]), reads=["rstd"], writes=["rstd"])
                    self.ts("vector", tw[:, t0:t0 + n], tmpA[:64, :n], 2.0, ALU.mult, ["rstd"], ["rtw"], s2=-1.0, op1=ALU.add)
                    self.act(ta[:, t0:t0 + n], ps[1][:64, :n], AF.Copy, ["ps1"], ["rta"])
                    for k, (dst, m) in ((2, (tg0, 128)), (3, (tg1, 32))):
                        self.act(tmpA[:m, :n], ps[k][:m, :n], AF.Exp, ["ps%d" % k], ["rstd"], scale=-1.0)
                        self.ts("vector", tmpA[:m, :n], tmpA[:m, :n], 1.0, ALU.add, ["rstd"], ["rstd"])
                        S.op("vector", "reciprocal", dict(out=tmpA[:m, :n], in_=tmpA[:m, :n]), reads=["rstd"], writes=["rstd"])
                        self.cp("vector", dst[:, t0:t0 + n], tmpA[:m, :n], ["rstd"], ["rtg%d" % (k - 2)])
                S.barrier()

            F = [self.sb("rF%d" % k, [128, 512], F32, ph) for k in range(12)]
            r_f, k_f, v_f, lw, a_f, g_f, kkn, k2, bon, cl, E1, E3 = F
            E2 = k_f
            BB = [bon, self.rstd]
            GF = [self.sq[:, 2, :], self.sq[:, 3, :]]
            yfm_sb = g_f
            gam = self.sb("rgam", [128, 2, 16], F32, ph)
            sqb = self.sq[:, 0, :]
            wpr = [self.sb("rwp%d" % k, [128, NCH, 128], BF16, ph) for k in range(6)]
            w2c = self.sb("rw2c", [64, 128], BF16, ph)
            a2c = self.sb("ra2c", [64, 128], BF16, ph)
            g2c0 = self.sb("rg2c0", [128, 128], BF16, ph)
            g2c1 = self.sb("rg2c1", [32, 128], BF16, ph)
            woc = self.sb("rwoc", [128, 1024], BF16, ph)
            xot = self.sq[:, 1, :]
            NK = 8
            pads = {}
            for kind in ("p",):
                pads[kind] = dict(
                    AR=self.sb("rAR" + kind, [128, NK, 4, C], BF16, ph),
                    B=self.sb("rB" + kind, [128, NK, 2, C], BF16, ph),
                    K=self.sb("rK" + kind, [128, NK, 2, C], BF16, ph),
                    V=self.sb("rV" + kind, [128, NK, 2, C], BF16, ph),
                    BH=self.sb("rBH" + kind, [128, NK, 2, C], BF16, ph),
                    KH=self.sb("rKH" + kind, [128, NK, 2, C], BF16, ph))
                for nm, t_ in pads[kind].items():
                    self.memset("gpsimd", t_[:, :, :, :], 0.0, ["pad" + kind + nm])
            import os as _os
            WSL = int(_os.environ.get("RW_WSL", "2"))
            Gs = self.sb("rGs", [128, NK, 128], BF16, ph)
            Zb = self.sb("rZb", [128, NK, 64], BF16, ph)
            Rqs = self.sb("rRqs", [128, NK, 128], BF16, ph)
            Yz = self.sb("rYz", [128, NK, 64], BF16, ph)
            SL = []
            for s_ in range(WSL):
                SL.append(dict(
                    XQ=[self.sb("rXQ%d_%d" % (s_, b_), [128, 256], BF16, ph) for b_ in range(2)],
                    Xt=[self.sb("rXt%d_%d" % (s_, b_), [128, 128], BF16, ph) for b_ in range(2)],
                    MM2=self.sb("rMM2_%d" % s_, [128, 256], BF16, ph),
                    MRB=self.sb("rMRB_%d" % s_, [128, 128], BF16, ph),
                    BhT=self.sb("rBhT_%d" % s_, [128, 128], BF16, ph),
                    KhT=self.sb("rKhT_%d" % s_, [128, 128], BF16, ph),
                    AT=self.sb("rAT_%d" % s_, [128, 128], BF16, ph),
                    AqT=self.sb("rAqT_%d" % s_, [128, 128], BF16, ph),
                    Vst=self.sb("rVst_%d" % s_, [128, 64], BF16, ph),
                    MakV=self.sb("rMakV_%d" % s_, [128, 64], BF16, ph),
                    Vq=self.sb("rVq_%d" % s_, [128, 64], BF16, ph)))
            Tfs = [self.sb("rTf%d" % b_, [128, 64], F32, ph) for b_ in range(2)]
            Tbs = [self.sb("rTb%d" % b_, [128, 64], BF16, ph) for b_ in range(2)]
            Tf, Tb = Tfs[0], Tbs[0]
            NYB = 4
            Ypad = [self.sb("rYpad%d" % b_, [128, 2, 64], BF16, ph) for b_ in range(NYB)]
            for b_ in range(NYB):
                self.memset("gpsimd", Ypad[b_][:, :, :], 0.0, ["rYpad%d" % b_])
            bsts = [self.sb("rbst%d" % b_, [128, 6], F32, ph) for b_ in range(NYB)]
            bags = [self.sb("rbag%d" % b_, [128, 4], F32, ph) for b_ in range(NYB)]
            Ysb = [self.sb("rYsb%d" % b_, [128, 64], F32, ph) for b_ in range(NYB)]
            ysmp = self.sb("rysmp", [128, NS], F32, ph)
            MSUIU = cst[:, self.C_SUIU:self.C_SUIU + 256]
            MSU = cst[:, self.C_SUIU:self.C_SUIU + 128]
            MIU = cst[:, self.C_SUIU + 128:self.C_SUIU + 256]
            MSL = cst[:, self.C_SL:self.C_SL + 128]
            IDN = cst[:, self.C_ID:self.C_ID + 128]
            wkv_st = self.wkv_st

            for cc in range(NCH):
                cs = slice(cc * 128, (cc + 1) * 128)
                for k, (widx, muj) in enumerate(((0, 0), (1, 2), (2, 3))):
                    prep_w(self.rwkv_w_rkv[widx][:, cs], 128, muj, wpr[2 * k], wpr[2 * k + 1], "rwp%d" % k)
                S.dma("gpsimd", "rw2c", w2c[:, :], self.rwkv_w2[:, cs], writes=["rw2c"])
                S.dma("gpsimd", "ra2c", a2c[:, :], self.rwkv_a2[:, cs], writes=["ra2c"])
                S.dma("gpsimd", "rg2c0", g2c0[:, :], self.rwkv_g2[0:128, cs], writes=["rg2c"])
                S.dma("gpsimd", "rg2c1", g2c1[:, :], self.rwkv_g2[128:160, cs], writes=["rg2c"])
                S.dma("gpsimd", "rwoc", woc[:, :], self.rwkv_wo[cs, :], writes=["rwoc"])
                self.memset("vector", Tf[:, :], 0.0, ["rTf0"])
                self.memset("vector", Tb[:, :], 0.0, ["rTb0"])
                def p1a(ti, info):
                    t0, n = self.tiles[ti]
                    par = ti % 2
                    samp = t0 >= L
                    if False:
                        yield
                    for k, (dst_, dres_) in enumerate(((r_f, "r_f"), (k_f, "k_f"), (v_f, "v_f"))):
                        for c in range(NCH):
                            self.mm(ps[6][:, :n], wpr[2 * k][:, c, :], u_ap(c, ti), c == 0, False, ["rwp%d" % k] + u_res(ti), ["ps6"])
                            if c % 4 == 3:
                                yield
                        for c in range(NCH):
                            self.mm(ps[6][:, :n], wpr[2 * k + 1][:, c, :], p_ap(c, ti), False, c == NCH - 1, ["rwp%d" % k] + u_res(ti), ["ps6"])
                            if c % 4 == 3:
                                yield
                        self.act(dst_[:, :n], ps[6][:, :n], AF.Copy, ["ps6"], [dres_])
                        yield
                    self.mm(ps[6][:, :n], w2c[:, :], tw[:, t0:t0 + n], True, True, ["rw2c", "rtw"], ["ps6"])
                    self.act(lw[:, :n], ps[6][:, :n], AF.Exp, ["ps6", "rxv"], ["lw"], scale=-1.0, bias=xv[:, cc:cc + 1])
                    yield
                    self.mm(ps[6][:, :n], a2c[:, :], ta[:, t0:t0 + n], True, True, ["ra2c", "rta"], ["ps6"])
                    self.ts("vector", lw[:, :n], lw[:, :n], 1.0, ALU.add, ["lw"], ["lw"])
                    S.op("vector", "reciprocal", dict(out=lw[:, :n], in_=lw[:, :n]), reads=["lw"], writes=["lw"])
                    self.ts("vector", lw[:, :n], lw[:, :n], -0.6065306597126334, ALU.mult, ["lw"], ["lw"])
                    yield
                    yield
                    self.act(a_f[:, :n], ps[6][:, :n], AF.Exp, ["ps6", "rxv"], ["a_f"], scale=-1.0, bias=xv[:, 8 + cc:9 + cc])
                    yield
                    self.mm(ps[6][:, :n], g2c0[:, :], tg0[:, t0:t0 + n], True, False, ["rg2c", "rtg0"], ["ps6"])
                    self.mm(ps[6][:, :n], g2c1[:, :], tg1[:, t0:t0 + n], False, True, ["rg2c", "rtg1"], ["ps6"])
                    self.ts("vector", a_f[:, :n], a_f[:, :n], 1.0, ALU.add, ["a_f"], ["a_f"])
                    S.op("vector", "reciprocal", dict(out=a_f[:, :n], in_=a_f[:, :n]), reads=["a_f"], writes=["a_f"])
                    yield
                    self.act(GF[par][:, :n], ps[6][:, :n], AF.Copy, ["ps6"], ["GF%d" % par])
                    yield
                    self.ts("vector", kkn[:, :n], k_f[:, :n], self.vcol("k_k", cc), ALU.mult, ["k_f", "vec"], ["kkn"])
                    self.tt("gpsimd", sqb[:, :n], kkn[:, :n], kkn[:, :n], ALU.mult, ["kkn"], ["rsqb"])
                    yield
                    self.mm(ps[6][:, :n], onesblk[:, :], sqb[:, :n], True, True, ["onesblk", "rsqb"], ["ps6"])
                    yield
                    self.act(E1[:, :n], ps[6][:, :n], AF.Ln, ["ps6", "rxv"], ["E1"], bias=xv[:, 16:17])
                    self.act(E1[:, :n], E1[:, :n], AF.Exp, ["E1"], ["E1"], scale=-0.5)
                    self.tt("vector", kkn[:, :n], kkn[:, :n], E1[:, :n], ALU.mult, ["kkn", "E1"], ["kkn"])
                    self.ts("vector", k2[:, :n], a_f[:, :n], self.vcol("k_a", cc), ALU.mult, ["a_f", "vec"], ["k2"], s2=self.vcol("omk_a", cc), op1=ALU.add)
                    self.tt("vector", k2[:, :n], k2[:, :n], k_f[:, :n], ALU.mult, ["k2", "k_f"], ["k2"])
                    yield
                    self.stt(sqb[:, :n], r_f[:, :n], self.vcol("r_k", cc), k2[:, :n], ALU.mult, ALU.mult, ["r_f", "k2", "vec", "rsqb"], ["rsqb"])
                    yield
                    self.mm(ps[6][:, :n], onesblk[:, :], sqb[:, :n], True, True, ["onesblk", "rsqb"], ["ps6"])
                    yield
                    self.tt("vector", BB[par][:, :n], ps[6][:, :n], v_f[:, :n], ALU.mult, ["ps6", "v_f"], ["BB%d" % par])
                    self.ts("vector", BB[par][:, :n], BB[par][:, :n], self.vcol("ln_b", cc), ALU.add, ["BB%d" % par, "vec"], ["BB%d" % par])
                    yield
                    self.tt("gpsimd", a_f[:, :n], a_f[:, :n], kkn[:, :n], ALU.mult, ["a_f", "kkn"], ["a_f"])
                    bv = a_f
                    if not samp:
                        S.op("vector", "tensor_tensor_scan", dict(out=cl[:, :n], data0=cm64[:, :n], data1=lw[:, :n], initial=0.0,
                                                                  op0=ALU.mult, op1=ALU.add), reads=["lw", "cm64"], writes=["cl"])
                    else:
                        self.cp("vector", cl[:, :n], lw[:, :n], ["lw"], ["cl"])
                    yield
                    self.act(E1[:, :n], cl[:, :n], AF.Exp, ["cl"], ["E1"])
                    self.act(E2[:, :n], cl[:, :n], AF.Exp, ["cl", "k_f"], ["k_f"], scale=-1.0)
                    self.tt("gpsimd", E3[:, :n], cl[:, :n], lw[:, :n], ALU.subtract, ["cl", "lw"], ["E3"])
                    self.act(E3[:, :n], E3[:, :n], AF.Exp, ["E3"], ["E3"])
                    yield
                    E4 = lw
                    if not samp:
                        nck = n // C
                        for q in range(nck):
                            last = q * C + C - 1
                            self.ts("vector", E4[:, q * C:(q + 1) * C], cl[:, q * C:(q + 1) * C], -1.0, ALU.mult,
                                    ["cl", "lw", "E3"], ["lw"], s2=cl[:, last:last + 1], op1=ALU.add)
                        self.act(E4[:, :n], E4[:, :n], AF.Exp, ["lw"], ["lw"])
                        yield
                        self.cp("vector", gam[:, par, 0:nck], E1[:, 0:n].rearrange("p (q k) -> p q k", k=C)[:, :, C - 1], ["E1"], ["gam%d" % par])
                        groups = [("p", 0, nck)]
                    else:
                        self.memset("vector", E4[:, :n], 1.0, ["lw"])
                        self.cp("vector", gam[:, par, 0:NS], E1[:, 0:NS], ["E1"], ["gam%d" % par])
                        groups = [("s", 0, NK), ("s", NK, NK)]
                    info["groups"] = groups
                    yield

                infos = [dict() for _ in range(ntl)]
                g_first = p1a(0, infos[0])
                for _ in g_first:
                    pass
                for ti, (t0, n) in enumerate(self.tiles):
                    samp = t0 >= L
                    par = ti % 2
                    bv = a_f
                    E4 = lw
                    groups = infos[ti]["groups"]
                    nxt_gen = p1a(ti + 1, infos[ti + 1]) if ti + 1 < ntl else None
                    if samp:
                        nxt_gen = None
                    for (kind, g0, gn) in groups:
                        P = pads["p"]
                        if kind == "s":
                            for nm, t_ in P.items():
                                nb = 4 if nm == "AR" else 2
                                for hh in range(2):
                                    hs = slice(64 * hh, 64 * hh + 64)
                                    blks = [hh, 2 + hh] if nm == "AR" else [hh]
                                    for blk in blks:
                                        self.memset("gpsimd" if hh else "vector", t_[hs, :, blk, :], 0.0, ["padp" + nm])

                        def pv(t_, hh, blk):
                            hs = slice(64 * hh, 64 * hh + 64)
                            if kind == "p":
                                return t_[hs, 0:gn, blk, :]
                            return t_[hs, 0:gn, blk, 0]

                        def fv(t_, hh):
                            hs = slice(64 * hh, 64 * hh + 64)
                            if kind == "p":
                                return t_[hs, 0:n].rearrange("p (q k) -> p q k", k=C)
                            return t_[hs, g0:g0 + gn]
                        for hh in range(2):
                            e1 = "vector" if hh == 0 else "gpsimd"
                            self.stt(pv(P["AR"], hh, hh), fv(kkn, hh), -1.0, fv(E3, hh), ALU.mult, ALU.mult, ["kkn", "E3"], ["padpAR"])
                            self.tt(e1, pv(P["AR"], hh, 2 + hh), fv(r_f, hh), fv(E1, hh), ALU.mult, ["r_f", "E1"], ["padpAR"])
                            self.tt(e1, pv(P["B"], hh, hh), fv(bv, hh), fv(E2, hh), ALU.mult, ["a_f", "k_f"], ["padpB"])
                            self.tt(e1, pv(P["K"], hh, hh), fv(k2, hh), fv(E2, hh), ALU.mult, ["k2", "k_f"], ["padpK"])
                            self.cp(e1, pv(P["V"], hh, hh), fv(v_f, hh), ["v_f"], ["padpV"])
                            self.tt(e1, pv(P["BH"], hh, hh), fv(bv, hh), fv(E4, hh), ALU.mult, ["a_f", "lw"], ["padpBH"])
                            self.tt(e1, pv(P["KH"], hh, hh), fv(k2, hh), fv(E4, hh), ALU.mult, ["k2", "lw"], ["padpKH"])
                        pr = lambda nm: "padp" + nm
                        identb = self.identb

                        def indep(q, sidx):
                            sl = SL[sidx]
                            bA, bB = ps[2 * sidx], ps[2 * sidx + 1]
                            rA, rB = "ps%d" % (2 * sidx), "ps%d" % (2 * sidx + 1)
                            sr = lambda nm: "sl%d_%s" % (sidx, nm)
                            ARq = P["AR"][:, q, :, :]
                            Aq = P["AR"][:, q, 0:2, :]
                            Rq_ = P["AR"][:, q, 2:4, :]
                            Bq = P["B"][:, q, :, :]
                            Kq = P["K"][:, q, :, :]
                            XQ, Xt = sl["XQ"], sl["Xt"]
                            tb0 = sidx * 256
                            psT = self.psT
                            XQE = "gpsimd" if _os.environ.get("RW_XQPOOL", "1") == "1" else "vector"
                            if "all" in _os.environ.get("RW_SKIP", ""):
                                return
                            yield

                            def tr(dst, src, res):
                                S.op("tensor", "transpose", dict(out=dst, in_=src, identity=identb[:, :]), reads=[res, "identb"], writes=["psT"])
                            if kind == "p":
                                self.mm(bA[:, 0:256], Bq, ARq, True, True, [pr("B"), pr("AR")], [rA])
                                self.mm(bB[:, 0:128], Aq, Bq, True, True, [pr("B"), pr("AR")], [rB])
                                tr(psT[:, tb0:tb0 + 128], P["BH"][:, q, :, :], pr("BH"))
                                tr(psT[:, tb0 + 128:tb0 + 256], P["KH"][:, q, :, :], pr("KH"))
                                yield
                                self.tt("vector", XQ[0][:, 0:128], bA[:, 0:128], MSU, ALU.mult, [rA, "cst"], [sr("XQ0")])
                                self.tt("vector", sl["MRB"][:, :], bA[:, 128:256], MIU, ALU.mult, [rA, "cst"], [sr("MRB")])
                                self.tt("vector", Xt[0][:, :], bB[:, 0:128], MSL, ALU.mult, [rB, "cst"], [sr("Xt0")])
                                self.act(sl["BhT"][:, :], psT[:, tb0:tb0 + 128], AF.Copy, ["psT"], [sr("BhT")])
                                self.act(sl["KhT"][:, :], psT[:, tb0 + 128:tb0 + 256], AF.Copy, ["psT"], [sr("KhT")])
                                yield
                                self.tt(XQE, XQ[0][:, 128:256], XQ[0][:, 0:128], IDN, ALU.add, [sr("XQ0"), "cst"], [sr("XQ0")])
                                self.mm(bA[:, 0:256], Kq, ARq, True, True, [pr("K"), pr("AR")], [rA])
                                self.mm(bB[:, 0:64], P["V"][:, q, :, :], i2b[:, :], True, True, [pr("V"), "i2b"], [rB])
                                tr(psT[:, tb0:tb0 + 128], Aq, pr("AR"))
                                yield
                                self.tt("vector", sl["MM2"][:, :], bA[:, 0:256], MSUIU, ALU.mult, [rA, "cst"], [sr("MM2")])
                                self.act(sl["Vst"][:, :], bB[:, 0:64], AF.Copy, [rB], [sr("Vst")])
                                self.act(sl["AT"][:, :], psT[:, tb0:tb0 + 128], AF.Copy, ["psT"], [sr("AT")])
                                yield
                                cur = 0
                                for lev in range(6):
                                    nxt = 1 - cur
                                    xc, xn = sr("XQ%d" % cur), sr("XQ%d" % nxt)
                                    tc, tn = sr("Xt%d" % cur), sr("Xt%d" % nxt)
                                    if lev == 0:
                                        self.mm(bA[:, 0:128], Xt[cur][:, :], XQ[cur][:, 0:128], True, True, [tc, xc], [rA])
                                        self.mm(bB[:, 0:128], XQ[cur][:, 0:128], Xt[cur][:, :], True, True, [tc, xc], [rB])
                                        yield
                                        self.act(XQ[nxt][:, 0:128], bA[:, 0:128], AF.Copy, [rA], [xn])
                                        self.cp(XQE, XQ[nxt][:, 128:256], XQ[cur][:, 128:256], [xc], [xn])
                                        self.cp("vector", Xt[nxt][:, :], bB[:, 0:128], [rB], [tn])
                                        yield
                                    elif lev < 5:
                                        self.mm(bA[:, 0:256], Xt[cur][:, :], XQ[cur][:, :], True, True, [tc, xc], [rA])
                                        self.mm(bB[:, 0:128], XQ[cur][:, 0:128], Xt[cur][:, :], True, True, [tc, xc], [rB])
                                        yield
                                        self.act(XQ[nxt][:, 0:128], bA[:, 0:128], AF.Copy, [rA], [xn])
                                        self.tt("vector", XQ[nxt][:, 128:256], bA[:, 128:256], XQ[cur][:, 128:256], ALU.add, [rA, xc], [xn])
                                        self.cp("vector", Xt[nxt][:, :], bB[:, 0:128], [rB], [tn])
                                        yield
                                    else:
                                        self.mm(bA[:, 0:128], Xt[cur][:, :], XQ[cur][:, 128:256], True, True, [tc, xc], [rA])
                                        self.mm(bB[:, 0:64], sl["MM2"][:, 0:128], sl["Vst"][:, :], True, True, [sr("MM2"), sr("Vst")], [rB])
                                        yield
                                        self.tt("vector", XQ[nxt][:, 128:256], bA[:, 0:128], XQ[cur][:, 128:256], ALU.add, [rA, xc], [xn])
                                        self.act(sl["MakV"][:, :], bB[:, 0:64], AF.Copy, [rB], [sr("MakV")])
                                        yield
                                    cur = nxt
                                Q = XQ[cur][:, 128:256]
                                qres = sr("XQ%d" % cur)
                                self.mm(bA[:, 0:128], Q, sl["AT"][:, :], True, True, [qres, sr("AT")], [rA])
                                self.mm(bB[:, 0:64], Q, sl["MakV"][:, :], True, True, [qres, sr("MakV")], [rB])
                                yield
                                self.act(sl["AqT"][:, :], bA[:, 0:128], AF.Copy, [rA], [sr("AqT")])
                                self.cp("vector", sl["Vq"][:, :], bB[:, 0:64], [rB], [sr("Vq")])
                                yield
                                AqT, aqres = sl["AqT"], sr("AqT")
                            else:
                                self.mm(bA[:, 0:128], Bq, Rq_, True, True, [pr("B"), pr("AR")], [rA])
                                self.mm(bB[:, 0:128], Kq, Rq_, True, True, [pr("K"), pr("AR")], [rB])
                                tr(psT[:, tb0:tb0 + 128], P["BH"][:, q, :, :], pr("BH"))
                                tr(psT[:, tb0 + 128:tb0 + 256], P["KH"][:, q, :, :], pr("KH"))
                                yield
                                self.tt("vector", sl["MRB"][:, :], bA[:, 0:128], MIU, ALU.mult, [rA, "cst"], [sr("MRB")])
                                self.tt("vector", sl["MM2"][:, 128:256], bB[:, 0:128], MIU, ALU.mult, [rB, "cst"], [sr("MM2")])
                                self.act(sl["BhT"][:, :], psT[:, tb0:tb0 + 128], AF.Copy, ["psT"], [sr("BhT")])
                                self.act(sl["KhT"][:, :], psT[:, tb0 + 128:tb0 + 256], AF.Copy, ["psT"], [sr("KhT")])
                                yield
                                self.mm(bB[:, 0:64], P["V"][:, q, :, :], i2b[:, :], True, True, [pr("V"), "i2b"], [rB])
                                tr(psT[:, tb0:tb0 + 128], Aq, pr("AR"))
                                yield
                                self.act(sl["Vst"][:, :], bB[:, 0:64], AF.Copy, [rB], [sr("Vst")])
                                self.cp("vector", sl["AT"][:, :], psT[:, tb0:tb0 + 128], ["psT"], [sr("AT")])
                                yield
                                AqT, aqres = sl["AT"], sr("AT")
                            self.mm(bA[:, 0:128], AqT[:, :], sl["BhT"][:, :], True, True, [aqres, sr("BhT")], [rA])
                            self.mm(bB[:, 0:128], AqT[:, :], sl["MRB"][:, :], True, True, [aqres, sr("MRB")], [rB])
                            yield
                            self.act(Gs[:, q, :], bA[:, 0:128], AF.Copy, [rA], ["rGs%d" % q])
                            self.tt("vector", Rqs[:, q, :], bB[:, 0:128], Rq_, ALU.add, [rB, pr("AR")], ["rRqs%d" % q])
                            yield
                            if kind == "p":
                                self.mm(bA[:, 0:64], sl["BhT"][:, :], sl["Vq"][:, :], True, False, [sr("BhT"), sr("Vq")], [rA])
                                self.mm(bA[:, 0:64], sl["KhT"][:, :], sl["Vst"][:, :], False, True, [sr("KhT"), sr("Vst")], [rA])
                                self.mm(bB[:, 0:64], sl["MRB"][:, :], sl["Vq"][:, :], True, False, [sr("MRB"), sr("Vq")], [rB])
                                self.mm(bB[:, 0:64], sl["MM2"][:, 128:256], sl["Vst"][:, :], False, True, [sr("MM2"), sr("Vst")], [rB])
                            else:
                                self.mm(bA[:, 0:64], sl["KhT"][:, :], sl["Vst"][:, :], True, True, [sr("KhT"), sr("Vst")], [rA])
                                self.mm(bB[:, 0:64], sl["MM2"][:, 128:256], sl["Vst"][:, :], True, True, [sr("MM2"), sr("Vst")], [rB])
                            yield
                            self.act(Zb[:, q, :], bA[:, 0:64], AF.Copy, [rA], ["rZb%d" % q])
                            self.cp("vector", Yz[:, q, :], bB[:, 0:64], [rB], ["rYz%d" % q])
                            yield

                        def sidegen(q, yb, row):
                            bst, bag, ysb, ypad = bsts[yb], bags[yb], Ysb[yb], Ypad[yb]
                            yield
                            self.act(bag[:, 2:3], bag[:, 1:2], AF.Ln, ["rbag%d" % yb, "consts"], ["rbag%d" % yb], bias=self.eps_t[:, 3:4])
                            self.act(bag[:, 2:3], bag[:, 2:3], AF.Exp, ["rbag%d" % yb], ["rbag%d" % yb], scale=-0.5)
                            yield
                            for hh in range(2):
                                hs = slice(64 * hh, 64 * hh + 64)
                                self.ts("vector", ypad[hs, hh, :], ysb[hs, :], bag[hs, 0:1], ALU.subtract,
                                        ["rYsb%d" % yb, "rbag%d" % yb], ["rYpad%d" % yb], s2=bag[hs, 2:3], op1=ALU.mult)
                            yield
                            self.mm(ps[5][:, 64:128], ypad[:, :, :], i2b[:, :], True, True, ["rYpad%d" % yb, "i2b"], ["ps5"])
                            yield
                            if kind == "p":
                                self.act(yfm_sb[:, q * C:(q + 1) * C], ps[5][:, 64:128], AF.Copy, ["ps5"], ["yfm"])
                            else:
                                self.cp("vector", yfm_sb[:, row:row + 1], ps[5][:, 64:65], ["ps5"], ["yfm"])
                            yield

                        def seqgen(done):
                            if "seq" in _os.environ.get("RW_SKIP", ""):
                                return
                            sides = []

                            def adv_sides():
                                for sd in list(sides):
                                    try:
                                        next(sd)
                                    except StopIteration:
                                        sides.remove(sd)
                            for q in range(gn):
                                while q not in done:
                                    adv_sides()
                                    yield
                                row = g0 + q
                                if kind == "s":
                                    tb_i = q % 2
                                    S.dma("sync", "rTf%d" % tb_i, Tfs[tb_i][:, :], wkv_st[row, cc], writes=["rTf%d" % tb_i])
                                    self.cp("vector", Tbs[tb_i][:, :], Tfs[tb_i][:, :], ["rTf%d" % tb_i], ["rTb%d" % tb_i])
                                    gcol = gam[:, par, row:row + 1]
                                else:
                                    tb_i = 0
                                    gcol = gam[:, par, q:q + 1]
                                tf_, tb_ = Tfs[tb_i], Tbs[tb_i]
                                tfr, tbr = "rTf%d" % tb_i, "rTb%d" % tb_i
                                yb = q % NYB
                                self.mm(ps[4][:, 0:64], Gs[:, q, :], tb_[:, :], True, False, ["rGs%d" % q, tbr], ["ps4"])
                                self.mm(ps[4][:, 0:64], identb[:, :], Zb[:, q, :], False, True, ["identb", "rZb%d" % q], ["ps4"])
                                self.mm(ps[5][:, 0:64], Rqs[:, q, :], tb_[:, :], True, False, ["rRqs%d" % q, tbr], ["ps5"])
                                self.mm(ps[5][:, 0:64], identb[:, :], Yz[:, q, :], False, True, ["identb", "rYz%d" % q], ["ps5"])
                                adv_sides()
                                yield
                                self.stt(tb_[:, :], tf_[:, :], gcol, ps[4][:, 0:64], ALU.mult, ALU.add, [tfr, "gam%d" % par, "ps4"], [tbr])
                                self.cp("vector", Ysb[yb][:, :], ps[5][:, 0:64], ["ps5"], ["rYsb%d" % yb])
                                self.stt(tf_[:, :], tf_[:, :], gcol, ps[4][:, 0:64], ALU.mult, ALU.add, [tfr, "gam%d" % par, "ps4"], [tfr])
                                if kind == "s":
                                    S.dma("sync", "rTfo%d" % tb_i, self.wkv_so[row, cc], tf_[:, :], reads=[tfr])
                                S.op("vector", "bn_stats", dict(out=bsts[yb][:, :], in_=Ysb[yb][:, :]), reads=["rYsb%d" % yb], writes=["rbst%d" % yb])
                                S.op("vector", "bn_aggr", dict(out=bags[yb][:, 0:2], in_=bsts[yb][:, :]), reads=["rbst%d" % yb], writes=["rbag%d" % yb])
                                sd = sidegen(q, yb, row)
                                next(sd)
                                sides.append(sd)
                                yield
                            while sides:
                                adv_sides()
                                yield

                        from collections import deque
                        pend = deque(range(gn))
                        slots = [None] * WSL
                        done = set()
                        sg = seqgen(done)
                        seq_alive = True
                        nx_alive = nxt_gen is not None and (kind == "p" or g0 > 0 or True)
                        while pend or any(x is not None for x in slots) or seq_alive or nx_alive:
                            if nx_alive:
                                try:
                                    next(nxt_gen)
                                except StopIteration:
                                    nx_alive = False
                            for sidx in range(WSL):
                                if slots[sidx] is None and pend:
                                    q_ = pend.popleft()
                                    slots[sidx] = (q_, indep(q_, sidx))
                                if slots[sidx] is not None:
                                    q_, g_ = slots[sidx]
                                    try:
                                        next(g_)
                                    except StopIteration:
                                        done.add(q_)
                                        slots[sidx] = None
                            if seq_alive:
                                try:
                                    next(sg)
                                except StopIteration:
                                    seq_alive = False
                    if not samp and t0 + n == L:
                        S.dma("sync", "rTfo0", self.wkv_po[cc], Tf[:, :], reads=["rTf0"])
                    self.stt(BB[par][:, :n], yfm_sb[:, :n], self.vcol("ln_w", cc), BB[par][:, :n], ALU.mult, ALU.add, ["yfm", "BB%d" % par, "vec"], ["BB%d" % par])
                    self.tt("vector", xot[:, :n], BB[par][:, :n], GF[par][:, :n], ALU.mult, ["BB%d" % par, "GF%d" % par], ["rxot"])
                    for d in range(NCH):
                        pb = ps[d % 2]
                        prr = "ps%d" % (d % 2)
                        self.mm(pb[:, :n], woc[:, d * 128:(d + 1) * 128], xot[:, :n], True, True, ["rwoc", "rxot"], [prr])
                        self.tt("vector", h[:, d, t0:t0 + n], h[:, d, t0:t0 + n], pb[:, :n], ALU.add, [prr, "h%d" % ti], ["h%d" % ti])
            S.barrier()

    def gla_layer(self, i):
        S = self.S
        L, NT = self.L, self.NT
        h, ub, ps = self.h, self.ubuf, self.ps
        CG = 128
        NCK = L // CG
        ntl = len(self.tiles)
        allu = ["u%d" % ti for ti in range(ntl)]
        self.mix_norm_all(i)
        w_in = self.gla_w_in
        with contextlib.ExitStack() as ph:
            self.cm128 = self.sb("cm128", [128, L], F32, ph)
            self.memset("gpsimd", self.cm128[:, :], 1.0, ["cm"])
            self.memset("gpsimd", self.cm128[:, :].rearrange("p (n k) -> p n k", k=128)[:, :, 0:1], 0.0, ["cm"])
            vtok = self.sb("vtok", [128, NCK, 256], BF16, ph)
            vtok_s = self.sb("vtok_s", [NS, 256], F32, ph)
            wvh = self.sb("gwvh", [128, NCH, 256], BF16, ph)
            glb = self.sb("glb", [16, NT], BF16, ph)
            wgl = self.sb("wgl", [128, NCH, 16], BF16, ph)
            negb = self.sb("negb", [128, 4], F32, ph)
            gb = VEC_COLS["gla_b_gk"]
            self.ts("vector", negb[:, :], self.vec[:, gb:gb + 4], -1.0, ALU.mult, ["vec"], ["negb"])
            self.load_w(wgl[:, :, :], w_in[:, 3072:3088], "wgl")
            if True:
                for ti, (t0, n) in enumerate(self.tiles):
                    for c in range(NCH):
                        self.mm(ps[2][:16, :n], wgl[:, c, :], ub[:, c, t0:t0 + n], c == 0, c == NCH - 1, ["wgl", "u%d" % ti], ["ps2"])
                    self.act(glb[:, t0:t0 + n], ps[2][:16, :n], AF.Copy, ["ps2"], ["glb"])
                S.barrier()
            bA = self.sb("gA", [128, NT], F32, ph)
            bB = self.sb("gB", [128, NT], F32, ph)
            qt = self.sb("gqt", [128, NT], BF16, ph)
            kt = self.sb("gkt", [128, NT], BF16, ph)
            kh = self.sb("gkh", [128, L], BF16, ph)
            oh = self.sb("goh", [128, 2, NT], BF16, ph)
            xoh = self.sb("gxo", [128, 2, 512], BF16, ph)
            wq = self.sb("gwq", [128, NCH, 128], BF16, ph)
            wk = self.sb("gwk", [128, NCH, 128], BF16, ph)
            wg = self.sb("gwg", [128, NCH, 256], BF16, ph)
            woh = self.sb("gwo", [128, 2, 1024], BF16, ph)
            wgk2 = self.sb("gwgk2", [16, 128], BF16, ph)
            Sf = self.sb("gSf", [128, 256], F32, ph)
            Sb = self.sb("gSb", [128, 256], BF16, ph)
            attb2 = [self.sb("gatt%d" % b_, [128, 128], BF16, ph) for b_ in range(2)]
            khT2 = [self.sb("gkhT%d" % b_, [128, 128], BF16, ph) for b_ in range(2)]
            ktok_s = self.sb("gktoks", [NS, 128], F32, ph)
            ksel = self.sb("gksel", [NS, 128], F32, ph)
            qs = self.sb("gqs", [128, NS], F32, ph)
            S0 = [self.sb("gS0_%d" % b, [128, 256], F32, ph) for b in range(2)]
            S1 = [self.sb("gS1_%d" % b, [128, 256], F32, ph) for b in range(2)]
            tmp = self.sb("gtmp", [128, 512], F32, ph)
            tmp2 = self.sb("gtmp2", [128, 512], F32, ph)
            rs = self.rstd
            sqh = self.sq
            for hd in range(4):
                self.load_w(wq[:, :, :], w_in[:, hd * 128:(hd + 1) * 128], "gwq")
                self.load_w(wk[:, :, :], w_in[:, 512 + hd * 128:512 + (hd + 1) * 128], "gwk")
                self.load_w(wg[:, :, :], w_in[:, 2048 + hd * 256:2048 + (hd + 1) * 256], "gwg")
                self.load_w(woh[:, :, :], self.gla_wo[hd * 256:(hd + 1) * 256, :], "gwo")
                S.dma("gpsimd", "gwgk2", wgk2[:, :], self.gla_w_gk2[:, hd * 128:(hd + 1) * 128], writes=["gwgk2"])
                self.load_w(wvh[:, :, :], w_in[:, 1024 + hd * 256:1024 + (hd + 1) * 256], "gwvh")
                for n in range(NCK + 1):
                    pb = ps[n % 2]
                    if n < NCK:
                        lo, cnt = n * CG, CG
                    else:
                        lo, cnt = L, NS
                    for c in range(NCH):
                        self.mm(pb[:cnt, :256], ub[:, c, lo:lo + cnt], wvh[:, c, :], c == 0, c == NCH - 1, ["gwvh"] + allu, ["ps%d" % (n % 2)])
                    if n < NCK:
                        self.act(vtok[:, n, :], pb[:, :256], AF.Copy, ["ps%d" % (n % 2)], ["vtok"])
                    else:
                        self.act(vtok_s[:, :], pb[:NS, :256], AF.Copy, ["ps%d" % (n % 2)], ["vtok_s"])
                for ti, (t0, n) in enumerate(self.tiles):
                    self.mm(ps[0][:, :n], wgk2[:, :], glb[:, t0:t0 + n], True, True, ["gwgk2", "glb"], ["ps0"])
                    self.act(bA[:, t0:t0 + n], ps[0][:, :n], AF.Exp, ["ps0", "negb"], ["gA"], scale=-1.0, bias=negb[:, hd:hd + 1])
                self.act(bA[:, :], bA[:, :], AF.Ln, ["gA", "consts"], ["gA"], bias=self.eps_t[:, 1:2])
                S.op("vector", "tensor_tensor_scan", dict(out=bB[:, 0:L], data0=self.cm128[:, 0:L], data1=bA[:, 0:L], initial=0.0,
                                                          op0=ALU.mult, op1=ALU.add), reads=["gA", "cm"], writes=["gB"])
                self.cp("vector", bB[:, L:NT], bA[:, L:NT], ["gA"], ["gB"])
                self.act(bA[:, :], bB[:, :], AF.Exp, ["gB"], ["gA"], scale=-1.0 / 16)
                self.act(bB[:, :], bB[:, :], AF.Exp, ["gB"], ["gB"], scale=1.0 / 16)
                for ti, (t0, n) in enumerate(self.tiles):
                    for c in range(NCH):
                        self.mm(ps[0][:, :n], wq[:, c, :], ub[:, c, t0:t0 + n], c == 0, c == NCH - 1, ["gwq", "u%d" % ti], ["ps0"])
                    for c in range(NCH):
                        self.mm(ps[1][:, :n], wk[:, c, :], ub[:, c, t0:t0 + n], c == 0, c == NCH - 1, ["gwk", "u%d" % ti], ["ps1"])
                    self.stt(qt[:, t0:t0 + n], ps[0][:, :n], 128.0 ** -0.5, bA[:, t0:t0 + n], ALU.mult, ALU.mult, ["ps0", "gA"], ["gqt"])
                    self.tt("vector", kt[:, t0:t0 + n], ps[1][:, :n], bB[:, t0:t0 + n], ALU.mult, ["ps1", "gB"], ["gkt"])
                    if t0 >= L:
                        self.ts("vector", qs[:, :], ps[0][:, :NS], 128.0 ** -0.5, ALU.mult, ["ps0"], ["gqs"])
                        for c in range(NCH):
                            self.mm(ps[2][:NS, :128], ub[:, c, L:NT], wk[:, c, :], c == 0, c == NCH - 1, ["gwk", "u%d" % ti], ["ps2"])
                        self.act(ktok_s[:, :], ps[2][:NS, :128], AF.Copy, ["ps2"], ["gktoks"])
                for n in range(NCK):
                    last = n * CG + CG - 1
                    self.ts("vector", kh[:, n * CG:(n + 1) * CG], kt[:, n * CG:(n + 1) * CG], bA[:, last:last + 1], ALU.mult, ["gkt", "gA"], ["gkh"])
                self.memset("vector", Sf[:, :], 0.0, ["gSf"])
                self.memset("vector", Sb[:, :], 0.0, ["gSb"])
                for n in range(NCK):
                    sl = slice(n * CG, (n + 1) * CG)
                    pp = n % 2
                    pa, pra = ps[pp], "ps%d" % pp
                    tbo = pp * 128
                    self.mm(pa[:, :128], kt[:, sl], qt[:, sl], True, True, ["gkt", "gqt"], [pra])
                    self.tt("vector", attb2[pp][:, :], pa[:, :128], self.cst[:, self.C_TRIU:self.C_TRIU + 128], ALU.mult, [pra, "cst"], ["gatt%d" % pp])
                    S.op("tensor", "transpose", dict(out=self.psT[:, tbo:tbo + 128], in_=kh[:, sl], identity=self.identb[:, :]),
                         reads=["gkh", "identb"], writes=["psT"])
                    self.act(khT2[pp][:, :], self.psT[:, tbo:tbo + 128], AF.Copy, ["psT"], ["gkhT%d" % pp])
                    for m in range(2):
                        pb = ps[2 + 2 * pp + m]
                        pr = "ps%d" % (2 + 2 * pp + m)
                        self.mm(pb[:, :128], Sb[:, m * 128:(m + 1) * 128], qt[:, sl], True, False, ["gSb", "gqt"], [pr])
                        self.mm(pb[:, :128], vtok[:, n, m * 128:(m + 1) * 128], attb2[pp][:, :], False, True,
                                ["vtok", "gatt%d" % pp], [pr])
                        self.act(oh[:, m, sl], pb[:, :128], AF.Copy, [pr], ["goh"])
                    self.mm(ps[6][:, :256], khT2[pp][:, :], vtok[:, n, :], True, True, ["gkhT%d" % pp, "vtok"], ["ps6"])
                    last = n * CG + CG - 1
                    self.stt(Sf[:, :], Sf[:, :], bA[:, last:last + 1], ps[6][:, :256], ALU.mult, ALU.add, ["gSf", "gA", "ps6"], ["gSf"])
                    if n < NCK - 1:
                        self.cp("vector", Sb[:, :], Sf[:, :], ["gSf"], ["gSb"])
                S.dma("sync", "gSf", self.gla_po[hd], Sf[:, :], reads=["gSf"])
                for r in range(NS):
                    b = r % 2
                    S.dma("sync", "gS0_%d" % b, S0[b][:, :], self.gla_st[r, hd], writes=["gS0_%d" % b])
                    self.ts("vector", ksel[:, :], ktok_s[:, :], self.cst[:NS, self.C_ID + r:self.C_ID + r + 1], ALU.mult, ["gktoks", "cst"], ["gksel"])
                    self.mm(ps[5][:, :256], ksel[:, :], vtok_s[:, :], True, True, ["gksel", "vtok_s"], ["ps5"])
                    self.stt(S1[b][:, :], S0[b][:, :], bA[:, L + r:L + r + 1], ps[5][:, :256], ALU.mult, ALU.add,
                             ["gS0_%d" % b, "gA", "ps5"], ["gS1_%d" % b])
                    S.dma("sync", "gS1_%d" % b, self.gla_so[r, hd], S1[b][:, :], reads=["gS1_%d" % b])
                    for m in range(2):
                        self.mm(ps[3][:, m:m + 1], S1[b][:, m * 128:(m + 1) * 128], qs[:, r:r + 1], True, True, ["gS1_%d" % b, "gqs"], ["ps3"])
                    self.cp("vector", oh[:, :, L + r], ps[3][:, 0:2], ["ps3"], ["goh"])
                for ti, (t0, n) in enumerate(self.tiles):
                    self.act(sqh[:, 0:2, :n], oh[:, :, t0:t0 + n], AF.Square, ["goh"], ["sq"])
                    for m in range(2):
                        self.mm(ps[6][:, :n], self.ones_bf[:, :], sqh[:, m, :n], m == 0, m == 1, ["sq", "ones"], ["ps6"])
                    self.act(rs[:, :n], ps[6][:, :n], AF.Ln, ["ps6", "consts"], ["rstd"], scale=1.0 / 256, bias=self.eps_t[:, 2:3])
                    self.act(rs[:, :n], rs[:, :n], AF.Exp, ["rstd"], ["rstd"], scale=-0.5)
                    for m in range(2):
                        pb = ps[m]
                        pr = "ps%d" % m
                        for c in range(NCH):
                            self.mm(pb[:, :n], wg[:, c, m * 128:(m + 1) * 128], ub[:, c, t0:t0 + n], c == 0, c == NCH - 1, ["gwg", "u%d" % ti], [pr])
                        self.act(tmp[:, :n], pb[:, :n], AF.Exp, [pr], ["gtmp"], scale=-1.0)
                        self.ts("vector", tmp[:, :n], tmp[:, :n], 1.0, ALU.add, ["gtmp"], ["gtmp"])
                        S.op("vector", "reciprocal", dict(out=tmp[:, :n], in_=tmp[:, :n]), reads=["gtmp"], writes=["gtmp"])
                        self.tt("vector", tmp[:, :n], tmp[:, :n], pb[:, :n], ALU.mult, ["gtmp", pr], ["gtmp"])
                        self.stt(tmp2[:, :n], oh[:, m, t0:t0 + n], self.vec[:, VEC_COLS["gla_norm"] + m:VEC_COLS["gla_norm"] + m + 1], rs[:, :n],
                                 ALU.mult, ALU.mult, ["goh", "vec", "rstd"], ["gtmp2"])
                        self.tt("vector", xoh[:, m, :n], tmp2[:, :n], tmp[:, :n], ALU.mult, ["gtmp", "gtmp2"], ["gxo"])
                    for d in range(NCH):
                        pb = ps[4 + d % 2]
                        pr = "ps%d" % (4 + d % 2)
                        for m in range(2):
                            self.mm(pb[:, :n], woh[:, m, d * 128:(d + 1) * 128], xoh[:, m, :n], m == 0, m == 1, ["gwo", "gxo"], [pr])
                        self.tt("vector", h[:, d, t0:t0 + n], h[:, d, t0:t0 + n], pb[:, :n], ALU.add, [pr, "h%d" % ti], ["h%d" % ti])
            S.barrier()

    def conv_layer(self, i):
        S = self.S
        L, NT = self.L, self.NT
        self.mix_norm_all(i)
        ub = self.ubuf
        with contextlib.ExitStack() as ph:
            xo = self.sb("cxo", [128, NCH, NT], BF16, ph)
            wo = self.sb("cwo", [128, NCH, 1024], BF16, ph)
            ws = [[self.sb("cw%d_%d" % (k, b), [128, NCH, 128], BF16, ph) for k in range(3)] for b in range(2)]
            zcb = self.sb("zcb", [128, 2 + L + 3 * NS], F32, ph)
            gbt = self.sb("gbt", [128, NT], F32, ph)
            cv = self.sb("cv", [128, NT], F32, ph)
            tmp = self.sb("ctmp", [128, 512], F32, ph)
            cst = self.sb("cst", [128, NCH, 2 + 2 * NS], F32, ph)
            self.load_w(wo[:, :, :], self.conv_wo[:, :], "cwo", nsplit=2)
            SB = 2 + L
            zs = zcb[:, SB:SB + 3 * NS].rearrange("p (r k) -> p r k", k=3)
            stv = self.conv_st.rearrange("(c p) r k -> p c r k", p=128)
            self.memset("gpsimd", zcb[:, 0:2], 0.0, ["zcb_halo"])
            def conv_load(cc):
                b = cc % 2
                for k in range(3):
                    self.load_w(ws[b][k][:, :, :], self.conv_w_in[:, k * 1024 + cc * 128:k * 1024 + (cc + 1) * 128], "cw%d_%d" % (k, b))
            conv_load(0)
            for cc in range(NCH):
                b = cc % 2
                if cc + 1 < NCH:
                    conv_load(cc + 1)
                S.dma("sync", "zcb_st", zs[:, :, 0:2], stv[:, cc, :, :], writes=["zcb_st"])
                for ti, (t0, n) in enumerate(self.tiles):
                    pbs = []
                    for k in range(3):
                        pb = self.ps[k]
                        for c in range(NCH):
                            self.mm(pb[:, :n], ws[b][k][:, c, :], ub[:, c, t0:t0 + n], c == 0, c == NCH - 1,
                                    ["cw%d_%d" % (k, b), "u%d" % ti], ["ps%d" % k])
                    self.act(gbt[:, t0:t0 + n], self.ps[0][:, :n], AF.Copy, ["ps0"], ["gbt%d" % ti])
                    self.act(tmp[:, :n], self.ps[1][:, :n], AF.Copy, ["ps1"], ["ctmp"])
                    if t0 < L:
                        dst = zcb[:, 2 + t0:2 + t0 + n]
                    else:
                        dst = zs[:, :, 2]
                    self.tt("vector", dst, tmp[:, :n], self.ps[2][:, :n], ALU.mult, ["ctmp", "ps2"], ["zcb%d" % ti])
                allz = ["zcb%d" % ti for ti in range(len(self.tiles))] + ["zcb_halo", "zcb_st"]
                self.ts("vector", cv[:, 0:L], zcb[:, 2:2 + L], self.vcol("conv_w2", cc), ALU.mult, allz + ["vec"], ["cv"])
                self.stt(cv[:, 0:L], zcb[:, 1:1 + L], self.vcol("conv_w1", cc), cv[:, 0:L], ALU.mult, ALU.add, allz + ["cv", "vec"], ["cv"])
                self.stt(cv[:, 0:L], zcb[:, 0:L], self.vcol("conv_w0", cc), cv[:, 0:L], ALU.mult, ALU.add, allz + ["cv", "vec"], ["cv"])
                self.ts("vector", cv[:, L:NT], zs[:, :, 2], self.vcol("conv_w2", cc), ALU.mult, allz + ["vec", "cv"], ["cv"])
                self.stt(cv[:, L:NT], zs[:, :, 1], self.vcol("conv_w1", cc), cv[:, L:NT], ALU.mult, ALU.add, allz + ["cv", "vec"], ["cv"])
                self.stt(cv[:, L:NT], zs[:, :, 0], self.vcol("conv_w0", cc), cv[:, L:NT], ALU.mult, ALU.add, allz + ["cv", "vec"], ["cv"])
                gres = ["gbt%d" % ti for ti in range(len(self.tiles))]
                self.tt("vector", xo[:, cc, :], gbt[:, :], cv[:, :], ALU.mult, gres + ["cv"], ["cxo%d" % cc])
                self.cp("vector", cst[:, cc, 0:2], zcb[:, 2 + L - 2:2 + L], allz, ["cst"])
                self.cp("vector", cst[:, cc, 2:2 + 2 * NS].rearrange("p (r k) -> p r k", k=2), zs[:, :, 1:3], allz, ["cst"])
            S.dma("sync", "cst", self.conv_o.rearrange("(c p) n -> p c n", p=128), cst[:, :, :], reads=["cst"])
            self.out_proj(xo, wo, "cwo", lambda j, ti: ["cxo%d" % j])
            S.barrier()

    def pool_layer(self, i):
        S = self.S
        L, NT = self.L, self.NT
        h = self.h
        ub = self.ubuf
        NB = 15 + L + 16 * NS
        with contextlib.ExitStack() as ph:
            rall = self.sb("rall", [128, NT], F32, ph)
            pw = self.sb("pw", [128, 4, 2, 256], BF16, ph)
            bufs = [self.sb("pbuf%d" % b, [128, NB], F32, ph) for b in range(3)]
            pst = self.sb("pst", [128, NCH, 15 + 15 * NS], F32, ph)
            S.dma("gpsimd", "pw", pw[:, :, :, :], self.pool_w.rearrange("g (k p) n -> p g k n", p=128), writes=["pw"])
            for ti, (t0, n) in enumerate(self.tiles):
                self.rms_rstd(ti, rall[:, t0:t0 + n], "rall")
            stv = self.pool_st.rearrange("(c p) r k -> p c r k", p=128)
            SB = 15 + L
            for b in range(3):
                self.memset("gpsimd", bufs[b][:, 0:15], 0.0, ["pbuf%d" % b])
            for c in range(NCH):
                gi = c // 2
                w = 2 << gi
                A = bufs[0]
                As = A[:, SB:SB + 16 * NS].rearrange("p (r k) -> p r k", k=16)
                S.dma("sync", "pbuf0", As[:, :, 0:15], stv[:, c, :, :], writes=["pbuf0"])
                allh = ["h%d" % ti for ti in range(len(self.tiles))]
                self.stt(A[:, 15:15 + L], h[:, c, 0:L], self.vcol("norm_mix%d" % i, c), rall[:, 0:L], ALU.mult, ALU.mult,
                         allh + ["rall", "vec"], ["pbuf0"])
                self.stt(As[:, :, 15], h[:, c, L:NT], self.vcol("norm_mix%d" % i, c), rall[:, L:NT], ALU.mult, ALU.mult,
                         allh + ["rall", "vec"], ["pbuf0"])
                self.cp("gpsimd", pst[:, c, 0:15], A[:, L:L + 15], ["pbuf0"], ["pst"])
                self.cp("gpsimd", pst[:, c, 15:15 + 15 * NS].rearrange("p (r k) -> p r k", k=15), As[:, :, 1:16], ["pbuf0"], ["pst"])
                src, si = A, 0
                lo = 0
                for k in range(gi + 1):
                    sh = 1 << k
                    lo += sh
                    di = 1 if si != 1 else 2
                    dst = bufs[di]
                    ss = src[:, SB:SB + 16 * NS].rearrange("p (r k) -> p r k", k=16)
                    ds = dst[:, SB:SB + 16 * NS].rearrange("p (r k) -> p r k", k=16)
                    self.tt("vector", dst[:, lo:15 + L], src[:, lo:15 + L], src[:, lo - sh:15 + L - sh], ALU.add,
                            ["pbuf%d" % si], ["pbuf%d" % di])
                    self.tt("gpsimd", ds[:, :, lo:16], ss[:, :, lo:16], ss[:, :, lo - sh:16 - sh], ALU.add,
                            ["pbuf%d" % si], ["pbuf%d" % di])
                    src, si = dst, di
                ss = src[:, SB:SB + 16 * NS].rearrange("p (r k) -> p r k", k=16)
                self.stt(ub[:, c, 0:L], src[:, 15:15 + L], 1.0 / w, A[:, 15:15 + L], ALU.mult, ALU.subtract,
                         ["pbuf%d" % si, "pbuf0"], ["pd%d" % c])
                self.stt(ub[:, c, L:NT], ss[:, :, 15], 1.0 / w, As[:, :, 15], ALU.mult, ALU.subtract,
                         ["pbuf%d" % si, "pbuf0"], ["pd%d" % c])
                nf = w - 1
                ic = VEC_COLS["invc"]
                ftmp = self.rstd
                self.tt("vector", ftmp[:, 0:nf], src[:, 15:15 + nf], self.vec[:, ic:ic + nf], ALU.mult, ["pbuf%d" % si, "vec"], ["rstd"])
                self.tt("vector", ub[:, c, 0:nf], ftmp[:, 0:nf], A[:, 15:15 + nf], ALU.subtract, ["rstd", "pbuf0"], ["pd%d" % c])
            S.dma("sync", "pst", self.pool_o.rearrange("(c p) n -> p c n", p=128), pst[:, :, :], reads=["pst"])
            for ti, (t0, n) in enumerate(self.tiles):
                for gi in range(4):
                    for m in range(2):
                        d = 2 * gi + m
                        pb = self.ps[4 + d % 2]
                        pr = "ps%d" % (4 + d % 2)
                        for k in range(2):
                            self.mm(pb[:, :n], pw[:, gi, k, m * 128:(m + 1) * 128], ub[:, 2 * gi + k, t0:t0 + n], k == 0, k == 1,
                                    ["pw", "pd%d" % (2 * gi + k)], [pr])
                        self.stt(h[:, d, t0:t0 + n], pb[:, :n], self.vcol("pool_scale", d), h[:, d, t0:t0 + n], ALU.mult, ALU.add,
                                 [pr, "h%d" % ti, "vec"], ["h%d" % ti])
            S.barrier()

    def build(self):
        L, NT = self.L, self.NT
        nc = bass.Bass("TRN2", target_bir_lowering=False)
        self.nc = nc
        self.xT = self.dram_in("xT", [D, NT])
        self.vec_d = self.dram_in("vec", [128, NVEC])
        self.ffn_up = self.dram_in("ffn_up", [4, D, DFF])
        self.ffn_down = self.dram_in("ffn_down", [4, DFF, D])
        self.yT = self.dram_out("yT", [D, NT])
        self.conv_w_in = self.dram_in("conv_w_in", [D, 3 * D])
        self.conv_wo = self.dram_in("conv_wo", [D, D])
        self.conv_st = self.dram_in("conv_st", [D, NS, 2])
        self.conv_o = self.dram_out("conv_o", [D, 2 + 2 * NS])
        self.pool_w = self.dram_in("pool_w", [4, 256, 256])
        self.pool_st = self.dram_in("pool_st", [D, NS, 15])
        self.pool_o = self.dram_out("pool_o", [D, 15 + 15 * NS])
        self.cst_d = self.dram_in("cst", [128, self.NCST])
        self.gla_w_in = self.dram_in("gla_w_in", [D, 3088])
        self.gla_w_gk2 = self.dram_in("gla_w_gk2", [16, 512])
        self.gla_wo = self.dram_in("gla_wo", [D, D])
        self.gla_st = self.dram_in("gla_st", [NS, 4, 128, 256])
        self.gla_po = self.dram_out("gla_po", [4, 128, 256])
        self.gla_so = self.dram_out("gla_so", [NS, 4, 128, 256])
        self.rwkv_w_rkv = [self.dram_in("rwkv_w_%s" % k, [D, D]) for k in "rkv"]
        self.rwkv_w1 = self.dram_in("rwkv_w1", [D, 64])
        self.rwkv_w2 = self.dram_in("rwkv_w2", [64, D])
        self.rwkv_a1 = self.dram_in("rwkv_a1", [D, 64])
        self.rwkv_a2 = self.dram_in("rwkv_a2", [64, D])
        self.rwkv_g1 = self.dram_in("rwkv_g1", [D, 160])
        self.rwkv_g2 = self.dram_in("rwkv_g2", [160, D])
        self.rwkv_wo = self.dram_in("rwkv_wo", [D, D])
        self.shift_st = self.dram_in("shift_st", [D, NS])
        self.wkv_st = self.dram_in("wkv_st", [NS, 8, 128, 64])
        self.shift_o = self.dram_out("shift_o", [D, 1 + NS])
        self.wkv_po = self.dram_out("wkv_po", [8, 128, 64])
        self.wkv_so = self.dram_out("wkv_so", [NS, 8, 128, 64])
        with contextlib.ExitStack() as st:
            self.st = st
            S = Sched(nc, st)
            self.S = S
            self.h = self.sb("h", [128, NCH, NT], F32)
            self.ubuf = self.sb("ubuf", [128, NCH, NT + 1 + NS], BF16)
            self.vec = self.sb("vecs", [128, NVEC], F32)
            self.ones_bf = self.sb("ones_bf", [128, 128], BF16)
            self.eps_t = self.sb("eps_t", [128, 4], F32)
            self.sq = self.sb("sq", [128, NCH, 512], BF16)
            self.rstd = self.sb("rstd", [128, 512], F32)
            self.ps = [st.enter_context(nc.psum_tensor("psb%d" % i, [128, 512], F32)) for i in range(7)]
            self.psT = st.enter_context(nc.psum_tensor("psT", [128, 1024], BF16))
            self.cst = self.sb("cst", [128, self.NCST], F32)
            self.identb = self.sb("identb", [128, 128], BF16)

            self.memset("vector", self.ones_bf[:], 1.0, ["ones"])
            self.memset("vector", self.eps_t[:, 0:1], NORM_EPS, ["consts"])
            self.memset("vector", self.eps_t[:, 1:2], 1.0, ["consts"])
            self.memset("vector", self.eps_t[:, 2:3], 1e-5, ["consts"])
            self.memset("vector", self.eps_t[:, 3:4], 64e-5, ["consts"])
            S.dma("sync", "cst", self.cst[:], self.cst_d, writes=["cst"])
            self.cp("vector", self.identb[:, :], self.cst[:, self.C_ID:self.C_ID + 128], ["cst"], ["identb"])

            S.dma("sync", "vec", self.vec[:], self.vec_d, writes=["vec"])
            xv = self.xT.rearrange("(c p) t -> p c t", p=128)
            for ti, (t0, n) in enumerate(self.tiles):
                S.dma("sync", "h%d" % ti, self.h[:, :, t0:t0 + n], xv[:, :, t0:t0 + n], writes=["h%d" % ti])
            for i in range(4):
                if self.layers[i]:
                    getattr(self, ["rwkv_layer", "gla_layer", "conv_layer", "pool_layer"][i])(i)
                if self.ffn:
                    self.ffn_layer(i)
            yv = self.yT.rearrange("(c p) t -> p c t", p=128)
            with contextlib.ExitStack() as ph:
                yo = [self.sb("yo%d" % b, [128, NCH, 512], F32, ph) for b in range(2)]
                for ti, (t0, n) in enumerate(self.tiles):
                    b = ti % 2
                    self.norm_tile(ti, "norm_final", lambda c: yo[b][:, c, :n], "yo%d" % b)
                    S.dma("sync", "yo%d" % b, yv[:, :, t0:t0 + n], yo[b][:, :, :n], reads=["yo%d" % b])
                S.barrier()
            with nc.Block() as block:
                S.replay(block)
        return nc


def pack_vec(inp):
    v = np.zeros((128, NVEC), np.float32)

    def put(name, arr):
        a = np.asarray(arr, np.float32).reshape(-1)
        k = a.size // 128
        c0 = VEC_COLS[name]
        v[:, c0:c0 + k] = a.reshape(k, 128).T
    for i in range(4):
        put("norm_mix%d" % i, inp["norm_mix"][i])
        put("norm_ffn%d" % i, inp["norm_ffn"][i])
    put("norm_final", inp["norm_final"])
    for j in range(6):
        put("mu%d" % j, inp["rwkv_mu"][0, j])
    put("w0", inp["rwkv_w0"][0]); put("a0", inp["rwkv_a0"][0]); put("k_k", inp["rwkv_k_k"][0]); put("k_a", inp["rwkv_k_a"][0])
    put("r_k", inp["rwkv_r_k"][0]); put("ln_w", inp["rwkv_ln_w"][0]); put("ln_b", inp["rwkv_ln_b"][0])
    for j in range(3):
        put("conv_w%d" % j, inp["conv_w"][0, j])
    put("pool_scale", inp["pool_scale"][0])
    put("gla_b_gk", inp["gla_b_gk"][0]); put("gla_norm", inp["gla_norm"][0])
    v[:, VEC_COLS["invc"]:VEC_COLS["invc"] + 16] = (1.0 / np.arange(1, 17, dtype=np.float64)).astype(np.float32)[None, :]
    return v


def make_in_maps(inp, L, ncores=8):
    vec = pack_vec(inp)
    maps = []
    for core in range(ncores):
        xp = np.asarray(inp["x_prompt"][core, :L], np.float32)
        xs = np.asarray(inp["x_sample"][core * NS:(core + 1) * NS, 0], np.float32)
        xT = np.ascontiguousarray(np.concatenate([xp, xs], axis=0).T)
        rs = slice(core * NS, (core + 1) * NS)
        m = {"xT": xT, "vec": vec,
             "ffn_up": np.asarray(inp["ffn_up"], np.float32), "ffn_down": np.asarray(inp["ffn_down"], np.float32),
             "conv_w_in": np.asarray(inp["conv_w_in"][0], np.float32), "conv_wo": np.asarray(inp["conv_wo"][0], np.float32),
             "conv_st": np.ascontiguousarray(np.transpose(inp["state_conv"][0, rs], (2, 0, 1))),
             "pool_w": np.asarray(inp["pool_w"][0], np.float32), "cst": make_cst(),
             "gla_w_in": np.asarray(inp["gla_w_in"][0], np.float32), "gla_w_gk2": np.asarray(inp["gla_w_gk2"][0], np.float32),
             "gla_wo": np.asarray(inp["gla_wo"][0], np.float32), "gla_st": np.ascontiguousarray(inp["state_gla"][0, rs]),
             "rwkv_w_r": np.asarray(inp["rwkv_w_rkv"][0, 0], np.float32), "rwkv_w_k": np.asarray(inp["rwkv_w_rkv"][0, 1], np.float32),
             "rwkv_w_v": np.asarray(inp["rwkv_w_rkv"][0, 2], np.float32),
             "rwkv_w1": np.asarray(inp["rwkv_w1"][0], np.float32), "rwkv_w2": np.asarray(inp["rwkv_w2"][0], np.float32),
             "rwkv_a1": np.asarray(inp["rwkv_a1"][0], np.float32), "rwkv_a2": np.asarray(inp["rwkv_a2"][0], np.float32),
             "rwkv_g1": np.asarray(inp["rwkv_g1"][0], np.float32), "rwkv_g2": np.asarray(inp["rwkv_g2"][0], np.float32),
             "rwkv_wo": np.asarray(inp["rwkv_wo"][0], np.float32),
             "shift_st": np.ascontiguousarray(np.asarray(inp["state_rwkv_shift"][0, rs], np.float32).T),
             "wkv_st": np.ascontiguousarray(np.transpose(inp["state_rwkv_wkv"][0, rs], (0, 1, 3, 2)).reshape(NS, 8, 128, 64)),
             "pool_st": np.ascontiguousarray(np.transpose(inp["state_pool"][0, rs], (2, 0, 1)))}
        maps.append(m)
    return maps


_CACHE = {}


def run(inp, L, ncores=8, **kw):
    key = (L, tuple(sorted(kw.items())))
    if key not in _CACHE:
        _CACHE[key] = Builder(L, **kw).build()
    nc = _CACHE[key]
    maps = make_in_maps(inp, L, ncores)
    res = run_bass_kernel_spmd(nc, maps, core_ids=list(range(ncores)))
    return res.results


def gather(res, L):
    outs = {}
    outs["y"] = (np.stack([r["yT"][:, :L].T for r in res], 0),
                 np.concatenate([r["yT"][:, L:].T for r in res], 0)[:, None, :])
    outs["conv"] = (np.stack([r["conv_o"][:, 0:2].T for r in res], 0)[None],
                    np.concatenate([np.transpose(r["conv_o"][:, 2:].reshape(D, NS, 2), (1, 2, 0)) for r in res], 0)[None])
    outs["wkv"] = (np.stack([np.transpose(r["wkv_po"].reshape(16, 64, 64), (0, 2, 1)) for r in res], 0)[None],
                   np.concatenate([np.transpose(r["wkv_so"].reshape(NS, 16, 64, 64), (0, 1, 3, 2)) for r in res], 0)[None])
    outs["shift"] = (np.stack([r["shift_o"][:, 0] for r in res], 0)[None],
                     np.concatenate([r["shift_o"][:, 1:].T for r in res], 0)[None])
    outs["gla"] = (np.stack([r["gla_po"] for r in res], 0)[None], np.concatenate([r["gla_so"] for r in res], 0)[None])
    outs["pool"] = (np.stack([r["pool_o"][:, 0:15].T for r in res], 0)[None],
                    np.concatenate([np.transpose(r["pool_o"][:, 15:].reshape(D, NS, 15), (1, 2, 0)) for r in res], 0)[None])
    return outs


def kernel(**inputs):
    L = 2048
    res = run(inputs, L)
    o = gather(res, L)
    f = lambda a: np.ascontiguousarray(a, dtype=np.float32)
    return (f(o["y"][0]), f(o["y"][1]), f(o["wkv"][0]), f(o["wkv"][1]), f(o["shift"][0]), f(o["shift"][1]),
            f(o["gla"][0]), f(o["gla"][1]), f(o["conv"][0]), f(o["conv"][1]), f(o["pool"][0]), f(o["pool"][1]))
```

```python
import numpy as np
import contextlib
import concourse.bass as bass
import concourse.mybir as mybir
from concourse.bass_utils import run_bass_kernel_spmd

F32 = mybir.dt.float32
BF16 = mybir.dt.bfloat16
AF = mybir.ActivationFunctionType
ALU = mybir.AluOpType
AX = mybir.AxisListType

D = 1024
NCH = 8
DFF = 4096
NS = 16
NORM_EPS = 1e-6


import os as _os_mod
INLINE_WAIT = _os_mod.environ.get("SCHED_INLINE", "1") == "1"


class Sched:
    ENGS = ["tensor", "vector", "scalar", "gpsimd", "sync"]

    def __init__(self, nc, stack):
        self.nc = nc
        self.stack = stack
        self.ops = {e: [] for e in self.ENGS}
        self.semobj = {e: stack.enter_context(nc.semaphore("s_" + e)) for e in self.ENGS}
        self.cnt = {e: 0 for e in self.ENGS}
        self.waited = {e: {} for e in self.ENGS}
        self.W = {}
        self.R = {}
        self.dcnt = {}
        self.dkeys = {}

    def dma_sem(self, name):
        if name in self.dkeys:
            return self.dkeys[name]
        key = "d%d" % len(self.dkeys)
        self.semobj[key] = self.stack.enter_context(self.nc.semaphore(key))
        self.dcnt[key] = 0
        self.dkeys[name] = key
        return key

    def _deps(self, eng, reads, writes, acc=False):
        need = {}

        def add(k, v):
            if need.get(k, 0) < v:
                need[k] = v
        for r in reads:
            if r in self.W:
                add(*self.W[r])
        for w in writes:
            if w in self.W and not (acc and self.W[w][0] == eng):
                add(*self.W[w])
            for k, v in self.R.get(w, {}).items():
                add(k, v)
        out = []
        for k, v in need.items():
            if self.waited[eng].get(k, 0) >= v:
                continue
            self.waited[eng][k] = v
            out.append((k, v))
        return out

    def op(self, eng, meth, kw, reads=(), writes=(), acc=False):
        waits = self._deps(eng, reads, writes, acc)
        self.cnt[eng] += 1
        v = self.cnt[eng]
        self.ops[eng].append((waits, (meth, kw), (eng, 1)))
        for r in reads:
            self.R.setdefault(r, {})[eng] = v
        for w in writes:
            self.W[w] = (eng, v)
            self.R[w] = {}

    def dma(self, eng, semname, out, in_, reads=(), writes=()):
        semkey = self.dma_sem(semname)
        waits = self._deps(eng, reads, writes)
        self.dcnt[semkey] += 16
        v = self.dcnt[semkey]
        self.ops[eng].append((waits, ("dma_start", dict(out=out, in_=in_)), (semkey, 16)))
        for r in reads:
            self.R.setdefault(r, {})[semkey] = v
        for w in writes:
            self.W[w] = (semkey, v)
            self.R[w] = {}

    def barrier(self):
        tot = {}
        for e in self.ENGS:
            if self.cnt[e]:
                tot[e] = self.cnt[e]
        for k, v in self.dcnt.items():
            if v:
                tot[k] = v
        for e in self.ENGS:
            waits = []
            for k, v in tot.items():
                if self.waited[e].get(k, 0) >= v:
                    continue
                self.waited[e][k] = v
                waits.append((k, v))
            if waits:
                self.ops[e].append((waits, None, None))

    def replay(self, block):
        for e in self.ENGS:
            ops = self.ops[e]
            if not ops:
                continue

            def body(engine, ops=ops):
                for waits, fn, inc in ops:
                    inl = None
                    NIN = int(_os_mod.environ.get("SCHED_NINLINE", "1"))
                    if INLINE_WAIT and fn is not None and waits and fn[0] != "dma_start":
                        inl = waits[-NIN:]
                        waits = waits[:-NIN]
                    for k, v in waits:
                        engine.wait_ge(self.semobj[k], v)
                    if fn is not None:
                        try:
                            ins = getattr(engine, fn[0])(**fn[1])
                        except Exception:
                            print("FAILED OP:", fn[0], {k: (getattr(v, "tensor", None) and v.tensor.name, getattr(v, "shape", v)) for k, v in fn[1].items()})
                            raise
                        if inl is not None:
                            for k_, v_ in inl:
                                ins._wait_ge(self.semobj[k_], v_)
                        ins.then_inc(self.semobj[inc[0]], inc[1])
            getattr(block, e)(body)


VEC_COLS = {}


def _vec_layout():
    names = []
    for i in range(4):
        names.append("norm_mix%d" % i)
    for i in range(4):
        names.append("norm_ffn%d" % i)
    names.append("norm_final")
    for j in range(6):
        names.append("mu%d" % j)
    for j in range(6):
        names.append("omu%d" % j)
    names += ["w0", "a0", "k_k", "k_a", "omk_a", "r_k", "ln_w", "ln_b",
              "conv_w0", "conv_w1", "conv_w2", "pool_scale"]
    col = 0
    for n in names:
        VEC_COLS[n] = col
        col += 8
    VEC_COLS["gla_b_gk"] = col
    col += 4
    VEC_COLS["gla_norm"] = col
    col += 2
    VEC_COLS["invc"] = col
    col += 16
    return col


NVEC = _vec_layout()


def make_cst():
    c = np.zeros((128, Builder.NCST), np.float32)
    c[:, Builder.C_ID:Builder.C_ID + 128] = np.eye(128, dtype=np.float32)
    c[:, Builder.C_TRIU:Builder.C_TRIU + 128] = np.triu(np.ones((128, 128), np.float32))
    su = np.triu(np.ones((64, 64), np.float32), 1)
    iu = np.triu(np.ones((64, 64), np.float32), 0)
    z = np.zeros((64, 64), np.float32)
    c[:, Builder.C_SUIU:Builder.C_SUIU + 128] = np.block([[su, z], [z, su]])
    c[:, Builder.C_SUIU + 128:Builder.C_SUIU + 256] = np.block([[iu, z], [z, iu]])
    c[:, Builder.C_SL:Builder.C_SL + 128] = np.block([[su.T, z], [z, su.T]])
    c[:, Builder.C_I2:Builder.C_I2 + 64] = np.concatenate([np.eye(64, dtype=np.float32)] * 2, 0)
    return c


class Builder:
    C_ID = 0
    C_TRIU = 128
    C_SUIU = 256
    C_SL = 512
    C_I2 = 640
    NCST = 704

    def __init__(self, L, layers=(1, 1, 1, 1), ffn=True, dbg=None):
        self.L = L
        self.NT = L + NS
        self.layers = layers
        self.ffn = ffn
        self.dbg = dbg
        tiles = []
        t = 0
        while t < L:
            n = min(512, L - t)
            tiles.append((t, n))
            t += n
        tiles.append((L, NS))
        self.tiles = tiles

    def sb(self, name, shape, dt, stack=None):
        self._uid = getattr(self, "_uid", 0) + 1
        return (stack or self.st).enter_context(self.nc.sbuf_tensor("%s_%d" % (name, self._uid), list(shape), dt))

    def dram_in(self, name, shape):
        return self.nc.dram_tensor(name, list(shape), F32, kind="ExternalInput").ap()

    def dram_out(self, name, shape):
        return self.nc.dram_tensor(name, list(shape), F32, kind="ExternalOutput").ap()

    def vcol(self, name, c=0):
        k = VEC_COLS[name] + c
        return self.vec[:, k:k + 1]

    def mm(self, out, lhsT, rhs, start, stop, reads, writes, nosw=False):
        self.S.op("tensor", "matmul", dict(out=out, lhsT=lhsT, rhs=rhs, start=start, stop=stop),
                  reads=reads, writes=writes, acc=(not start) or nosw)

    def act(self, out, in_, func, reads, writes, scale=1.0, bias=None, eng="scalar"):
        kw = dict(out=out, in_=in_, func=func, scale=scale)
        if bias is not None:
            kw["bias"] = bias
        self.S.op("scalar", "activation", kw, reads=reads, writes=writes)

    def tt(self, eng, out, in0, in1, op, reads, writes):
        self.S.op(eng, "tensor_tensor", dict(out=out, in0=in0, in1=in1, op=op), reads=reads, writes=writes)

    def ts(self, eng, out, in0, s1, op0, reads, writes, s2=None, op1=None):
        kw = dict(out=out, in0=in0, scalar1=s1, scalar2=s2, op0=op0)
        if op1 is not None:
            kw["op1"] = op1
        self.S.op(eng, "tensor_scalar", kw, reads=reads, writes=writes)

    def stt(self, out, in0, scalar, in1, op0, op1, reads, writes):
        self.S.op("vector", "scalar_tensor_tensor", dict(out=out, in0=in0, scalar=scalar, in1=in1, op0=op0, op1=op1),
                  reads=reads, writes=writes)

    def cp(self, eng, out, in_, reads, writes):
        self.S.op(eng, "tensor_copy", dict(out=out, in_=in_), reads=reads, writes=writes)

    def memset(self, eng, ap, val, writes):
        self.S.op(eng, "memset", dict(ap=ap, constant=val), writes=writes)

    def rms_rstd(self, ti, rstd, rstd_res):
        t0, n = self.tiles[ti]
        sq, pb = self.sq, self.ps[6]
        h = self.h
        self.act(sq[:, :, :n], h[:, :, t0:t0 + n], AF.Square, ["h%d" % ti], ["sq"])
        for c in range(NCH):
            self.mm(pb[:, :n], self.ones_bf[:, :], sq[:, c, :n], c == 0, c == NCH - 1, ["sq", "ones"], ["ps6"])
        self.act(rstd[:, :n], pb[:, :n], AF.Ln, ["ps6", "consts"], [rstd_res], scale=1.0 / D, bias=self.eps_t[:, 0:1])
        self.act(rstd[:, :n], rstd[:, :n], AF.Exp, [rstd_res], [rstd_res], scale=-0.5)

    def norm_tile(self, ti, gname, dst_fn, dst_res):
        t0, n = self.tiles[ti]
        rstd = self.rstd
        self.rms_rstd(ti, rstd, "rstd")
        h = self.h
        for c in range(NCH):
            self.stt(dst_fn(c), h[:, c, t0:t0 + n], self.vcol(gname, c), rstd[:, :n], ALU.mult, ALU.mult,
                     ["h%d" % ti, "rstd", "vec"], [dst_res])

    def load_w(self, dst, src, res, nsplit=1):
        k = dst.shape[1]
        srcv = src.rearrange("(k p) n -> p k n", p=128)
        step = max(1, k // nsplit)
        for a in range(0, k, step):
            b = min(k, a + step)
            self.S.dma("gpsimd", res, dst[:, a:b, :], srcv[:, a:b, :], writes=[res])

    def ffn_layer(self, i):
        saved_tiles = self.tiles
        nt_ = -(-self.NT // 512)
        base_, rem_ = divmod(self.NT, nt_)
        ft, t_ = [], 0
        for k_ in range(nt_):
            sz = base_ + (1 if k_ < rem_ else 0)
            ft.append((t_, sz))
            t_ += sz
        self.tiles = ft
        try:
            self._ffn_layer(i)
        finally:
            self.tiles = saved_tiles

    def _ffn_layer(self, i):
        S = self.S
        with contextlib.ExitStack() as ph:
            wup = [self.sb("wup%d" % b, [128, NCH, 1024], BF16, ph) for b in range(2)]
            wdn = [self.sb("wdn%d" % b, [128, NCH, 1024], BF16, ph) for b in range(2)]
            hid = self.sb("hid", [128, NCH, 512], BF16, ph)
            rtmp = [self.sb("rtmp%d" % b, [128, 512], BF16, ph) for b in range(2)]
            ub = self.ubuf
            h = self.h
            def ffn_load(q):
                b = q % 2
                self.load_w(wup[b][:, :, :], self.ffn_up[i, :, q * 1024:(q + 1) * 1024], "wup%d" % b, nsplit=2)
                self.load_w(wdn[b][:, :, :], self.ffn_down[i, q * 1024:(q + 1) * 1024, :], "wdn%d" % b, nsplit=2)
            ffn_load(0)
            for q in range(4):
                b = q % 2
                if q + 1 < 4:
                    ffn_load(q + 1)
                for ti, (t0, n) in enumerate(self.tiles):
                    if q == 0:
                        self.norm_tile(ti, "norm_ffn%d" % i, lambda c: ub[:, c, t0:t0 + n], "u%d" % ti)
                    for j in range(NCH):
                        pb = self.ps[j % 4]
                        pr = "ps%d" % (j % 4)
                        for c in range(NCH):
                            self.mm(pb[:, :n], wup[b][:, c, j * 128:(j + 1) * 128], ub[:, c, t0:t0 + n], c == 0, c == NCH - 1,
                                    ["wup%d" % b, "u%d" % ti], [pr])
                        rt = rtmp[j % 2]
                        self.act(rt[:, :n], pb[:, :n], AF.Relu, [pr], ["rtmp%d" % (j % 2)])
                        self.tt("vector", hid[:, j, :n], rt[:, :n], rt[:, :n], ALU.mult, ["rtmp%d" % (j % 2)], ["hid%d" % j])
                    for d in range(NCH):
                        pb = self.ps[4 + d % 2]
                        pr = "ps%d" % (4 + d % 2)
                        for j in range(NCH):
                            self.mm(pb[:, :n], wdn[b][:, j, d * 128:(d + 1) * 128], hid[:, j, :n], j == 0, j == NCH - 1,
                                    ["wdn%d" % b, "hid%d" % j], [pr])
                        self.tt("vector", h[:, d, t0:t0 + n], h[:, d, t0:t0 + n], pb[:, :n], ALU.add, [pr, "h%d" % ti], ["h%d" % ti])
            S.barrier()

    def mix_norm_all(self, i, col0=0):
        ub = self.ubuf
        for ti, (t0, n) in enumerate(self.tiles):
            self.norm_tile(ti, "norm_mix%d" % i, lambda c: ub[:, c, col0 + t0:col0 + t0 + n], "u%d" % ti)

    def out_proj(self, xo, wo, wres, xres_fn):
        h = self.h
        for ti, (t0, n) in enumerate(self.tiles):
            for d in range(NCH):
                pb = self.ps[4 + d % 2]
                pr = "ps%d" % (4 + d % 2)
                for j in range(NCH):
                    self.mm(pb[:, :n], wo[:, j, d * 128:(d + 1) * 128], xo[:, j, t0:t0 + n], j == 0, j == NCH - 1,
                            [wres] + xres_fn(j, ti), [pr])
                self.tt("vector", h[:, d, t0:t0 + n], h[:, d, t0:t0 + n], pb[:, :n], ALU.add, [pr, "h%d" % ti], ["h%d" % ti])

    def rwkv_layer(self, i):
        S = self.S
        L, NT = self.L, self.NT
        h, ub, ps, vec = self.h, self.ubuf, self.ps, self.vec
        C = 64
        ntl = len(self.tiles)
        cst = self.cst
        V = VEC_COLS
        self.memset("vector", ub[:, :, 0:1], 0.0, ["ushift"])
        S.dma("gpsimd", "ushift", ub[:, :, NT + 1:NT + 1 + NS], self.shift_st.rearrange("(c p) r -> p c r", p=128), writes=["ushift"])
        with contextlib.ExitStack() as ph:
            sho = self.sb("sho", [128, NCH, 1 + NS], F32, ph)
            for ti, (t0, n) in enumerate(self.tiles):
                self.norm_tile(ti, "norm_mix%d" % i, lambda c: ub[:, c, 1 + t0:1 + t0 + n], "u%d" % ti)
                if t0 + n == L:
                    for c in range(NCH):
                        self.stt(sho[:, c, 0:1], h[:, c, L - 1:L], self.vcol("norm_mix%d" % i, c), self.rstd[:, n - 1:n], ALU.mult, ALU.mult,
                                 ["h%d" % ti, "rstd", "vec"], ["sho"])
                if t0 >= L:
                    for c in range(NCH):
                        self.stt(sho[:, c, 1:1 + NS], h[:, c, L:NT], self.vcol("norm_mix%d" % i, c), self.rstd[:, :NS], ALU.mult, ALU.mult,
                                 ["h%d" % ti, "rstd", "vec"], ["sho"])
            S.dma("sync", "sho", self.shift_o.rearrange("(c p) n -> p c n", p=128), sho[:, :, :], reads=["sho"])

            def u_ap(c, ti):
                t0, n = self.tiles[ti]
                return ub[:, c, 1 + t0:1 + t0 + n]

            def p_ap(c, ti):
                t0, n = self.tiles[ti]
                if t0 < L:
                    return ub[:, c, t0:t0 + n]
                return ub[:, c, NT + 1:NT + 1 + NS]

            def u_res(ti):
                return ["u%d" % ti, "ushift"] + (["u%d" % (ti - 1)] if ti > 0 else [])

            xv = self.sb("rxv", [128, 64], F32, ph)
            self.ts("vector", xv[:, 0:8], vec[:, V["w0"]:V["w0"] + 8], -1.0, ALU.mult, ["vec"], ["rxv"])
            self.ts("vector", xv[:, 8:16], vec[:, V["a0"]:V["a0"] + 8], -1.0, ALU.mult, ["vec"], ["rxv"])
            self.memset("vector", xv[:, 16:17], 1e-24, ["rxv"])
            for j in range(6):
                self.ts("vector", vec[:, V["omu%d" % j]:V["omu%d" % j] + 8], vec[:, V["mu%d" % j]:V["mu%d" % j] + 8], -1.0, ALU.mult,
                        ["vec"], ["vec"], s2=1.0, op1=ALU.add)
            self.ts("vector", vec[:, V["omk_a"]:V["omk_a"] + 8], vec[:, V["k_a"]:V["k_a"] + 8], -1.0, ALU.mult, ["vec"], ["vec"], s2=1.0, op1=ALU.add)
            onesblk = self.sb("onesblk", [128, 128], BF16, ph)
            self.memset("vector", onesblk[:, :], 0.0, ["onesblk"])
            self.memset("vector", onesblk[0:64, 0:64], 1.0, ["onesblk"])
            self.memset("vector", onesblk[64:128, 64:128], 1.0, ["onesblk"])
            i2b = self.sb("i2b", [128, 64], BF16, ph)
            self.cp("vector", i2b[:, :], cst[:, self.C_I2:self.C_I2 + 64], ["cst"], ["i2b"])
            cm64 = self.sb("cm64", [128, 512], F32, ph)
            self.memset("gpsimd", cm64[:, :], 1.0, ["cm64"])
            self.memset("gpsimd", cm64[:, :].rearrange("p (n k) -> p n k", k=64)[:, :, 0:1], 0.0, ["cm64"])

            stage = self.sb("rstage", [128, NCH, 160], F32, ph)
            self._eng_rr = 0

            def prep_w(src, m, muj, dA, dB, res):
                S.dma("sync", "rstage", stage[:, :, :m], src.rearrange("(c p) n -> p c n", p=128), writes=["rstage"])
                for c in range(NCH):
                    e = "vector"
                    self.ts(e, dA[:, c, :], stage[:, c, :m], vec[:, V["omu%d" % muj] + c:V["omu%d" % muj] + c + 1], ALU.mult, ["rstage", "vec"], [res])
                    self.ts(e, dB[:, c, :], stage[:, c, :m], vec[:, V["mu%d" % muj] + c:V["mu%d" % muj] + c + 1], ALU.mult, ["rstage", "vec"], [res])

            tw = self.sb("rtw", [64, NT], BF16, ph)
            ta = self.sb("rta", [64, NT], BF16, ph)
            tg0 = self.sb("rtg0", [128, NT], BF16, ph)
            tg1 = self.sb("rtg1", [32, NT], BF16, ph)
            tmpA = self.rstd
            with contextlib.ExitStack() as ph2:
                l1 = [self.sb("rl1_%d" % k, [128, NCH, m], BF16, ph2) for k, m in enumerate([64, 64, 64, 64, 160, 160])]
                prep_w(self.rwkv_w1, 64, 1, l1[0], l1[1], "rl1w")
                prep_w(self.rwkv_a1, 64, 4, l1[2], l1[3], "rl1a")
                prep_w(self.rwkv_g1, 160, 5, l1[4], l1[5], "rl1g")
                for ti, (t0, n) in enumerate(self.tiles):
                    specs = [(l1[0][:, :, :], l1[1][:, :, :], 64, "rl1w"), (l1[2][:, :, :], l1[3][:, :, :], 64, "rl1a"),
                             (l1[4][:, :, 0:128], l1[5][:, :, 0:128], 128, "rl1g"), (l1[4][:, :, 128:160], l1[5][:, :, 128:160], 32, "rl1g")]
                    for k, (wa, wb, m, res) in enumerate(specs):
                        for c in range(NCH):
                            self.mm(ps[k][:m, :n], wa[:, c, :], u_ap(c, ti), c == 0, False, [res] + u_res(ti), ["ps%d" % k])
                        for c in range(NCH):
                            self.mm(ps[k][:m, :n], wb[:, c, :], p_ap(c, ti), False, c == NCH - 1, [res] + u_res(ti), ["ps%d" % k])
                    self.act(tmpA[:64, :n], ps[0][:64, :n], AF.Exp, ["ps0"], ["rstd"], scale=-2.0)
                    self.ts("vector", tmpA[:64, :n], tmpA[:64, :n], 1.0, ALU.add, ["rstd"], ["rstd"])
                    S.op("vector", "reciprocal", dict(out=tmpA[:64, :n], in_=tmpA[:64, :n]), reads=["rstd"], writes=["rstd"])
                    self.ts("vector", tw[:, t0:t0 + n], tmpA[:64, :n], 2.0, ALU.mult, ["rstd"], ["rtw"], s2=-1.0, op1=ALU.add)
                    self.act(ta[:, t0:t0 + n], ps[1][:64, :n], AF.Copy, ["ps1"], ["rta"])
                    for k, (dst, m) in ((2, (tg0, 128)), (3, (tg1, 32))):
                        self.act(tmpA[:m, :n], ps[k][:m, :n], AF.Exp, ["ps%d" % k], ["rstd"], scale=-1.0)
                        self.ts("vector", tmpA[:m, :n], tmpA[:m, :n], 1.0, ALU.add, ["rstd"], ["rstd"])
                        S.op("vector", "reciprocal", dict(out=tmpA[:m, :n], in_=tmpA[:m, :n]), reads=["rstd"], writes=["rstd"])
                        self.cp("vector", dst[:, t0:t0 + n], tmpA[:m, :n], ["rstd"], ["rtg%d" % (k - 2)])
                S.barrier()

            F = [self.sb("rF%d" % k, [128, 512], F32, ph) for k in range(12)]
            r_f, k_f, v_f, lw, a_f, g_f, kkn, k2, bon, cl, E1, E3 = F
            E2 = k_f
            BB = [bon, self.rstd]
            GF = [self.sq[:, 2, :], self.sq[:, 3, :]]
            yfm_sb = g_f
            gam = self.sb("rgam", [128, 2, 16], F32, ph)
            sqb = self.sq[:, 0, :]
            wpr = [self.sb("rwp%d" % k, [128, NCH, 128], BF16, ph) for k in range(6)]
            w2c = self.sb("rw2c", [64, 128], BF16, ph)
            a2c = self.sb("ra2c", [64, 128], BF16, ph)
            g2c0 = self.sb("rg2c0", [128, 128], BF16, ph)
            g2c1 = self.sb("rg2c1", [32, 128], BF16, ph)
            woc = self.sb("rwoc", [128, 1024], BF16, ph)
            xot = self.sq[:, 1, :]
            NK = 8
            pads = {}
            for kind in ("p",):
                pads[kind] = dict(
                    AR=self.sb("rAR" + kind, [128, NK, 4, C], BF16, ph),
                    B=self.sb("rB" + kind, [128, NK, 2, C], BF16, ph),
                    K=self.sb("rK" + kind, [128, NK, 2, C], BF16, ph),
                    V=self.sb("rV" + kind, [128, NK, 2, C], BF16, ph),
                    BH=self.sb("rBH" + kind, [128, NK, 2, C], BF16, ph),
                    KH=self.sb("rKH" + kind, [128, NK, 2, C], BF16, ph))
                for nm, t_ in pads[kind].items():
                    self.memset("gpsimd", t_[:, :, :, :], 0.0, ["pad" + kind + nm])
            import os as _os
            WSL = int(_os.environ.get("RW_WSL", "2"))
            Gs = self.sb("rGs", [128, NK, 128], BF16, ph)
            Zb = self.sb("rZb", [128, NK, 64], BF16, ph)
            Rqs = self.sb("rRqs", [128, NK, 128], BF16, ph)
            Yz = self.sb("rYz", [128, NK, 64], BF16, ph)
            SL = []
            for s_ in range(WSL):
                SL.append(dict(
                    XQ=[self.sb("rXQ%d_%d" % (s_, b_), [128, 256], BF16, ph) for b_ in range(2)],
                    Xt=[self.sb("rXt%d_%d" % (s_, b_), [128, 128], BF16, ph) for b_ in range(2)],
                    MM2=self.sb("rMM2_%d" % s_, [128, 256], BF16, ph),
                    MRB=self.sb("rMRB_%d" % s_, [128, 128], BF16, ph),
                    BhT=self.sb("rBhT_%d" % s_, [128, 128], BF16, ph),
                    KhT=self.sb("rKhT_%d" % s_, [128, 128], BF16, ph),
                    AT=self.sb("rAT_%d" % s_, [128, 128], BF16, ph),
                    AqT=self.sb("rAqT_%d" % s_, [128, 128], BF16, ph),
                    Vst=self.sb("rVst_%d" % s_, [128, 64], BF16, ph),
                    MakV=self.sb("rMakV_%d" % s_, [128, 64], BF16, ph),
                    Vq=self.sb("rVq_%d" % s_, [128, 64], BF16, ph)))
            Tfs = [self.sb("rTf%d" % b_, [128, 64], F32, ph) for b_ in range(2)]
            Tbs = [self.sb("rTb%d" % b_, [128, 64], BF16, ph) for b_ in range(2)]
            Tf, Tb = Tfs[0], Tbs[0]
            NYB = 4
            Ypad = [self.sb("rYpad%d" % b_, [128, 2, 64], BF16, ph) for b_ in range(NYB)]
            for b_ in range(NYB):
                self.memset("gpsimd", Ypad[b_][:, :, :], 0.0, ["rYpad%d" % b_])
            bsts = [self.sb("rbst%d" % b_, [128, 6], F32, ph) for b_ in range(NYB)]
            bags = [self.sb("rbag%d" % b_, [128, 4], F32, ph) for b_ in range(NYB)]
            Ysb = [self.sb("rYsb%d" % b_, [128, 64], F32, ph) for b_ in range(NYB)]
            ysmp = self.sb("rysmp", [128, NS], F32, ph)
            MSUIU = cst[:, self.C_SUIU:self.C_SUIU + 256]
            MSU = cst[:, self.C_SUIU:self.C_SUIU + 128]
            MIU = cst[:, self.C_SUIU + 128:self.C_SUIU + 256]
            MSL = cst[:, self.C_SL:self.C_SL + 128]
            IDN = cst[:, self.C_ID:self.C_ID + 128]
            wkv_st = self.wkv_st

            for cc in range(NCH):
                cs = slice(cc * 128, (cc + 1) * 128)
                for k, (widx, muj) in enumerate(((0, 0), (1, 2), (2, 3))):
                    prep_w(self.rwkv_w_rkv[widx][:, cs], 128, muj, wpr[2 * k], wpr[2 * k + 1], "rwp%d" % k)
                S.dma("gpsimd", "rw2c", w2c[:, :], self.rwkv_w2[:, cs], writes=["rw2c"])
                S.dma("gpsimd", "ra2c", a2c[:, :], self.rwkv_a2[:, cs], writes=["ra2c"])
                S.dma("gpsimd", "rg2c0", g2c0[:, :], self.rwkv_g2[0:128, cs], writes=["rg2c"])
                S.dma("gpsimd", "rg2c1", g2c1[:, :], self.rwkv_g2[128:160, cs], writes=["rg2c"])
                S.dma("gpsimd", "rwoc", woc[:, :], self.rwkv_wo[cs, :], writes=["rwoc"])
                self.memset("vector", Tf[:, :], 0.0, ["rTf0"])
                self.memset("vector", Tb[:, :], 0.0, ["rTb0"])
                def p1a(ti, info):
                    t0, n = self.tiles[ti]
                    par = ti % 2
                    samp = t0 >= L
                    if False:
                        yield
                    for k, (dst_, dres_) in enumerate(((r_f, "r_f"), (k_f, "k_f"), (v_f, "v_f"))):
                        for c in range(NCH):
                            self.mm(ps[6][:, :n], wpr[2 * k][:, c, :], u_ap(c, ti), c == 0, False, ["rwp%d" % k] + u_res(ti), ["ps6"])
                            if c % 4 == 3:
                                yield
                        for c in range(NCH):
                            self.mm(ps[6][:, :n], wpr[2 * k + 1][:, c, :], p_ap(c, ti), False, c == NCH - 1, ["rwp%d" % k] + u_res(ti), ["ps6"])
                            if c % 4 == 3:
                                yield
                        self.act(dst_[:, :n], ps[6][:, :n], AF.Copy, ["ps6"], [dres_])
                        yield
                    self.mm(ps[6][:, :n], w2c[:, :], tw[:, t0:t0 + n], True, True, ["rw2c", "rtw"], ["ps6"])
                    self.act(lw[:, :n], ps[6][:, :n], AF.Exp, ["ps6", "rxv"], ["lw"], scale=-1.0, bias=xv[:, cc:cc + 1])
                    yield
                    self.mm(ps[6][:, :n], a2c[:, :], ta[:, t0:t0 + n], True, True, ["ra2c", "rta"], ["ps6"])
                    self.ts("vector", lw[:, :n], lw[:, :n], 1.0, ALU.add, ["lw"], ["lw"])
                    S.op("vector", "reciprocal", dict(out=lw[:, :n], in_=lw[:, :n]), reads=["lw"], writes=["lw"])
                    self.ts("vector", lw[:, :n], lw[:, :n], -0.6065306597126334, ALU.mult, ["lw"], ["lw"])
                    yield
                    yield
                    self.act(a_f[:, :n], ps[6][:, :n], AF.Exp, ["ps6", "rxv"], ["a_f"], scale=-1.0, bias=xv[:, 8 + cc:9 + cc])
                    yield
                    self.mm(ps[6][:, :n], g2c0[:, :], tg0[:, t0:t0 + n], True, False, ["rg2c", "rtg0"], ["ps6"])
                    self.mm(ps[6][:, :n], g2c1[:, :], tg1[:, t0:t0 + n], False, True, ["rg2c", "rtg1"], ["ps6"])
                    self.ts("vector", a_f[:, :n], a_f[:, :n], 1.0, ALU.add, ["a_f"], ["a_f"])
                    S.op("vector", "reciprocal", dict(out=a_f[:, :n], in_=a_f[:, :n]), reads=["a_f"], writes=["a_f"])
                    yield
                    self.act(GF[par][:, :n], ps[6][:, :n], AF.Copy, ["ps6"], ["GF%d" % par])
                    yield
                    self.ts("vector", kkn[:, :n], k_f[:, :n], self.vcol("k_k", cc), ALU.mult, ["k_f", "vec"], ["kkn"])
                    self.tt("gpsimd", sqb[:, :n], kkn[:, :n], kkn[:, :n], ALU.mult, ["kkn"], ["rsqb"])
                    yield
                    self.mm(ps[6][:, :n], onesblk[:, :], sqb[:, :n], True, True, ["onesblk", "rsqb"], ["ps6"])
                    yield
                    self.act(E1[:, :n], ps[6][:, :n], AF.Ln, ["ps6", "rxv"], ["E1"], bias=xv[:, 16:17])
                    self.act(E1[:, :n], E1[:, :n], AF.Exp, ["E1"], ["E1"], scale=-0.5)
                    self.tt("vector", kkn[:, :n], kkn[:, :n], E1[:, :n], ALU.mult, ["kkn", "E1"], ["kkn"])
                    self.ts("vector", k2[:, :n], a_f[:, :n], self.vcol("k_a", cc), ALU.mult, ["a_f", "vec"], ["k2"], s2=self.vcol("omk_a", cc), op1=ALU.add)
                    self.tt("vector", k2[:, :n], k2[:, :n], k_f[:, :n], ALU.mult, ["k2", "k_f"], ["k2"])
                    yield
                    self.stt(sqb[:, :n], r_f[:, :n], self.vcol("r_k", cc), k2[:, :n], ALU.mult, ALU.mult, ["r_f", "k2", "vec", "rsqb"], ["rsqb"])
                    yield
                    self.mm(ps[6][:, :n], onesblk[:, :], sqb[:, :n], True, True, ["onesblk", "rsqb"], ["ps6"])
                    yield
                    self.tt("vector", BB[par][:, :n], ps[6][:, :n], v_f[:, :n], ALU.mult, ["ps6", "v_f"], ["BB%d" % par])
                    self.ts("vector", BB[par][:, :n], BB[par][:, :n], self.vcol("ln_b", cc), ALU.add, ["BB%d" % par, "vec"], ["BB%d" % par])
                    yield
                    self.tt("gpsimd", a_f[:, :n], a_f[:, :n], kkn[:, :n], ALU.mult, ["a_f", "kkn"], ["a_f"])
                    bv = a_f
                    if not samp:
                        S.op("vector", "tensor_tensor_scan", dict(out=cl[:, :n], data0=cm64[:, :n], data1=lw[:, :n], initial=0.0,
                                                                  op0=ALU.mult, op1=ALU.add), reads=["lw", "cm64"], writes=["cl"])
                    else:
                        self.cp("vector", cl[:, :n], lw[:, :n], ["lw"], ["cl"])
                    yield
                    self.act(E1[:, :n], cl[:, :n], AF.Exp, ["cl"], ["E1"])
                    self.act(E2[:, :n], cl[:, :n], AF.Exp, ["cl", "k_f"], ["k_f"], scale=-1.0)
                    self.tt("gpsimd", E3[:, :n], cl[:, :n], lw[:, :n], ALU.subtract, ["cl", "lw"], ["E3"])
                    self.act(E3[:, :n], E3[:, :n], AF.Exp, ["E3"], ["E3"])
                    yield
                    E4 = lw
                    if not samp:
                        nck = n // C
                        for q in range(nck):
                            last = q * C + C - 1
                            self.ts("vector", E4[:, q * C:(q + 1) * C], cl[:, q * C:(q + 1) * C], -1.0, ALU.mult,
                                    ["cl", "lw", "E3"], ["lw"], s2=cl[:, last:last + 1], op1=ALU.add)
                        self.act(E4[:, :n], E4[:, :n], AF.Exp, ["lw"], ["lw"])
                        yield
                        self.cp("vector", gam[:, par, 0:nck], E1[:, 0:n].rearrange("p (q k) -> p q k", k=C)[:, :, C - 1], ["E1"], ["gam%d" % par])
                        groups = [("p", 0, nck)]
                    else:
                        self.memset("vector", E4[:, :n], 1.0, ["lw"])
                        self.cp("vector", gam[:, par, 0:NS], E1[:, 0:NS], ["E1"], ["gam%d" % par])
                        groups = [("s", 0, NK), ("s", NK, NK)]
                    info["groups"] = groups
                    yield

                infos = [dict() for _ in range(ntl)]
                g_first = p1a(0, infos[0])
                for _ in g_first:
                    pass
                for ti, (t0, n) in enumerate(self.tiles):
                    samp = t0 >= L
                    par = ti % 2
                    bv = a_f
                    E4 = lw
                    groups = infos[ti]["groups"]
                    nxt_gen = p1a(ti + 1, infos[ti + 1]) if ti + 1 < ntl else None
                    if samp:
                        nxt_gen = None
                    for (kind, g0, gn) in groups:
                        P = pads["p"]
                        if kind == "s":
                            for nm, t_ in P.items():
                                nb = 4 if nm == "AR" else 2
                                for hh in range(2):
                                    hs = slice(64 * hh, 64 * hh + 64)
                                    blks = [hh, 2 + hh] if nm == "AR" else [hh]
                                    for blk in blks:
                                        self.memset("gpsimd" if hh else "vector", t_[hs, :, blk, :], 0.0, ["padp" + nm])

                        def pv(t_, hh, blk):
                            hs = slice(64 * hh, 64 * hh + 64)
                            if kind == "p":
                                return t_[hs, 0:gn, blk, :]
                            return t_[hs, 0:gn, blk, 0]

                        def fv(t_, hh):
                            hs = slice(64 * hh, 64 * hh + 64)
                            if kind == "p":
                                return t_[hs, 0:n].rearrange("p (q k) -> p q k", k=C)
                            return t_[hs, g0:g0 + gn]
                        for hh in range(2):
                            e1 = "vector" if hh == 0 else "gpsimd"
                            self.stt(pv(P["AR"], hh, hh), fv(kkn, hh), -1.0, fv(E3, hh), ALU.mult, ALU.mult, ["kkn", "E3"], ["padpAR"])
                            self.tt(e1, pv(P["AR"], hh, 2 + hh), fv(r_f, hh), fv(E1, hh), ALU.mult, ["r_f", "E1"], ["padpAR"])
                            self.tt(e1, pv(P["B"], hh, hh), fv(bv, hh), fv(E2, hh), ALU.mult, ["a_f", "k_f"], ["padpB"])
                            self.tt(e1, pv(P["K"], hh, hh), fv(k2, hh), fv(E2, hh), ALU.mult, ["k2", "k_f"], ["padpK"])
                            self.cp(e1, pv(P["V"], hh, hh), fv(v_f, hh), ["v_f"], ["padpV"])
                            self.tt(e1, pv(P["BH"], hh, hh), fv(bv, hh), fv(E4, hh), ALU.mult, ["a_f", "lw"], ["padpBH"])
                            self.tt(e1, pv(P["KH"], hh, hh), fv(k2, hh), fv(E4, hh), ALU.mult, ["k2", "lw"], ["padpKH"])
                        pr = lambda nm: "padp" + nm
                        identb = self.identb

                        def indep(q, sidx):
                            sl = SL[sidx]
                            bA, bB = ps[2 * sidx], ps[2 * sidx + 1]
                            rA, rB = "ps%d" % (2 * sidx), "ps%d" % (2 * sidx + 1)
                            sr = lambda nm: "sl%d_%s" % (sidx, nm)
                            ARq = P["AR"][:, q, :, :]
                            Aq = P["AR"][:, q, 0:2, :]
                            Rq_ = P["AR"][:, q, 2:4, :]
                            Bq = P["B"][:, q, :, :]
                            Kq = P["K"][:, q, :, :]
                            XQ, Xt = sl["XQ"], sl["Xt"]
                            tb0 = sidx * 256
                            psT = self.psT
                            XQE = "gpsimd" if _os.environ.get("RW_XQPOOL", "1") == "1" else "vector"
                            if "all" in _os.environ.get("RW_SKIP", ""):
                                return
                            yield

                            def tr(dst, src, res):
                                S.op("tensor", "transpose", dict(out=dst, in_=src, identity=identb[:, :]), reads=[res, "identb"], writes=["psT"])
                            if kind == "p":
                                self.mm(bA[:, 0:256], Bq, ARq, True, True, [pr("B"), pr("AR")], [rA])
                                self.mm(bB[:, 0:128], Aq, Bq, True, True, [pr("B"), pr("AR")], [rB])
                                tr(psT[:, tb0:tb0 + 128], P["BH"][:, q, :, :], pr("BH"))
                                tr(psT[:, tb0 + 128:tb0 + 256], P["KH"][:, q, :, :], pr("KH"))
                                yield
                                self.tt("vector", XQ[0][:, 0:128], bA[:, 0:128], MSU, ALU.mult, [rA, "cst"], [sr("XQ0")])
                                self.tt("vector", sl["MRB"][:, :], bA[:, 128:256], MIU, ALU.mult, [rA, "cst"], [sr("MRB")])
                                self.tt("vector", Xt[0][:, :], bB[:, 0:128], MSL, ALU.mult, [rB, "cst"], [sr("Xt0")])
                                self.act(sl["BhT"][:, :], psT[:, tb0:tb0 + 128], AF.Copy, ["psT"], [sr("BhT")])
                                self.act(sl["KhT"][:, :], psT[:, tb0 + 128:tb0 + 256], AF.Copy, ["psT"], [sr("KhT")])
                                yield
                                self.tt(XQE, XQ[0][:, 128:256], XQ[0][:, 0:128], IDN, ALU.add, [sr("XQ0"), "cst"], [sr("XQ0")])
                                self.mm(bA[:, 0:256], Kq, ARq, True, True, [pr("K"), pr("AR")], [rA])
                                self.mm(bB[:, 0:64], P["V"][:, q, :, :], i2b[:, :], True, True, [pr("V"), "i2b"], [rB])
                                tr(psT[:, tb0:tb0 + 128], Aq, pr("AR"))
                                yield
                                self.tt("vector", sl["MM2"][:, :], bA[:, 0:256], MSUIU, ALU.mult, [rA, "cst"], [sr("MM2")])
                                self.act(sl["Vst"][:, :], bB[:, 0:64], AF.Copy, [rB], [sr("Vst")])
                                self.act(sl["AT"][:, :], psT[:, tb0:tb0 + 128], AF.Copy, ["psT"], [sr("AT")])
                                yield
                                cur = 0
                                for lev in range(6):
                                    nxt = 1 - cur
                                    xc, xn = sr("XQ%d" % cur), sr("XQ%d" % nxt)
                                    tc, tn = sr("Xt%d" % cur), sr("Xt%d" % nxt)
                                    if lev == 0:
                                        self.mm(bA[:, 0:128], Xt[cur][:, :], XQ[cur][:, 0:128], True, True, [tc, xc], [rA])
                                        self.mm(bB[:, 0:128], XQ[cur][:, 0:128], Xt[cur][:, :], True, True, [tc, xc], [rB])
                                        yield
                                        self.act(XQ[nxt][:, 0:128], bA[:, 0:128], AF.Copy, [rA], [xn])
                                        self.cp(XQE, XQ[nxt][:, 128:256], XQ[cur][:, 128:256], [xc], [xn])
                                        self.cp("vector", Xt[nxt][:, :], bB[:, 0:128], [rB], [tn])
                                        yield
                                    elif lev < 5:
                                        self.mm(bA[:, 0:256], Xt[cur][:, :], XQ[cur][:, :], True, True, [tc, xc], [rA])
                                        self.mm(bB[:, 0:128], XQ[cur][:, 0:128], Xt[cur][:, :], True, True, [tc, xc], [rB])
                                        yield
                                        self.act(XQ[nxt][:, 0:128], bA[:, 0:128], AF.Copy, [rA], [xn])
                                        self.tt("vector", XQ[nxt][:, 128:256], bA[:, 128:256], XQ[cur][:, 128:256], ALU.add, [rA, xc], [xn])
                                        self.cp("vector", Xt[nxt][:, :], bB[:, 0:128], [rB], [tn])
                                        yield
                                    else:
                                        self.mm(bA[:, 0:128], Xt[cur][:, :], XQ[cur][:, 128:256], True, True, [tc, xc], [rA])
                                        self.mm(bB[:, 0:64], sl["MM2"][:, 0:128], sl["Vst"][:, :], True, True, [sr("MM2"), sr("Vst")], [rB])
                                        yield
                                        self.tt("vector", XQ[nxt][:, 128:256], bA[:, 0:128], XQ[cur][:, 128:256], ALU.add, [rA, xc], [xn])
                                        self.act(sl["MakV"][:, :], bB[:, 0:64], AF.Copy, [rB], [sr("MakV")])
                                        yield
                                    cur = nxt
                                Q = XQ[cur][:, 128:256]
                                qres = sr("XQ%d" % cur)
                                self.mm(bA[:, 0:128], Q, sl["AT"][:, :], True, True, [qres, sr("AT")], [rA])
                                self.mm(bB[:, 0:64], Q, sl["MakV"][:, :], True, True, [qres, sr("MakV")], [rB])
                                yield
                                self.act(sl["AqT"][:, :], bA[:, 0:128], AF.Copy, [rA], [sr("AqT")])
                                self.cp("vector", sl["Vq"][:, :], bB[:, 0:64], [rB], [sr("Vq")])
                                yield
                                AqT, aqres = sl["AqT"], sr("AqT")
                            else:
                                self.mm(bA[:, 0:128], Bq, Rq_, True, True, [pr("B"), pr("AR")], [rA])
                                self.mm(bB[:, 0:128], Kq, Rq_, True, True, [pr("K"), pr("AR")], [rB])
                                tr(psT[:, tb0:tb0 + 128], P["BH"][:, q, :, :], pr("BH"))
                                tr(psT[:, tb0 + 128:tb0 + 256], P["KH"][:, q, :, :], pr("KH"))
                                yield
                                self.tt("vector", sl["MRB"][:, :], bA[:, 0:128], MIU, ALU.mult, [rA, "cst"], [sr("MRB")])
                                self.tt("vector", sl["MM2"][:, 128:256], bB[:, 0:128], MIU, ALU.mult, [rB, "cst"], [sr("MM2")])
                                self.act(sl["BhT"][:, :], psT[:, tb0:tb0 + 128], AF.Copy, ["psT"], [sr("BhT")])
                                self.act(sl["KhT"][:, :], psT[:, tb0 + 128:tb0 + 256], AF.Copy, ["psT"], [sr("KhT")])
                                yield
                                self.mm(bB[:, 0:64], P["V"][:, q, :, :], i2b[:, :], True, True, [pr("V"), "i2b"], [rB])
                                tr(psT[:, tb0:tb0 + 128], Aq, pr("AR"))
                                yield
                                self.act(sl["Vst"][:, :], bB[:, 0:64], AF.Copy, [rB], [sr("Vst")])
                                self.cp("vector", sl["AT"][:, :], psT[:, tb0:tb0 + 128], ["psT"], [sr("AT")])
                                yield
                                AqT, aqres = sl["AT"], sr("AT")
                            self.mm(bA[:, 0:128], AqT[:, :], sl["BhT"][:, :], True, True, [aqres, sr("BhT")], [rA])
                            self.mm(bB[:, 0:128], AqT[:, :], sl["MRB"][:, :], True, True, [aqres, sr("MRB")], [rB])
                            yield
                            self.act(Gs[:, q, :], bA[:, 0:128], AF.Copy, [rA], ["rGs%d" % q])
                            self.tt("vector", Rqs[:, q, :], bB[:, 0:128], Rq_, ALU.add, [rB, pr("AR")], ["rRqs%d" % q])
                            yield
                            if kind == "p":
                                self.mm(bA[:, 0:64], sl["BhT"][:, :], sl["Vq"][:, :], True, False, [sr("BhT"), sr("Vq")], [rA])
                                self.mm(bA[:, 0:64], sl["KhT"][:, :], sl["Vst"][:, :], False, True, [sr("KhT"), sr("Vst")], [rA])
                                self.mm(bB[:, 0:64], sl["MRB"][:, :], sl["Vq"][:, :], True, False, [sr("MRB"), sr("Vq")], [rB])
                                self.mm(bB[:, 0:64], sl["MM2"][:, 128:256], sl["Vst"][:, :], False, True, [sr("MM2"), sr("Vst")], [rB])
                            else:
                                self.mm(bA[:, 0:64], sl["KhT"][:, :], sl["Vst"][:, :], True, True, [sr("KhT"), sr("Vst")], [rA])
                                self.mm(bB[:, 0:64], sl["MM2"][:, 128:256], sl["Vst"][:, :], True, True, [sr("MM2"), sr("Vst")], [rB])
                            yield
                            self.act(Zb[:, q, :], bA[:, 0:64], AF.Copy, [rA], ["rZb%d" % q])
                            self.cp("vector", Yz[:, q, :], bB[:, 0:64], [rB], ["rYz%d" % q])
                            yield

                        def sidegen(q, yb, row):
                            bst, bag, ysb, ypad = bsts[yb], bags[yb], Ysb[yb], Ypad[yb]
                            yield
                            self.act(bag[:, 2:3], bag[:, 1:2], AF.Ln, ["rbag%d" % yb, "consts"], ["rbag%d" % yb], bias=self.eps_t[:, 3:4])
                            self.act(bag[:, 2:3], bag[:, 2:3], AF.Exp, ["rbag%d" % yb], ["rbag%d" % yb], scale=-0.5)
                            yield
                            for hh in range(2):
                                hs = slice(64 * hh, 64 * hh + 64)
                                self.ts("vector", ypad[hs, hh, :], ysb[hs, :], bag[hs, 0:1], ALU.subtract,
                                        ["rYsb%d" % yb, "rbag%d" % yb], ["rYpad%d" % yb], s2=bag[hs, 2:3], op1=ALU.mult)
                            yield
                            self.mm(ps[5][:, 64:128], ypad[:, :, :], i2b[:, :], True, True, ["rYpad%d" % yb, "i2b"], ["ps5"])
                            yield
                            if kind == "p":
                                self.act(yfm_sb[:, q * C:(q + 1) * C], ps[5][:, 64:128], AF.Copy, ["ps5"], ["yfm"])
                            else:
                                self.cp("vector", yfm_sb[:, row:row + 1], ps[5][:, 64:65], ["ps5"], ["yfm"])
                            yield

                        def seqgen(done):
                            if "seq" in _os.environ.get("RW_SKIP", ""):
                                return
                            sides = []

                            def adv_sides():
                                for sd in list(sides):
                                    try:
                                        next(sd)
                                    except StopIteration:
                                        sides.remove(sd)
                            for q in range(gn):
                                while q not in done:
                                    adv_sides()
                                    yield
                                row = g0 + q
                                if kind == "s":
                                    tb_i = q % 2
                                    S.dma("sync", "rTf%d" % tb_i, Tfs[tb_i][:, :], wkv_st[row, cc], writes=["rTf%d" % tb_i])
                                    self.cp("vector", Tbs[tb_i][:, :], Tfs[tb_i][:, :], ["rTf%d" % tb_i], ["rTb%d" % tb_i])
                                    gcol = gam[:, par, row:row + 1]
                                else:
                                    tb_i = 0
                                    gcol = gam[:, par, q:q + 1]
                                tf_, tb_ = Tfs[tb_i], Tbs[tb_i]
                                tfr, tbr = "rTf%d" % tb_i, "rTb%d" % tb_i
                                yb = q % NYB
                                self.mm(ps[4][:, 0:64], Gs[:, q, :], tb_[:, :], True, False, ["rGs%d" % q, tbr], ["ps4"])
                                self.mm(ps[4][:, 0:64], identb[:, :], Zb[:, q, :], False, True, ["identb", "rZb%d" % q], ["ps4"])
                                self.mm(ps[5][:, 0:64], Rqs[:, q, :], tb_[:, :], True, False, ["rRqs%d" % q, tbr], ["ps5"])
                                self.mm(ps[5][:, 0:64], identb[:, :], Yz[:, q, :], False, True, ["identb", "rYz%d" % q], ["ps5"])
                                adv_sides()
                                yield
                                self.stt(tb_[:, :], tf_[:, :], gcol, ps[4][:, 0:64], ALU.mult, ALU.add, [tfr, "gam%d" % par, "ps4"], [tbr])
                                self.cp("vector", Ysb[yb][:, :], ps[5][:, 0:64], ["ps5"], ["rYsb%d" % yb])
                                self.stt(tf_[:, :], tf_[:, :], gcol, ps[4][:, 0:64], ALU.mult, ALU.add, [tfr, "gam%d" % par, "ps4"], [tfr])
                                if kind == "s":
                                    S.dma("sync", "rTfo%d" % tb_i, self.wkv_so[row, cc], tf_[:, :], reads=[tfr])
                                S.op("vector", "bn_stats", dict(out=bsts[yb][:, :], in_=Ysb[yb][:, :]), reads=["rYsb%d" % yb], writes=["rbst%d" % yb])
                                S.op("vector", "bn_aggr", dict(out=bags[yb][:, 0:2], in_=bsts[yb][:, :]), reads=["rbst%d" % yb], writes=["rbag%d" % yb])
                                sd = sidegen(q, yb, row)
                                next(sd)
                                sides.append(sd)
                                yield
                            while sides:
                                adv_sides()
                                yield

                        from collections import deque
                        pend = deque(range(gn))
                        slots = [None] * WSL
                        done = set()
                        sg = seqgen(done)
                        seq_alive = True
                        nx_alive = nxt_gen is not None and (kind == "p" or g0 > 0 or True)
                        while pend or any(x is not None for x in slots) or seq_alive or nx_alive:
                            if nx_alive:
                                try:
                                    next(nxt_gen)
                                except StopIteration:
                                    nx_alive = False
                            for sidx in range(WSL):
                                if slots[sidx] is None and pend:
                                    q_ = pend.popleft()
                                    slots[sidx] = (q_, indep(q_, sidx))
                                if slots[sidx] is not None:
                                    q_, g_ = slots[sidx]
                                    try:
                                        next(g_)
                                    except StopIteration:
                                        done.add(q_)
                                        slots[sidx] = None
                            if seq_alive:
                                try:
                                    next(sg)
                                except StopIteration:
                                    seq_alive = False
                    if not samp and t0 + n == L:
                        S.dma("sync", "rTfo0", self.wkv_po[cc], Tf[:, :], reads=["rTf0"])
                    self.stt(BB[par][:, :n], yfm_sb[:, :n], self.vcol("ln_w", cc), BB[par][:, :n], ALU.mult, ALU.add, ["yfm", "BB%d" % par, "vec"], ["BB%d" % par])
                    self.tt("vector", xot[:, :n], BB[par][:, :n], GF[par][:, :n], ALU.mult, ["BB%d" % par, "GF%d" % par], ["rxot"])
                    for d in range(NCH):
                        pb = ps[d % 2]
                        prr = "ps%d" % (d % 2)
                        self.mm(pb[:, :n], woc[:, d * 128:(d + 1) * 128], xot[:, :n], True, True, ["rwoc", "rxot"], [prr])
                        self.tt("vector", h[:, d, t0:t0 + n], h[:, d, t0:t0 + n], pb[:, :n], ALU.add, [prr, "h%d" % ti], ["h%d" % ti])
            S.barrier()

    def gla_layer(self, i):
        S = self.S
        L, NT = self.L, self.NT
        h, ub, ps = self.h, self.ubuf, self.ps
        CG = 128
        NCK = L // CG
        ntl = len(self.tiles)
        allu = ["u%d" % ti for ti in range(ntl)]
        self.mix_norm_all(i)
        w_in = self.gla_w_in
        with contextlib.ExitStack() as ph:
            self.cm128 = self.sb("cm128", [128, L], F32, ph)
            self.memset("gpsimd", self.cm128[:, :], 1.0, ["cm"])
            self.memset("gpsimd", self.cm128[:, :].rearrange("p (n k) -> p n k", k=128)[:, :, 0:1], 0.0, ["cm"])
            vtok = self.sb("vtok", [128, NCK, 256], BF16, ph)
            vtok_s = self.sb("vtok_s", [NS, 256], F32, ph)
            wvh = self.sb("gwvh", [128, NCH, 256], BF16, ph)
            glb = self.sb("glb", [16, NT], BF16, ph)
            wgl = self.sb("wgl", [128, NCH, 16], BF16, ph)
            negb = self.sb("negb", [128, 4], F32, ph)
            gb = VEC_COLS["gla_b_gk"]
            self.ts("vector", negb[:, :], self.vec[:, gb:gb + 4], -1.0, ALU.mult, ["vec"], ["negb"])
            self.load_w(wgl[:, :, :], w_in[:, 3072:3088], "wgl")
            if True:
                for ti, (t0, n) in enumerate(self.tiles):
                    for c in range(NCH):
                        self.mm(ps[2][:16, :n], wgl[:, c, :], ub[:, c, t0:t0 + n], c == 0, c == NCH - 1, ["wgl", "u%d" % ti], ["ps2"])
                    self.act(glb[:, t0:t0 + n], ps[2][:16, :n], AF.Copy, ["ps2"], ["glb"])
                S.barrier()
            bA = self.sb("gA", [128, NT], F32, ph)
            bB = self.sb("gB", [128, NT], F32, ph)
            qt = self.sb("gqt", [128, NT], BF16, ph)
            kt = self.sb("gkt", [128, NT], BF16, ph)
            kh = self.sb("gkh", [128, L], BF16, ph)
            oh = self.sb("goh", [128, 2, NT], BF16, ph)
            xoh = self.sb("gxo", [128, 2, 512], BF16, ph)
            wq = self.sb("gwq", [128, NCH, 128], BF16, ph)
            wk = self.sb("gwk", [128, NCH, 128], BF16, ph)
            wg = self.sb("gwg", [128, NCH, 256], BF16, ph)
            woh = self.sb("gwo", [128, 2, 1024], BF16, ph)
            wgk2 = self.sb("gwgk2", [16, 128], BF16, ph)
            Sf = self.sb("gSf", [128, 256], F32, ph)
            Sb = self.sb("gSb", [128, 256], BF16, ph)
            attb2 = [self.sb("gatt%d" % b_, [128, 128], BF16, ph) for b_ in range(2)]
            khT2 = [self.sb("gkhT%d" % b_, [128, 128], BF16, ph) for b_ in range(2)]
            ktok_s = self.sb("gktoks", [NS, 128], F32, ph)
            ksel2 = [self.sb("gksel%d" % b_, [NS, 128], F32, ph) for b_ in range(2)]
            qs = self.sb("gqs", [128, NS], F32, ph)
            S0 = [self.sb("gS0_%d" % b, [128, 256], F32, ph) for b in range(2)]
            S1 = [self.sb("gS1_%d" % b, [128, 256], F32, ph) for b in range(2)]
            tmp = self.sb("gtmp", [128, 512], F32, ph)
            tmp2 = self.sb("gtmp2", [128, 512], F32, ph)
            rs = self.rstd
            sqh = self.sq
            for hd in range(4):
                self.load_w(wq[:, :, :], w_in[:, hd * 128:(hd + 1) * 128], "gwq")
                self.load_w(wk[:, :, :], w_in[:, 512 + hd * 128:512 + (hd + 1) * 128], "gwk")
                self.load_w(wg[:, :, :], w_in[:, 2048 + hd * 256:2048 + (hd + 1) * 256], "gwg")
                self.load_w(woh[:, :, :], self.gla_wo[hd * 256:(hd + 1) * 256, :], "gwo")
                S.dma("gpsimd", "gwgk2", wgk2[:, :], self.gla_w_gk2[:, hd * 128:(hd + 1) * 128], writes=["gwgk2"])
                self.load_w(wvh[:, :, :], w_in[:, 1024 + hd * 256:1024 + (hd + 1) * 256], "gwvh")
                for n in range(NCK + 1):
                    pb = ps[n % 2]
                    if n < NCK:
                        lo, cnt = n * CG, CG
                    else:
                        lo, cnt = L, NS
                    for c in range(NCH):
                        self.mm(pb[:cnt, :256], ub[:, c, lo:lo + cnt], wvh[:, c, :], c == 0, c == NCH - 1, ["gwvh"] + allu, ["ps%d" % (n % 2)])
                    if n < NCK:
                        self.act(vtok[:, n, :], pb[:, :256], AF.Copy, ["ps%d" % (n % 2)], ["vtok"])
                    else:
                        self.act(vtok_s[:, :], pb[:NS, :256], AF.Copy, ["ps%d" % (n % 2)], ["vtok_s"])
                for ti, (t0, n) in enumerate(self.tiles):
                    self.mm(ps[0][:, :n], wgk2[:, :], glb[:, t0:t0 + n], True, True, ["gwgk2", "glb"], ["ps0"])
                    self.act(bA[:, t0:t0 + n], ps[0][:, :n], AF.Exp, ["ps0", "negb"], ["gA"], scale=-1.0, bias=negb[:, hd:hd + 1])
                self.act(bA[:, :], bA[:, :], AF.Ln, ["gA", "consts"], ["gA"], bias=self.eps_t[:, 1:2])
                S.op("vector", "tensor_tensor_scan", dict(out=bB[:, 0:L], data0=self.cm128[:, 0:L], data1=bA[:, 0:L], initial=0.0,
                                                          op0=ALU.mult, op1=ALU.add), reads=["gA", "cm"], writes=["gB"])
                self.cp("vector", bB[:, L:NT], bA[:, L:NT], ["gA"], ["gB"])
                self.act(bA[:, :], bB[:, :], AF.Exp, ["gB"], ["gA"], scale=-1.0 / 16)
                self.act(bB[:, :], bB[:, :], AF.Exp, ["gB"], ["gB"], scale=1.0 / 16)
                for ti, (t0, n) in enumerate(self.tiles):
                    for c in range(NCH):
                        self.mm(ps[0][:, :n], wq[:, c, :], ub[:, c, t0:t0 + n], c == 0, c == NCH - 1, ["gwq", "u%d" % ti], ["ps0"])
                    for c in range(NCH):
                        self.mm(ps[1][:, :n], wk[:, c, :], ub[:, c, t0:t0 + n], c == 0, c == NCH - 1, ["gwk", "u%d" % ti], ["ps1"])
                    self.stt(qt[:, t0:t0 + n], ps[0][:, :n], 128.0 ** -0.5, bA[:, t0:t0 + n], ALU.mult, ALU.mult, ["ps0", "gA"], ["gqt"])
                    self.tt("vector", kt[:, t0:t0 + n], ps[1][:, :n], bB[:, t0:t0 + n], ALU.mult, ["ps1", "gB"], ["gkt"])
                    if t0 >= L:
                        self.ts("vector", qs[:, :], ps[0][:, :NS], 128.0 ** -0.5, ALU.mult, ["ps0"], ["gqs"])
                        for c in range(NCH):
                            self.mm(ps[2][:NS, :128], ub[:, c, L:NT], wk[:, c, :], c == 0, c == NCH - 1, ["gwk", "u%d" % ti], ["ps2"])
                        self.act(ktok_s[:, :], ps[2][:NS, :128], AF.Copy, ["ps2"], ["gktoks"])
                for n in range(NCK):
                    last = n * CG + CG - 1
                    self.ts("vector", kh[:, n * CG:(n + 1) * CG], kt[:, n * CG:(n + 1) * CG], bA[:, last:last + 1], ALU.mult, ["gkt", "gA"], ["gkh"])
                self.memset("vector", Sf[:, :], 0.0, ["gSf"])
                self.memset("vector", Sb[:, :], 0.0, ["gSb"])
                for n in range(NCK):
                    sl = slice(n * CG, (n + 1) * CG)
                    pp = n % 2
                    pa, pra = ps[pp], "ps%d" % pp
                    tbo = pp * 128
                    self.mm(pa[:, :128], kt[:, sl], qt[:, sl], True, True, ["gkt", "gqt"], [pra])
                    self.tt("vector", attb2[pp][:, :], pa[:, :128], self.cst[:, self.C_TRIU:self.C_TRIU + 128], ALU.mult, [pra, "cst"], ["gatt%d" % pp])
                    S.op("tensor", "transpose", dict(out=self.psT[:, tbo:tbo + 128], in_=kh[:, sl], identity=self.identb[:, :]),
                         reads=["gkh", "identb"], writes=["psT"])
                    self.act(khT2[pp][:, :], self.psT[:, tbo:tbo + 128], AF.Copy, ["psT"], ["gkhT%d" % pp])
                    for m in range(2):
                        pb = ps[2 + 2 * pp + m]
                        pr = "ps%d" % (2 + 2 * pp + m)
                        self.mm(pb[:, :128], Sb[:, m * 128:(m + 1) * 128], qt[:, sl], True, False, ["gSb", "gqt"], [pr])
                        self.mm(pb[:, :128], vtok[:, n, m * 128:(m + 1) * 128], attb2[pp][:, :], False, True,
                                ["vtok", "gatt%d" % pp], [pr])
                        self.act(oh[:, m, sl], pb[:, :128], AF.Copy, [pr], ["goh"])
                    self.mm(ps[6][:, :256], khT2[pp][:, :], vtok[:, n, :], True, True, ["gkhT%d" % pp, "vtok"], ["ps6"])
                    last = n * CG + CG - 1
                    self.stt(Sf[:, :], Sf[:, :], bA[:, last:last + 1], ps[6][:, :256], ALU.mult, ALU.add, ["gSf", "gA", "ps6"], ["gSf"])
                    if n < NCK - 1:
                        self.cp("vector", Sb[:, :], Sf[:, :], ["gSf"], ["gSb"])
                S.dma("sync", "gSf", self.gla_po[hd], Sf[:, :], reads=["gSf"])
                for r in range(NS):
                    b = r % 2
                    S.dma("sync", "gS0_%d" % b, S0[b][:, :], self.gla_st[r, hd], writes=["gS0_%d" % b])
                    self.ts("vector", ksel2[b][:, :], ktok_s[:, :], self.cst[:NS, self.C_ID + r:self.C_ID + r + 1], ALU.mult, ["gktoks", "cst"], ["gksel%d" % b])
                    zb_, zr_ = ps[5 + b], "ps%d" % (5 + b)
                    self.mm(zb_[:, :256], ksel2[b][:, :], vtok_s[:, :], True, True, ["gksel%d" % b, "vtok_s"], [zr_])
                    self.stt(S1[b][:, :], S0[b][:, :], bA[:, L + r:L + r + 1], zb_[:, :256], ALU.mult, ALU.add,
                             ["gS0_%d" % b, "gA", zr_], ["gS1_%d" % b])
                    S.dma("sync", "gS1_%d" % b, self.gla_so[r, hd], S1[b][:, :], reads=["gS1_%d" % b])
                    ob_, or_ = ps[3 + b], "ps%d" % (3 + b)
                    self.mm(ob_[:, 0:1], S1[b][:, 0:128], qs[:, r:r + 1], True, True, ["gS1_%d" % b, "gqs"], [or_])
                    self.cp("vector", oh[:, 0, L + r:L + r + 1], ob_[:, 0:1], [or_], ["goh"])
                    self.mm(ob_[:, 0:1], S1[b][:, 128:256], qs[:, r:r + 1], True, True, ["gS1_%d" % b, "gqs"], [or_])
                    self.cp("vector", oh[:, 1, L + r:L + r + 1], ob_[:, 0:1], [or_], ["goh"])
                for ti, (t0, n) in enumerate(self.tiles):
                    self.act(sqh[:, 0:2, :n], oh[:, :, t0:t0 + n], AF.Square, ["goh"], ["sq"])
                    for m in range(2):
                        self.mm(ps[6][:, :n], self.ones_bf[:, :], sqh[:, m, :n], m == 0, m == 1, ["sq", "ones"], ["ps6"])
                    self.act(rs[:, :n], ps[6][:, :n], AF.Ln, ["ps6", "consts"], ["rstd"], scale=1.0 / 256, bias=self.eps_t[:, 2:3])
                    self.act(rs[:, :n], rs[:, :n], AF.Exp, ["rstd"], ["rstd"], scale=-0.5)
                    for m in range(2):
                        pb = ps[m]
                        pr = "ps%d" % m
                        for c in range(NCH):
                            self.mm(pb[:, :n], wg[:, c, m * 128:(m + 1) * 128], ub[:, c, t0:t0 + n], c == 0, c == NCH - 1, ["gwg", "u%d" % ti], [pr])
                        self.act(tmp[:, :n], pb[:, :n], AF.Exp, [pr], ["gtmp"], scale=-1.0)
                        self.ts("vector", tmp[:, :n], tmp[:, :n], 1.0, ALU.add, ["gtmp"], ["gtmp"])
                        S.op("vector", "reciprocal", dict(out=tmp[:, :n], in_=tmp[:, :n]), reads=["gtmp"], writes=["gtmp"])
                        self.tt("vector", tmp[:, :n], tmp[:, :n], pb[:, :n], ALU.mult, ["gtmp", pr], ["gtmp"])
                        self.stt(tmp2[:, :n], oh[:, m, t0:t0 + n], self.vec[:, VEC_COLS["gla_norm"] + m:VEC_COLS["gla_norm"] + m + 1], rs[:, :n],
                                 ALU.mult, ALU.mult, ["goh", "vec", "rstd"], ["gtmp2"])
                        self.tt("vector", xoh[:, m, :n], tmp2[:, :n], tmp[:, :n], ALU.mult, ["gtmp", "gtmp2"], ["gxo"])
                    for d in range(NCH):
                        pb = ps[4 + d % 2]
                        pr = "ps%d" % (4 + d % 2)
                        for m in range(2):
                            self.mm(pb[:, :n], woh[:, m, d * 128:(d + 1) * 128], xoh[:, m, :n], m == 0, m == 1, ["gwo", "gxo"], [pr])
                        self.tt("vector", h[:, d, t0:t0 + n], h[:, d, t0:t0 + n], pb[:, :n], ALU.add, [pr, "h%d" % ti], ["h%d" % ti])
            S.barrier()

    def conv_layer(self, i):
        S = self.S
        L, NT = self.L, self.NT
        self.mix_norm_all(i)
        ub = self.ubuf
        with contextlib.ExitStack() as ph:
            xo = self.sb("cxo", [128, NCH, NT], BF16, ph)
            wo = self.sb("cwo", [128, NCH, 1024], BF16, ph)
            ws = [[self.sb("cw%d_%d" % (k, b), [128, NCH, 128], BF16, ph) for k in range(3)] for b in range(2)]
            zcb = self.sb("zcb", [128, 2 + L + 3 * NS], F32, ph)
            gbt = self.sb("gbt", [128, NT], F32, ph)
            cv = self.sb("cv", [128, NT], F32, ph)
            tmp = self.sb("ctmp", [128, 512], F32, ph)
            cst = self.sb("cst", [128, NCH, 2 + 2 * NS], F32, ph)
            self.load_w(wo[:, :, :], self.conv_wo[:, :], "cwo", nsplit=2)
            SB = 2 + L
            zs = zcb[:, SB:SB + 3 * NS].rearrange("p (r k) -> p r k", k=3)
            stv = self.conv_st.rearrange("(c p) r k -> p c r k", p=128)
            self.memset("gpsimd", zcb[:, 0:2], 0.0, ["zcb_halo"])
            def conv_load(cc):
                b = cc % 2
                for k in range(3):
                    self.load_w(ws[b][k][:, :, :], self.conv_w_in[:, k * 1024 + cc * 128:k * 1024 + (cc + 1) * 128], "cw%d_%d" % (k, b))
            conv_load(0)
            for cc in range(NCH):
                b = cc % 2
                if cc + 1 < NCH:
                    conv_load(cc + 1)
                S.dma("sync", "zcb_st", zs[:, :, 0:2], stv[:, cc, :, :], writes=["zcb_st"])
                for ti, (t0, n) in enumerate(self.tiles):
                    pbs = []
                    for k in range(3):
                        pb = self.ps[k]
                        for c in range(NCH):
                            self.mm(pb[:, :n], ws[b][k][:, c, :], ub[:, c, t0:t0 + n], c == 0, c == NCH - 1,
                                    ["cw%d_%d" % (k, b), "u%d" % ti], ["ps%d" % k])
                    self.act(gbt[:, t0:t0 + n], self.ps[0][:, :n], AF.Copy, ["ps0"], ["gbt%d" % ti])
                    self.act(tmp[:, :n], self.ps[1][:, :n], AF.Copy, ["ps1"], ["ctmp"])
                    if t0 < L:
                        dst = zcb[:, 2 + t0:2 + t0 + n]
                    else:
                        dst = zs[:, :, 2]
                    self.tt("vector", dst, tmp[:, :n], self.ps[2][:, :n], ALU.mult, ["ctmp", "ps2"], ["zcb%d" % ti])
                allz = ["zcb%d" % ti for ti in range(len(self.tiles))] + ["zcb_halo", "zcb_st"]
                self.ts("vector", cv[:, 0:L], zcb[:, 2:2 + L], self.vcol("conv_w2", cc), ALU.mult, allz + ["vec"], ["cv"])
                self.stt(cv[:, 0:L], zcb[:, 1:1 + L], self.vcol("conv_w1", cc), cv[:, 0:L], ALU.mult, ALU.add, allz + ["cv", "vec"], ["cv"])
                self.stt(cv[:, 0:L], zcb[:, 0:L], self.vcol("conv_w0", cc), cv[:, 0:L], ALU.mult, ALU.add, allz + ["cv", "vec"], ["cv"])
                self.ts("vector", cv[:, L:NT], zs[:, :, 2], self.vcol("conv_w2", cc), ALU.mult, allz + ["vec", "cv"], ["cv"])
                self.stt(cv[:, L:NT], zs[:, :, 1], self.vcol("conv_w1", cc), cv[:, L:NT], ALU.mult, ALU.add, allz + ["cv", "vec"], ["cv"])
                self.stt(cv[:, L:NT], zs[:, :, 0], self.vcol("conv_w0", cc), cv[:, L:NT], ALU.mult, ALU.add, allz + ["cv", "vec"], ["cv"])
                gres = ["gbt%d" % ti for ti in range(len(self.tiles))]
                self.tt("vector", xo[:, cc, :], gbt[:, :], cv[:, :], ALU.mult, gres + ["cv"], ["cxo%d" % cc])
                self.cp("vector", cst[:, cc, 0:2], zcb[:, 2 + L - 2:2 + L], allz, ["cst"])
                self.cp("vector", cst[:, cc, 2:2 + 2 * NS].rearrange("p (r k) -> p r k", k=2), zs[:, :, 1:3], allz, ["cst"])
            S.dma("sync", "cst", self.conv_o.rearrange("(c p) n -> p c n", p=128), cst[:, :, :], reads=["cst"])
            self.out_proj(xo, wo, "cwo", lambda j, ti: ["cxo%d" % j])
            S.barrier()

    def pool_layer(self, i):
        S = self.S
        L, NT = self.L, self.NT
        h = self.h
        ub = self.ubuf
        NB = 15 + L + 16 * NS
        with contextlib.ExitStack() as ph:
            rall = self.sb("rall", [128, NT], F32, ph)
            pw = self.sb("pw", [128, 4, 2, 256], BF16, ph)
            bufs = [self.sb("pbuf%d" % b, [128, NB], F32, ph) for b in range(3)]
            pst = self.sb("pst", [128, NCH, 15 + 15 * NS], F32, ph)
            S.dma("gpsimd", "pw", pw[:, :, :, :], self.pool_w.rearrange("g (k p) n -> p g k n", p=128), writes=["pw"])
            for ti, (t0, n) in enumerate(self.tiles):
                self.rms_rstd(ti, rall[:, t0:t0 + n], "rall")
            stv = self.pool_st.rearrange("(c p) r k -> p c r k", p=128)
            SB = 15 + L
            for b in range(3):
                self.memset("gpsimd", bufs[b][:, 0:15], 0.0, ["pbuf%d" % b])
            for c in range(NCH):
                gi = c // 2
                w = 2 << gi
                A = bufs[0]
                As = A[:, SB:SB + 16 * NS].rearrange("p (r k) -> p r k", k=16)
                S.dma("sync", "pbuf0", As[:, :, 0:15], stv[:, c, :, :], writes=["pbuf0"])
                allh = ["h%d" % ti for ti in range(len(self.tiles))]
                self.stt(A[:, 15:15 + L], h[:, c, 0:L], self.vcol("norm_mix%d" % i, c), rall[:, 0:L], ALU.mult, ALU.mult,
                         allh + ["rall", "vec"], ["pbuf0"])
                self.stt(As[:, :, 15], h[:, c, L:NT], self.vcol("norm_mix%d" % i, c), rall[:, L:NT], ALU.mult, ALU.mult,
                         allh + ["rall", "vec"], ["pbuf0"])
                self.cp("gpsimd", pst[:, c, 0:15], A[:, L:L + 15], ["pbuf0"], ["pst"])
                self.cp("gpsimd", pst[:, c, 15:15 + 15 * NS].rearrange("p (r k) -> p r k", k=15), As[:, :, 1:16], ["pbuf0"], ["pst"])
                src, si = A, 0
                lo = 0
                for k in range(gi + 1):
                    sh = 1 << k
                    lo += sh
                    di = 1 if si != 1 else 2
                    dst = bufs[di]
                    ss = src[:, SB:SB + 16 * NS].rearrange("p (r k) -> p r k", k=16)
                    ds = dst[:, SB:SB + 16 * NS].rearrange("p (r k) -> p r k", k=16)
                    self.tt("vector", dst[:, lo:15 + L], src[:, lo:15 + L], src[:, lo - sh:15 + L - sh], ALU.add,
                            ["pbuf%d" % si], ["pbuf%d" % di])
                    self.tt("gpsimd", ds[:, :, lo:16], ss[:, :, lo:16], ss[:, :, lo - sh:16 - sh], ALU.add,
                            ["pbuf%d" % si], ["pbuf%d" % di])
                    src, si = dst, di
                ss = src[:, SB:SB + 16 * NS].rearrange("p (r k) -> p r k", k=16)
                self.stt(ub[:, c, 0:L], src[:, 15:15 + L], 1.0 / w, A[:, 15:15 + L], ALU.mult, ALU.subtract,
                         ["pbuf%d" % si, "pbuf0"], ["pd%d" % c])
                self.stt(ub[:, c, L:NT], ss[:, :, 15], 1.0 / w, As[:, :, 15], ALU.mult, ALU.subtract,
                         ["pbuf%d" % si, "pbuf0"], ["pd%d" % c])
                nf = w - 1
                ic = VEC_COLS["invc"]
                ftmp = self.rstd
                self.tt("vector", ftmp[:, 0:nf], src[:, 15:15 + nf], self.vec[:, ic:ic + nf], ALU.mult, ["pbuf%d" % si, "vec"], ["rstd"])
                self.tt("vector", ub[:, c, 0:nf], ftmp[:, 0:nf], A[:, 15:15 + nf], ALU.subtract, ["rstd", "pbuf0"], ["pd%d" % c])
            S.dma("sync", "pst", self.pool_o.rearrange("(c p) n -> p c n", p=128), pst[:, :, :], reads=["pst"])
            for ti, (t0, n) in enumerate(self.tiles):
                for gi in range(4):
                    for m in range(2):
                        d = 2 * gi + m
                        pb = self.ps[4 + d % 2]
                        pr = "ps%d" % (4 + d % 2)
                        for k in range(2):
                            self.mm(pb[:, :n], pw[:, gi, k, m * 128:(m + 1) * 128], ub[:, 2 * gi + k, t0:t0 + n], k == 0, k == 1,
                                    ["pw", "pd%d" % (2 * gi + k)], [pr])
                        self.stt(h[:, d, t0:t0 + n], pb[:, :n], self.vcol("pool_scale", d), h[:, d, t0:t0 + n], ALU.mult, ALU.add,
                                 [pr, "h%d" % ti, "vec"], ["h%d" % ti])
            S.barrier()

    def build(self):
        L, NT = self.L, self.NT
        nc = bass.Bass("TRN2", target_bir_lowering=False)
        self.nc = nc
        self.xT = self.dram_in("xT", [D, NT])
        self.vec_d = self.dram_in("vec", [128, NVEC])
        self.ffn_up = self.dram_in("ffn_up", [4, D, DFF])
        self.ffn_down = self.dram_in("ffn_down", [4, DFF, D])
        self.yT = self.dram_out("yT", [D, NT])
        self.conv_w_in = self.dram_in("conv_w_in", [D, 3 * D])
        self.conv_wo = self.dram_in("conv_wo", [D, D])
        self.conv_st = self.dram_in("conv_st", [D, NS, 2])
        self.conv_o = self.dram_out("conv_o", [D, 2 + 2 * NS])
        self.pool_w = self.dram_in("pool_w", [4, 256, 256])
        self.pool_st = self.dram_in("pool_st", [D, NS, 15])
        self.pool_o = self.dram_out("pool_o", [D, 15 + 15 * NS])
        self.cst_d = self.dram_in("cst", [128, self.NCST])
        self.gla_w_in = self.dram_in("gla_w_in", [D, 3088])
        self.gla_w_gk2 = self.dram_in("gla_w_gk2", [16, 512])
        self.gla_wo = self.dram_in("gla_wo", [D, D])
        self.gla_st = self.dram_in("gla_st", [NS, 4, 128, 256])
        self.gla_po = self.dram_out("gla_po", [4, 128, 256])
        self.gla_so = self.dram_out("gla_so", [NS, 4, 128, 256])
        self.rwkv_w_rkv = [self.dram_in("rwkv_w_%s" % k, [D, D]) for k in "rkv"]
        self.rwkv_w1 = self.dram_in("rwkv_w1", [D, 64])
        self.rwkv_w2 = self.dram_in("rwkv_w2", [64, D])
        self.rwkv_a1 = self.dram_in("rwkv_a1", [D, 64])
        self.rwkv_a2 = self.dram_in("rwkv_a2", [64, D])
        self.rwkv_g1 = self.dram_in("rwkv_g1", [D, 160])
        self.rwkv_g2 = self.dram_in("rwkv_g2", [160, D])
        self.rwkv_wo = self.dram_in("rwkv_wo", [D, D])
        self.shift_st = self.dram_in("shift_st", [D, NS])
        self.wkv_st = self.dram_in("wkv_st", [NS, 8, 128, 64])
        self.shift_o = self.dram_out("shift_o", [D, 1 + NS])
        self.wkv_po = self.dram_out("wkv_po", [8, 128, 64])
        self.wkv_so = self.dram_out("wkv_so", [NS, 8, 128, 64])
        with contextlib.ExitStack() as st:
            self.st = st
            S = Sched(nc, st)
            self.S = S
            self.h = self.sb("h", [128, NCH, NT], F32)
            self.ubuf = self.sb("ubuf", [128, NCH, NT + 1 + NS], BF16)
            self.vec = self.sb("vecs", [128, NVEC], F32)
            self.ones_bf = self.sb("ones_bf", [128, 128], BF16)
            self.eps_t = self.sb("eps_t", [128, 4], F32)
            self.sq = self.sb("sq", [128, NCH, 512], BF16)
            self.rstd = self.sb("rstd", [128, 512], F32)
            self.ps = [st.enter_context(nc.psum_tensor("psb%d" % i, [128, 512], F32)) for i in range(7)]
            self.psT = st.enter_context(nc.psum_tensor("psT", [128, 1024], BF16))
            self.cst = self.sb("cst", [128, self.NCST], F32)
            self.identb = self.sb("identb", [128, 128], BF16)

            self.memset("vector", self.ones_bf[:], 1.0, ["ones"])
            self.memset("vector", self.eps_t[:, 0:1], NORM_EPS, ["consts"])
            self.memset("vector", self.eps_t[:, 1:2], 1.0, ["consts"])
            self.memset("vector", self.eps_t[:, 2:3], 1e-5, ["consts"])
            self.memset("vector", self.eps_t[:, 3:4], 64e-5, ["consts"])
            S.dma("sync", "cst", self.cst[:], self.cst_d, writes=["cst"])
            self.cp("vector", self.identb[:, :], self.cst[:, self.C_ID:self.C_ID + 128], ["cst"], ["identb"])

            S.dma("sync", "vec", self.vec[:], self.vec_d, writes=["vec"])
            xv = self.xT.rearrange("(c p) t -> p c t", p=128)
            for ti, (t0, n) in enumerate(self.tiles):
                S.dma("sync", "h%d" % ti, self.h[:, :, t0:t0 + n], xv[:, :, t0:t0 + n], writes=["h%d" % ti])
            S.barrier()
            for i in range(4):
                if self.layers[i]:
                    getattr(self, ["rwkv_layer", "gla_layer", "conv_layer", "pool_layer"][i])(i)
                if self.ffn:
                    self.ffn_layer(i)
            yv = self.yT.rearrange("(c p) t -> p c t", p=128)
            with contextlib.ExitStack() as ph:
                yo = [self.sb("yo%d" % b, [128, NCH, 512], F32, ph) for b in range(2)]
                for ti, (t0, n) in enumerate(self.tiles):
                    b = ti % 2
                    self.norm_tile(ti, "norm_final", lambda c: yo[b][:, c, :n], "yo%d" % b)
                    S.dma("sync", "yo%d" % b, yv[:, :, t0:t0 + n], yo[b][:, :, :n], reads=["yo%d" % b])
                S.barrier()
            with nc.Block() as block:
                S.replay(block)
        return nc


def pack_vec(inp):
    v = np.zeros((128, NVEC), np.float32)

    def put(name, arr):
        a = np.asarray(arr, np.float32).reshape(-1)
        k = a.size // 128
        c0 = VEC_COLS[name]
        v[:, c0:c0 + k] = a.reshape(k, 128).T
    for i in range(4):
        put("norm_mix%d" % i, inp["norm_mix"][i])
        put("norm_ffn%d" % i, inp["norm_ffn"][i])
    put("norm_final", inp["norm_final"])
    for j in range(6):
        put("mu%d" % j, inp["rwkv_mu"][0, j])
    put("w0", inp["rwkv_w0"][0]); put("a0", inp["rwkv_a0"][0]); put("k_k", inp["rwkv_k_k"][0]); put("k_a", inp["rwkv_k_a"][0])
    put("r_k", inp["rwkv_r_k"][0]); put("ln_w", inp["rwkv_ln_w"][0]); put("ln_b", inp["rwkv_ln_b"][0])
    for j in range(3):
        put("conv_w%d" % j, inp["conv_w"][0, j])
    put("pool_scale", inp["pool_scale"][0])
    put("gla_b_gk", inp["gla_b_gk"][0]); put("gla_norm", inp["gla_norm"][0])
    v[:, VEC_COLS["invc"]:VEC_COLS["invc"] + 16] = (1.0 / np.arange(1, 17, dtype=np.float64)).astype(np.float32)[None, :]
    return v


def make_in_maps(inp, L, ncores=8):
    vec = pack_vec(inp)
    maps = []
    for core in range(ncores):
        xp = np.asarray(inp["x_prompt"][core, :L], np.float32)
        xs = np.asarray(inp["x_sample"][core * NS:(core + 1) * NS, 0], np.float32)
        xT = np.ascontiguousarray(np.concatenate([xp, xs], axis=0).T)
        rs = slice(core * NS, (core + 1) * NS)
        m = {"xT": xT, "vec": vec,
             "ffn_up": np.asarray(inp["ffn_up"], np.float32), "ffn_down": np.asarray(inp["ffn_down"], np.float32),
             "conv_w_in": np.asarray(inp["conv_w_in"][0], np.float32), "conv_wo": np.asarray(inp["conv_wo"][0], np.float32),
             "conv_st": np.ascontiguousarray(np.transpose(inp["state_conv"][0, rs], (2, 0, 1))),
             "pool_w": np.asarray(inp["pool_w"][0], np.float32), "cst": make_cst(),
             "gla_w_in": np.asarray(inp["gla_w_in"][0], np.float32), "gla_w_gk2": np.asarray(inp["gla_w_gk2"][0], np.float32),
             "gla_wo": np.asarray(inp["gla_wo"][0], np.float32), "gla_st": np.ascontiguousarray(inp["state_gla"][0, rs]),
             "rwkv_w_r": np.asarray(inp["rwkv_w_rkv"][0, 0], np.float32), "rwkv_w_k": np.asarray(inp["rwkv_w_rkv"][0, 1], np.float32),
             "rwkv_w_v": np.asarray(inp["rwkv_w_rkv"][0, 2], np.float32),
             "rwkv_w1": np.asarray(inp["rwkv_w1"][0], np.float32), "rwkv_w2": np.asarray(inp["rwkv_w2"][0], np.float32),
             "rwkv_a1": np.asarray(inp["rwkv_a1"][0], np.float32), "rwkv_a2": np.asarray(inp["rwkv_a2"][0], np.float32),
             "rwkv_g1": np.asarray(inp["rwkv_g1"][0], np.float32), "rwkv_g2": np.asarray(inp["rwkv_g2"][0], np.float32),
             "rwkv_wo": np.asarray(inp["rwkv_wo"][0], np.float32),
             "shift_st": np.ascontiguousarray(np.asarray(inp["state_rwkv_shift"][0, rs], np.float32).T),
             "wkv_st": np.ascontiguousarray(np.transpose(inp["state_rwkv_wkv"][0, rs], (0, 1, 3, 2)).reshape(NS, 8, 128, 64)),
             "pool_st": np.ascontiguousarray(np.transpose(inp["state_pool"][0, rs], (2, 0, 1)))}
        maps.append(m)
    return maps


_CACHE = {}


def run(inp, L, ncores=8, **kw):
    key = (L, tuple(sorted(kw.items())))
    if key not in _CACHE:
        _CACHE[key] = Builder(L, **kw).build()
    nc = _CACHE[key]
    maps = make_in_maps(inp, L, ncores)
    res = run_bass_kernel_spmd(nc, maps, core_ids=list(range(ncores)))
    return res.results


def gather(res, L):
    outs = {}
    outs["y"] = (np.stack([r["yT"][:, :L].T for r in res], 0),
                 np.concatenate([r["yT"][:, L:].T for r in res], 0)[:, None, :])
    outs["conv"] = (np.stack([r["conv_o"][:, 0:2].T for r in res], 0)[None],
                    np.concatenate([np.transpose(r["conv_o"][:, 2:].reshape(D, NS, 2), (1, 2, 0)) for r in res], 0)[None])
    outs["wkv"] = (np.stack([np.transpose(r["wkv_po"].reshape(16, 64, 64), (0, 2, 1)) for r in res], 0)[None],
                   np.concatenate([np.transpose(r["wkv_so"].reshape(NS, 16, 64, 64), (0, 1, 3, 2)) for r in res], 0)[None])
    outs["shift"] = (np.stack([r["shift_o"][:, 0] for r in res], 0)[None],
                     np.concatenate([r["shift_o"][:, 1:].T for r in res], 0)[None])
    outs["gla"] = (np.stack([r["gla_po"] for r in res], 0)[None], np.concatenate([r["gla_so"] for r in res], 0)[None])
    outs["pool"] = (np.stack([r["pool_o"][:, 0:15].T for r in res], 0)[None],
                    np.concatenate([np.transpose(r["pool_o"][:, 15:].reshape(D, NS, 15), (1, 2, 0)) for r in res], 0)[None])
    return outs


def kernel(**inputs):
    L = 2048
    res = run(inputs, L)
    o = gather(res, L)
    f = lambda a: np.ascontiguousarray(a, dtype=np.float32)
    return (f(o["y"][0]), f(o["y"][1]), f(o["wkv"][0]), f(o["wkv"][1]), f(o["shift"][0]), f(o["shift"][1]),
            f(o["gla"][0]), f(o["gla"][1]), f(o["conv"][0]), f(o["conv"][1]), f(o["pool"][0]), f(o["pool"][1]))
```

```python
import numpy as np
import contextlib
import concourse.bass as bass
import concourse.mybir as mybir
from concourse.bass_utils import run_bass_kernel_spmd

F32 = mybir.dt.float32
BF16 = mybir.dt.bfloat16
AF = mybir.ActivationFunctionType
ALU = mybir.AluOpType
AX = mybir.AxisListType

D = 1024
NCH = 8
DFF = 4096
NS = 16
NORM_EPS = 1e-6


import os as _os_mod
INLINE_WAIT = _os_mod.environ.get("SCHED_INLINE", "1") == "1"


class Sched:
    ENGS = ["tensor", "vector", "scalar", "gpsimd", "sync"]

    def __init__(self, nc, stack):
        self.nc = nc
        self.stack = stack
        self.ops = {e: [] for e in self.ENGS}
        self.semobj = {e: stack.enter_context(nc.semaphore("s_" + e)) for e in self.ENGS}
        self.cnt = {e: 0 for e in self.ENGS}
        self.waited = {e: {} for e in self.ENGS}
        self.W = {}
        self.R = {}
        self.dcnt = {}
        self.dkeys = {}

    def dma_sem(self, name):
        if name in self.dkeys:
            return self.dkeys[name]
        key = "d%d" % len(self.dkeys)
        self.semobj[key] = self.stack.enter_context(self.nc.semaphore(key))
        self.dcnt[key] = 0
        self.dkeys[name] = key
        return key

    def _deps(self, eng, reads, writes, acc=False):
        need = {}

        def add(k, v):
            if need.get(k, 0) < v:
                need[k] = v
        for r in reads:
            if r in self.W:
                add(*self.W[r])
        for w in writes:
            if w in self.W and not (acc and self.W[w][0] == eng):
                add(*self.W[w])
            for k, v in self.R.get(w, {}).items():
                add(k, v)
        out = []
        for k, v in need.items():
            if self.waited[eng].get(k, 0) >= v:
                continue
            self.waited[eng][k] = v
            out.append((k, v))
        return out

    def op(self, eng, meth, kw, reads=(), writes=(), acc=False):
        waits = self._deps(eng, reads, writes, acc)
        self.cnt[eng] += 1
        v = self.cnt[eng]
        self.ops[eng].append((waits, (meth, kw), (eng, 1)))
        for r in reads:
            self.R.setdefault(r, {})[eng] = v
        for w in writes:
            self.W[w] = (eng, v)
            self.R[w] = {}

    def dma(self, eng, semname, out, in_, reads=(), writes=()):
        semkey = self.dma_sem(semname)
        waits = self._deps(eng, reads, writes)
        self.dcnt[semkey] += 16
        v = self.dcnt[semkey]
        self.ops[eng].append((waits, ("dma_start", dict(out=out, in_=in_)), (semkey, 16)))
        for r in reads:
            self.R.setdefault(r, {})[semkey] = v
        for w in writes:
            self.W[w] = (semkey, v)
            self.R[w] = {}

    def barrier(self):
        tot = {}
        for e in self.ENGS:
            if self.cnt[e]:
                tot[e] = self.cnt[e]
        for k, v in self.dcnt.items():
            if v:
                tot[k] = v
        for e in self.ENGS:
            waits = []
            for k, v in tot.items():
                if self.waited[e].get(k, 0) >= v:
                    continue
                self.waited[e][k] = v
                waits.append((k, v))
            if waits:
                self.ops[e].append((waits, None, None))

    def replay(self, block):
        for e in self.ENGS:
            ops = self.ops[e]
            if not ops:
                continue

            def body(engine, ops=ops):
                for waits, fn, inc in ops:
                    inl = None
                    NIN = int(_os_mod.environ.get("SCHED_NINLINE", "1"))
                    if INLINE_WAIT and fn is not None and waits and fn[0] != "dma_start":
                        inl = waits[-NIN:]
                        waits = waits[:-NIN]
                    for k, v in waits:
                        engine.wait_ge(self.semobj[k], v)
                    if fn is not None:
                        try:
                            ins = getattr(engine, fn[0])(**fn[1])
                        except Exception:
                            print("FAILED OP:", fn[0], {k: (getattr(v, "tensor", None) and v.tensor.name, getattr(v, "shape", v)) for k, v in fn[1].items()})
                            raise
                        if inl is not None:
                            for k_, v_ in inl:
                                ins._wait_ge(self.semobj[k_], v_)
                        ins.then_inc(self.semobj[inc[0]], inc[1])
            getattr(block, e)(body)


VEC_COLS = {}


def _vec_layout():
    names = []
    for i in range(4):
        names.append("norm_mix%d" % i)
    for i in range(4):
        names.append("norm_ffn%d" % i)
    names.append("norm_final")
    for j in range(6):
        names.append("mu%d" % j)
    for j in range(6):
        names.append("omu%d" % j)
    names += ["w0", "a0", "k_k", "k_a", "omk_a", "r_k", "ln_w", "ln_b",
              "conv_w0", "conv_w1", "conv_w2", "pool_scale"]
    col = 0
    for n in names:
        VEC_COLS[n] = col
        col += 8
    VEC_COLS["gla_b_gk"] = col
    col += 4
    VEC_COLS["gla_norm"] = col
    col += 2
    VEC_COLS["invc"] = col
    col += 16
    return col


NVEC = _vec_layout()


def make_cst():
    c = np.zeros((128, Builder.NCST), np.float32)
    c[:, Builder.C_ID:Builder.C_ID + 128] = np.eye(128, dtype=np.float32)
    c[:, Builder.C_TRIU:Builder.C_TRIU + 128] = np.triu(np.ones((128, 128), np.float32))
    su = np.triu(np.ones((64, 64), np.float32), 1)
    iu = np.triu(np.ones((64, 64), np.float32), 0)
    z = np.zeros((64, 64), np.float32)
    c[:, Builder.C_SUIU:Builder.C_SUIU + 128] = np.block([[su, z], [z, su]])
    c[:, Builder.C_SUIU + 128:Builder.C_SUIU + 256] = np.block([[iu, z], [z, iu]])
    c[:, Builder.C_SL:Builder.C_SL + 128] = np.block([[su.T, z], [z, su.T]])
    c[:, Builder.C_I2:Builder.C_I2 + 64] = np.concatenate([np.eye(64, dtype=np.float32)] * 2, 0)
    return c


class Builder:
    C_ID = 0
    C_TRIU = 128
    C_SUIU = 256
    C_SL = 512
    C_I2 = 640
    NCST = 704

    def __init__(self, L, layers=(1, 1, 1, 1), ffn=True, dbg=None):
        self.L = L
        self.NT = L + NS
        self.layers = layers
        self.ffn = ffn
        self.dbg = dbg
        tiles = []
        t = 0
        while t < L:
            n = min(512, L - t)
            tiles.append((t, n))
            t += n
        tiles.append((L, NS))
        self.tiles = tiles

    def sb(self, name, shape, dt, stack=None):
        self._uid = getattr(self, "_uid", 0) + 1
        return (stack or self.st).enter_context(self.nc.sbuf_tensor("%s_%d" % (name, self._uid), list(shape), dt))

    def dram_in(self, name, shape):
        return self.nc.dram_tensor(name, list(shape), F32, kind="ExternalInput").ap()

    def dram_out(self, name, shape):
        return self.nc.dram_tensor(name, list(shape), F32, kind="ExternalOutput").ap()

    def vcol(self, name, c=0):
        k = VEC_COLS[name] + c
        return self.vec[:, k:k + 1]

    def mm(self, out, lhsT, rhs, start, stop, reads, writes, nosw=False):
        self.S.op("tensor", "matmul", dict(out=out, lhsT=lhsT, rhs=rhs, start=start, stop=stop),
                  reads=reads, writes=writes, acc=(not start) or nosw)

    def act(self, out, in_, func, reads, writes, scale=1.0, bias=None, eng="scalar"):
        kw = dict(out=out, in_=in_, func=func, scale=scale)
        if bias is not None:
            kw["bias"] = bias
        self.S.op("scalar", "activation", kw, reads=reads, writes=writes)

    def tt(self, eng, out, in0, in1, op, reads, writes):
        self.S.op(eng, "tensor_tensor", dict(out=out, in0=in0, in1=in1, op=op), reads=reads, writes=writes)

    def ts(self, eng, out, in0, s1, op0, reads, writes, s2=None, op1=None):
        kw = dict(out=out, in0=in0, scalar1=s1, scalar2=s2, op0=op0)
        if op1 is not None:
            kw["op1"] = op1
        self.S.op(eng, "tensor_scalar", kw, reads=reads, writes=writes)

    def stt(self, out, in0, scalar, in1, op0, op1, reads, writes):
        self.S.op("vector", "scalar_tensor_tensor", dict(out=out, in0=in0, scalar=scalar, in1=in1, op0=op0, op1=op1),
                  reads=reads, writes=writes)

    def cp(self, eng, out, in_, reads, writes):
        self.S.op(eng, "tensor_copy", dict(out=out, in_=in_), reads=reads, writes=writes)

    def memset(self, eng, ap, val, writes):
        self.S.op(eng, "memset", dict(ap=ap, constant=val), writes=writes)

    def rms_rstd(self, ti, rstd, rstd_res):
        t0, n = self.tiles[ti]
        sq, pb = self.sq, self.ps[6]
        h = self.h
        self.act(sq[:, :, :n], h[:, :, t0:t0 + n], AF.Square, ["h%d" % ti], ["sq"])
        for c in range(NCH):
            self.mm(pb[:, :n], self.ones_bf[:, :], sq[:, c, :n], c == 0, c == NCH - 1, ["sq", "ones"], ["ps6"])
        self.act(rstd[:, :n], pb[:, :n], AF.Ln, ["ps6", "consts"], [rstd_res], scale=1.0 / D, bias=self.eps_t[:, 0:1])
        self.act(rstd[:, :n], rstd[:, :n], AF.Exp, [rstd_res], [rstd_res], scale=-0.5)

    def norm_tile(self, ti, gname, dst_fn, dst_res):
        t0, n = self.tiles[ti]
        rstd = self.rstd
        self.rms_rstd(ti, rstd, "rstd")
        h = self.h
        for c in range(NCH):
            self.stt(dst_fn(c), h[:, c, t0:t0 + n], self.vcol(gname, c), rstd[:, :n], ALU.mult, ALU.mult,
                     ["h%d" % ti, "rstd", "vec"], [dst_res])

    def load_w(self, dst, src, res, nsplit=1):
        k = dst.shape[1]
        srcv = src.rearrange("(k p) n -> p k n", p=128)
        step = max(1, k // nsplit)
        for a in range(0, k, step):
            b = min(k, a + step)
            self.S.dma("gpsimd", res, dst[:, a:b, :], srcv[:, a:b, :], writes=[res])

    def ffn_layer(self, i):
        saved_tiles = self.tiles
        nt_ = -(-self.NT // 512)
        base_, rem_ = divmod(self.NT, nt_)
        ft, t_ = [], 0
        for k_ in range(nt_):
            sz = base_ + (1 if k_ < rem_ else 0)
            ft.append((t_, sz))
            t_ += sz
        self.tiles = ft
        try:
            self._ffn_layer(i)
        finally:
            self.tiles = saved_tiles

    def _ffn_layer(self, i):
        S = self.S
        with contextlib.ExitStack() as ph:
            wup = [self.sb("wup%d" % b, [128, NCH, 1024], BF16, ph) for b in range(2)]
            wdn = [self.sb("wdn%d" % b, [128, NCH, 1024], BF16, ph) for b in range(2)]
            hid = self.sb("hid", [128, NCH, 512], BF16, ph)
            rtmp = [self.sb("rtmp%d" % b, [128, 512], BF16, ph) for b in range(2)]
            ub = self.ubuf
            h = self.h
            def ffn_load(q):
                b = q % 2
                self.load_w(wup[b][:, :, :], self.ffn_up[i, :, q * 1024:(q + 1) * 1024], "wup%d" % b, nsplit=2)
                self.load_w(wdn[b][:, :, :], self.ffn_down[i, q * 1024:(q + 1) * 1024, :], "wdn%d" % b, nsplit=2)
            ffn_load(0)
            for q in range(4):
                b = q % 2
                if q + 1 < 4:
                    ffn_load(q + 1)
                for ti, (t0, n) in enumerate(self.tiles):
                    if q == 0:
                        self.norm_tile(ti, "norm_ffn%d" % i, lambda c: ub[:, c, t0:t0 + n], "u%d" % ti)
                    for j in range(NCH):
                        pb = self.ps[j % 4]
                        pr = "ps%d" % (j % 4)
                        for c in range(NCH):
                            self.mm(pb[:, :n], wup[b][:, c, j * 128:(j + 1) * 128], ub[:, c, t0:t0 + n], c == 0, c == NCH - 1,
                                    ["wup%d" % b, "u%d" % ti], [pr])
                        rt = rtmp[j % 2]
                        self.act(rt[:, :n], pb[:, :n], AF.Relu, [pr], ["rtmp%d" % (j % 2)])
                        self.tt("vector", hid[:, j, :n], rt[:, :n], rt[:, :n], ALU.mult, ["rtmp%d" % (j % 2)], ["hid%d" % j])
                    for d in range(NCH):
                        pb = self.ps[4 + d % 2]
                        pr = "ps%d" % (4 + d % 2)
                        for j in range(NCH):
                            self.mm(pb[:, :n], wdn[b][:, j, d * 128:(d + 1) * 128], hid[:, j, :n], j == 0, j == NCH - 1,
                                    ["wdn%d" % b, "hid%d" % j], [pr])
                        self.tt("vector", h[:, d, t0:t0 + n], h[:, d, t0:t0 + n], pb[:, :n], ALU.add, [pr, "h%d" % ti], ["h%d" % ti])
            S.barrier()

    def mix_norm_all(self, i, col0=0):
        ub = self.ubuf
        for ti, (t0, n) in enumerate(self.tiles):
            self.norm_tile(ti, "norm_mix%d" % i, lambda c: ub[:, c, col0 + t0:col0 + t0 + n], "u%d" % ti)

    def out_proj(self, xo, wo, wres, xres_fn):
        h = self.h
        for ti, (t0, n) in enumerate(self.tiles):
            for d in range(NCH):
                pb = self.ps[4 + d % 2]
                pr = "ps%d" % (4 + d % 2)
                for j in range(NCH):
                    self.mm(pb[:, :n], wo[:, j, d * 128:(d + 1) * 128], xo[:, j, t0:t0 + n], j == 0, j == NCH - 1,
                            [wres] + xres_fn(j, ti), [pr])
                self.tt("vector", h[:, d, t0:t0 + n], h[:, d, t0:t0 + n], pb[:, :n], ALU.add, [pr, "h%d" % ti], ["h%d" % ti])

    def rwkv_layer(self, i):
        S = self.S
        L, NT = self.L, self.NT
        h, ub, ps, vec = self.h, self.ubuf, self.ps, self.vec
        C = 64
        ntl = len(self.tiles)
        cst = self.cst
        V = VEC_COLS
        self.memset("vector", ub[:, :, 0:1], 0.0, ["ushift"])
        S.dma("gpsimd", "ushift", ub[:, :, NT + 1:NT + 1 + NS], self.shift_st.rearrange("(c p) r -> p c r", p=128), writes=["ushift"])
        with contextlib.ExitStack() as ph:
            sho = self.sb("sho", [128, NCH, 1 + NS], F32, ph)
            for ti, (t0, n) in enumerate(self.tiles):
                self.norm_tile(ti, "norm_mix%d" % i, lambda c: ub[:, c, 1 + t0:1 + t0 + n], "u%d" % ti)
                if t0 + n == L:
                    for c in range(NCH):
                        self.stt(sho[:, c, 0:1], h[:, c, L - 1:L], self.vcol("norm_mix%d" % i, c), self.rstd[:, n - 1:n], ALU.mult, ALU.mult,
                                 ["h%d" % ti, "rstd", "vec"], ["sho"])
                if t0 >= L:
                    for c in range(NCH):
                        self.stt(sho[:, c, 1:1 + NS], h[:, c, L:NT], self.vcol("norm_mix%d" % i, c), self.rstd[:, :NS], ALU.mult, ALU.mult,
                                 ["h%d" % ti, "rstd", "vec"], ["sho"])
            S.dma("sync", "sho", self.shift_o.rearrange("(c p) n -> p c n", p=128), sho[:, :, :], reads=["sho"])

            def u_ap(c, ti):
                t0, n = self.tiles[ti]
                return ub[:, c, 1 + t0:1 + t0 + n]

            def p_ap(c, ti):
                t0, n = self.tiles[ti]
                if t0 < L:
                    return ub[:, c, t0:t0 + n]
                return ub[:, c, NT + 1:NT + 1 + NS]

            def u_res(ti):
                return ["u%d" % ti, "ushift"] + (["u%d" % (ti - 1)] if ti > 0 else [])

            xv = self.sb("rxv", [128, 64], F32, ph)
            self.ts("vector", xv[:, 0:8], vec[:, V["w0"]:V["w0"] + 8], -1.0, ALU.mult, ["vec"], ["rxv"])
            self.ts("vector", xv[:, 8:16], vec[:, V["a0"]:V["a0"] + 8], -1.0, ALU.mult, ["vec"], ["rxv"])
            self.memset("vector", xv[:, 16:17], 1e-24, ["rxv"])
            for j in range(6):
                self.ts("vector", vec[:, V["omu%d" % j]:V["omu%d" % j] + 8], vec[:, V["mu%d" % j]:V["mu%d" % j] + 8], -1.0, ALU.mult,
                        ["vec"], ["vec"], s2=1.0, op1=ALU.add)
            self.ts("vector", vec[:, V["omk_a"]:V["omk_a"] + 8], vec[:, V["k_a"]:V["k_a"] + 8], -1.0, ALU.mult, ["vec"], ["vec"], s2=1.0, op1=ALU.add)
            onesblk = self.sb("onesblk", [128, 128], BF16, ph)
            self.memset("vector", onesblk[:, :], 0.0, ["onesblk"])
            self.memset("vector", onesblk[0:64, 0:64], 1.0, ["onesblk"])
            self.memset("vector", onesblk[64:128, 64:128], 1.0, ["onesblk"])
            i2b = self.sb("i2b", [128, 64], BF16, ph)
            self.cp("vector", i2b[:, :], cst[:, self.C_I2:self.C_I2 + 64], ["cst"], ["i2b"])
            cm64 = self.sb("cm64", [128, 512], F32, ph)
            self.memset("gpsimd", cm64[:, :], 1.0, ["cm64"])
            self.memset("gpsimd", cm64[:, :].rearrange("p (n k) -> p n k", k=64)[:, :, 0:1], 0.0, ["cm64"])

            stage = self.sb("rstage", [128, NCH, 160], F32, ph)
            self._eng_rr = 0

            def prep_w(src, m, muj, dA, dB, res):
                S.dma("sync", "rstage", stage[:, :, :m], src.rearrange("(c p) n -> p c n", p=128), writes=["rstage"])
                for c in range(NCH):
                    e = "vector"
                    self.ts(e, dA[:, c, :], stage[:, c, :m], vec[:, V["omu%d" % muj] + c:V["omu%d" % muj] + c + 1], ALU.mult, ["rstage", "vec"], [res])
                    self.ts(e, dB[:, c, :], stage[:, c, :m], vec[:, V["mu%d" % muj] + c:V["mu%d" % muj] + c + 1], ALU.mult, ["rstage", "vec"], [res])

            tw = self.sb("rtw", [64, NT], BF16, ph)
            ta = self.sb("rta", [64, NT], BF16, ph)
            tg0 = self.sb("rtg0", [128, NT], BF16, ph)
            tg1 = self.sb("rtg1", [32, NT], BF16, ph)
            tmpA = self.rstd
            with contextlib.ExitStack() as ph2:
                l1 = [self.sb("rl1_%d" % k, [128, NCH, m], BF16, ph2) for k, m in enumerate([64, 64, 64, 64, 160, 160])]
                prep_w(self.rwkv_w1, 64, 1, l1[0], l1[1], "rl1w")
                prep_w(self.rwkv_a1, 64, 4, l1[2], l1[3], "rl1a")
                prep_w(self.rwkv_g1, 160, 5, l1[4], l1[5], "rl1g")
                for ti, (t0, n) in enumerate(self.tiles):
                    specs = [(l1[0][:, :, :], l1[1][:, :, :], 64, "rl1w"), (l1[2][:, :, :], l1[3][:, :, :], 64, "rl1a"),
                             (l1[4][:, :, 0:128], l1[5][:, :, 0:128], 128, "rl1g"), (l1[4][:, :, 128:160], l1[5][:, :, 128:160], 32, "rl1g")]
                    for k, (wa, wb, m, res) in enumerate(specs):
                        for c in range(NCH):
                            self.mm(ps[k][:m, :n], wa[:, c, :], u_ap(c, ti), c == 0, False, [res] + u_res(ti), ["ps%d" % k])
                        for c in range(NCH):
                            self.mm(ps[k][:m, :n], wb[:, c, :], p_ap(c, ti), False, c == NCH - 1, [res] + u_res(ti), ["ps%d" % k])
                    self.act(tmpA[:64, :n], ps[0][:64, :n], AF.Exp, ["ps0"], ["rstd"], scale=-2.0)
                    self.ts("vector", tmpA[:64, :n], tmpA[:64, :n], 1.0, ALU.add, ["rstd"], ["rstd"])
                    S.op("vector", "reciprocal", dict(out=tmpA[:64, :n], in_=tmpA[:64, :n]), reads=["rstd"], writes=["rstd"])
                    self.ts("vector", tw[:, t0:t0 + n], tmpA[:64, :n], 2.0, ALU.mult, ["rstd"], ["rtw"], s2=-1.0, op1=ALU.add)
                    self.act(ta[:, t0:t0 + n], ps[1][:64, :n], AF.Copy, ["ps1"], ["rta"])
                    for k, (dst, m) in ((2, (tg0, 128)), (3, (tg1, 32))):
                        self.act(tmpA[:m, :n], ps[k][:m, :n], AF.Exp, ["ps%d" % k], ["rstd"], scale=-1.0)
                        self.ts("vector", tmpA[:m, :n], tmpA[:m, :n], 1.0, ALU.add, ["rstd"], ["rstd"])
                        S.op("vector", "reciprocal", dict(out=tmpA[:m, :n], in_=tmpA[:m, :n]), reads=["rstd"], writes=["rstd"])
                        self.cp("vector", dst[:, t0:t0 + n], tmpA[:m, :n], ["rstd"], ["rtg%d" % (k - 2)])
                S.barrier()

            F = [self.sb("rF%d" % k, [128, 512], F32, ph) for k in range(12)]
            r_f, k_f, v_f, lw, a_f, g_f, kkn, k2, bon, cl, E1, E3 = F
            E2 = k_f
            BB = [bon, self.rstd]
            GF = [self.sq[:, 2, :], self.sq[:, 3, :]]
            yfm_sb = g_f
            gam = self.sb("rgam", [128, 2, 16], F32, ph)
            sqb = self.sq[:, 0, :]
            wpr = [self.sb("rwp%d" % k, [128, NCH, 128], BF16, ph) for k in range(6)]
            w2c = self.sb("rw2c", [64, 128], BF16, ph)
            a2c = self.sb("ra2c", [64, 128], BF16, ph)
            g2c0 = self.sb("rg2c0", [128, 128], BF16, ph)
            g2c1 = self.sb("rg2c1", [32, 128], BF16, ph)
            woc = self.sb("rwoc", [128, 1024], BF16, ph)
            xot = self.sq[:, 1, :]
            NK = 8
            pads = {}
            for kind in ("p",):
                pads[kind] = dict(
                    AR=self.sb("rAR" + kind, [128, NK, 4, C], BF16, ph),
                    B=self.sb("rB" + kind, [128, NK, 2, C], BF16, ph),
                    K=self.sb("rK" + kind, [128, NK, 2, C], BF16, ph),
                    V=self.sb("rV" + kind, [128, NK, 2, C], BF16, ph),
                    BH=self.sb("rBH" + kind, [128, NK, 2, C], BF16, ph),
                    KH=self.sb("rKH" + kind, [128, NK, 2, C], BF16, ph))
                for nm, t_ in pads[kind].items():
                    self.memset("gpsimd", t_[:, :, :, :], 0.0, ["pad" + kind + nm])
            import os as _os
            WSL = int(_os.environ.get("RW_WSL", "2"))
            Gs = self.sb("rGs", [128, NK, 128], BF16, ph)
            Zb = self.sb("rZb", [128, NK, 64], BF16, ph)
            Rqs = self.sb("rRqs", [128, NK, 128], BF16, ph)
            Yz = self.sb("rYz", [128, NK, 64], BF16, ph)
            SL = []
            for s_ in range(WSL):
                SL.append(dict(
                    XQ=[self.sb("rXQ%d_%d" % (s_, b_), [128, 256], BF16, ph) for b_ in range(2)],
                    Xt=[self.sb("rXt%d_%d" % (s_, b_), [128, 128], BF16, ph) for b_ in range(2)],
                    MM2=self.sb("rMM2_%d" % s_, [128, 256], BF16, ph),
                    MRB=self.sb("rMRB_%d" % s_, [128, 128], BF16, ph),
                    BhT=self.sb("rBhT_%d" % s_, [128, 128], BF16, ph),
                    KhT=self.sb("rKhT_%d" % s_, [128, 128], BF16, ph),
                    AT=self.sb("rAT_%d" % s_, [128, 128], BF16, ph),
                    AqT=self.sb("rAqT_%d" % s_, [128, 128], BF16, ph),
                    Vst=self.sb("rVst_%d" % s_, [128, 64], BF16, ph),
                    MakV=self.sb("rMakV_%d" % s_, [128, 64], BF16, ph),
                    Vq=self.sb("rVq_%d" % s_, [128, 64], BF16, ph)))
            Tfs = [self.sb("rTf%d" % b_, [128, 64], F32, ph) for b_ in range(2)]
            Tbs = [self.sb("rTb%d" % b_, [128, 64], BF16, ph) for b_ in range(2)]
            Tf, Tb = Tfs[0], Tbs[0]
            NYB = 4
            Ypad = [self.sb("rYpad%d" % b_, [128, 2, 64], BF16, ph) for b_ in range(NYB)]
            for b_ in range(NYB):
                self.memset("gpsimd", Ypad[b_][:, :, :], 0.0, ["rYpad%d" % b_])
            bsts = [self.sb("rbst%d" % b_, [128, 6], F32, ph) for b_ in range(NYB)]
            bags = [self.sb("rbag%d" % b_, [128, 4], F32, ph) for b_ in range(NYB)]
            Ysb = [self.sb("rYsb%d" % b_, [128, 64], F32, ph) for b_ in range(NYB)]
            ysmp = self.sb("rysmp", [128, NS], F32, ph)
            MSUIU = cst[:, self.C_SUIU:self.C_SUIU + 256]
            MSU = cst[:, self.C_SUIU:self.C_SUIU + 128]
            MIU = cst[:, self.C_SUIU + 128:self.C_SUIU + 256]
            MSL = cst[:, self.C_SL:self.C_SL + 128]
            IDN = cst[:, self.C_ID:self.C_ID + 128]
            wkv_st = self.wkv_st

            for cc in range(NCH):
                cs = slice(cc * 128, (cc + 1) * 128)
                for k, (widx, muj) in enumerate(((0, 0), (1, 2), (2, 3))):
                    prep_w(self.rwkv_w_rkv[widx][:, cs], 128, muj, wpr[2 * k], wpr[2 * k + 1], "rwp%d" % k)
                S.dma("gpsimd", "rw2c", w2c[:, :], self.rwkv_w2[:, cs], writes=["rw2c"])
                S.dma("gpsimd", "ra2c", a2c[:, :], self.rwkv_a2[:, cs], writes=["ra2c"])
                S.dma("gpsimd", "rg2c0", g2c0[:, :], self.rwkv_g2[0:128, cs], writes=["rg2c"])
                S.dma("gpsimd", "rg2c1", g2c1[:, :], self.rwkv_g2[128:160, cs], writes=["rg2c"])
                S.dma("gpsimd", "rwoc", woc[:, :], self.rwkv_wo[cs, :], writes=["rwoc"])
                self.memset("vector", Tf[:, :], 0.0, ["rTf0"])
                self.memset("vector", Tb[:, :], 0.0, ["rTb0"])
                def p1a(ti, info):
                    t0, n = self.tiles[ti]
                    par = ti % 2
                    samp = t0 >= L
                    if False:
                        yield
                    for k, (dst_, dres_) in enumerate(((r_f, "r_f"), (k_f, "k_f"), (v_f, "v_f"))):
                        for c in range(NCH):
                            self.mm(ps[6][:, :n], wpr[2 * k][:, c, :], u_ap(c, ti), c == 0, False, ["rwp%d" % k] + u_res(ti), ["ps6"])
                            if c % 4 == 3:
                                yield
                        for c in range(NCH):
                            self.mm(ps[6][:, :n], wpr[2 * k + 1][:, c, :], p_ap(c, ti), False, c == NCH - 1, ["rwp%d" % k] + u_res(ti), ["ps6"])
                            if c % 4 == 3:
                                yield
                        self.act(dst_[:, :n], ps[6][:, :n], AF.Copy, ["ps6"], [dres_])
                        yield
                    self.mm(ps[6][:, :n], w2c[:, :], tw[:, t0:t0 + n], True, True, ["rw2c", "rtw"], ["ps6"])
                    self.act(lw[:, :n], ps[6][:, :n], AF.Exp, ["ps6", "rxv"], ["lw"], scale=-1.0, bias=xv[:, cc:cc + 1])
                    yield
                    self.mm(ps[6][:, :n], a2c[:, :], ta[:, t0:t0 + n], True, True, ["ra2c", "rta"], ["ps6"])
                    self.ts("vector", lw[:, :n], lw[:, :n], 1.0, ALU.add, ["lw"], ["lw"])
                    S.op("vector", "reciprocal", dict(out=lw[:, :n], in_=lw[:, :n]), reads=["lw"], writes=["lw"])
                    self.ts("vector", lw[:, :n], lw[:, :n], -0.6065306597126334, ALU.mult, ["lw"], ["lw"])
                    yield
                    yield
                    self.act(a_f[:, :n], ps[6][:, :n], AF.Exp, ["ps6", "rxv"], ["a_f"], scale=-1.0, bias=xv[:, 8 + cc:9 + cc])
                    yield
                    self.mm(ps[6][:, :n], g2c0[:, :], tg0[:, t0:t0 + n], True, False, ["rg2c", "rtg0"], ["ps6"])
                    self.mm(ps[6][:, :n], g2c1[:, :], tg1[:, t0:t0 + n], False, True, ["rg2c", "rtg1"], ["ps6"])
                    self.ts("vector", a_f[:, :n], a_f[:, :n], 1.0, ALU.add, ["a_f"], ["a_f"])
                    S.op("vector", "reciprocal", dict(out=a_f[:, :n], in_=a_f[:, :n]), reads=["a_f"], writes=["a_f"])
                    yield
                    self.act(GF[par][:, :n], ps[6][:, :n], AF.Copy, ["ps6"], ["GF%d" % par])
                    yield
                    self.ts("vector", kkn[:, :n], k_f[:, :n], self.vcol("k_k", cc), ALU.mult, ["k_f", "vec"], ["kkn"])
                    self.tt("gpsimd", sqb[:, :n], kkn[:, :n], kkn[:, :n], ALU.mult, ["kkn"], ["rsqb"])
                    yield
                    self.mm(ps[6][:, :n], onesblk[:, :], sqb[:, :n], True, True, ["onesblk", "rsqb"], ["ps6"])
                    yield
                    self.act(E1[:, :n], ps[6][:, :n], AF.Ln, ["ps6", "rxv"], ["E1"], bias=xv[:, 16:17])
                    self.act(E1[:, :n], E1[:, :n], AF.Exp, ["E1"], ["E1"], scale=-0.5)
                    self.tt("vector", kkn[:, :n], kkn[:, :n], E1[:, :n], ALU.mult, ["kkn", "E1"], ["kkn"])
                    self.ts("vector", k2[:, :n], a_f[:, :n], self.vcol("k_a", cc), ALU.mult, ["a_f", "vec"], ["k2"], s2=self.vcol("omk_a", cc), op1=ALU.add)
                    self.tt("vector", k2[:, :n], k2[:, :n], k_f[:, :n], ALU.mult, ["k2", "k_f"], ["k2"])
                    yield
                    self.stt(sqb[:, :n], r_f[:, :n], self.vcol("r_k", cc), k2[:, :n], ALU.mult, ALU.mult, ["r_f", "k2", "vec", "rsqb"], ["rsqb"])
                    yield
                    self.mm(ps[6][:, :n], onesblk[:, :], sqb[:, :n], True, True, ["onesblk", "rsqb"], ["ps6"])
                    yield
                    self.tt("vector", BB[par][:, :n], ps[6][:, :n], v_f[:, :n], ALU.mult, ["ps6", "v_f"], ["BB%d" % par])
                    self.ts("vector", BB[par][:, :n], BB[par][:, :n], self.vcol("ln_b", cc), ALU.add, ["BB%d" % par, "vec"], ["BB%d" % par])
                    yield
                    self.tt("gpsimd", a_f[:, :n], a_f[:, :n], kkn[:, :n], ALU.mult, ["a_f", "kkn"], ["a_f"])
                    bv = a_f
                    if not samp:
                        S.op("vector", "tensor_tensor_scan", dict(out=cl[:, :n], data0=cm64[:, :n], data1=lw[:, :n], initial=0.0,
                                                                  op0=ALU.mult, op1=ALU.add), reads=["lw", "cm64"], writes=["cl"])
                    else:
                        self.cp("vector", cl[:, :n], lw[:, :n], ["lw"], ["cl"])
                    yield
                    self.act(E1[:, :n], cl[:, :n], AF.Exp, ["cl"], ["E1"])
                    self.act(E2[:, :n], cl[:, :n], AF.Exp, ["cl", "k_f"], ["k_f"], scale=-1.0)
                    self.tt("gpsimd", E3[:, :n], cl[:, :n], lw[:, :n], ALU.subtract, ["cl", "lw"], ["E3"])
                    self.act(E3[:, :n], E3[:, :n], AF.Exp, ["E3"], ["E3"])
                    yield
                    E4 = lw
                    if not samp:
                        nck = n // C
                        for q in range(nck):
                            last = q * C + C - 1
                            self.ts("vector", E4[:, q * C:(q + 1) * C], cl[:, q * C:(q + 1) * C], -1.0, ALU.mult,
                                    ["cl", "lw", "E3"], ["lw"], s2=cl[:, last:last + 1], op1=ALU.add)
                        self.act(E4[:, :n], E4[:, :n], AF.Exp, ["lw"], ["lw"])
                        yield
                        self.cp("vector", gam[:, par, 0:nck], E1[:, 0:n].rearrange("p (q k) -> p q k", k=C)[:, :, C - 1], ["E1"], ["gam%d" % par])
                        groups = [("p", 0, nck)]
                    else:
                        self.memset("vector", E4[:, :n], 1.0, ["lw"])
                        self.cp("vector", gam[:, par, 0:NS], E1[:, 0:NS], ["E1"], ["gam%d" % par])
                        groups = [("s", 0, NK), ("s", NK, NK)]
                    info["groups"] = groups
                    yield

                infos = [dict() for _ in range(ntl)]
                g_first = p1a(0, infos[0])
                for _ in g_first:
                    pass
                for ti, (t0, n) in enumerate(self.tiles):
                    samp = t0 >= L
                    par = ti % 2
                    bv = a_f
                    E4 = lw
                    groups = infos[ti]["groups"]
                    nxt_gen = p1a(ti + 1, infos[ti + 1]) if ti + 1 < ntl else None
                    if samp:
                        nxt_gen = None
                    for (kind, g0, gn) in groups:
                        P = pads["p"]
                        if kind == "s":
                            for nm, t_ in P.items():
                                nb = 4 if nm == "AR" else 2
                                for hh in range(2):
                                    hs = slice(64 * hh, 64 * hh + 64)
                                    blks = [hh, 2 + hh] if nm == "AR" else [hh]
                                    for blk in blks:
                                        self.memset("gpsimd" if hh else "vector", t_[hs, :, blk, :], 0.0, ["padp" + nm])

                        def pv(t_, hh, blk):
                            hs = slice(64 * hh, 64 * hh + 64)
                            if kind == "p":
                                return t_[hs, 0:gn, blk, :]
                            return t_[hs, 0:gn, blk, 0]

                        def fv(t_, hh):
                            hs = slice(64 * hh, 64 * hh + 64)
                            if kind == "p":
                                return t_[hs, 0:n].rearrange("p (q k) -> p q k", k=C)
                            return t_[hs, g0:g0 + gn]
                        for hh in range(2):
                            e1 = "vector" if hh == 0 else "gpsimd"
                            self.stt(pv(P["AR"], hh, hh), fv(kkn, hh), -1.0, fv(E3, hh), ALU.mult, ALU.mult, ["kkn", "E3"], ["padpAR"])
                            self.tt(e1, pv(P["AR"], hh, 2 + hh), fv(r_f, hh), fv(E1, hh), ALU.mult, ["r_f", "E1"], ["padpAR"])
                            self.tt(e1, pv(P["B"], hh, hh), fv(bv, hh), fv(E2, hh), ALU.mult, ["a_f", "k_f"], ["padpB"])
                            self.tt(e1, pv(P["K"], hh, hh), fv(k2, hh), fv(E2, hh), ALU.mult, ["k2", "k_f"], ["padpK"])
                            self.cp(e1, pv(P["V"], hh, hh), fv(v_f, hh), ["v_f"], ["padpV"])
                            self.tt(e1, pv(P["BH"], hh, hh), fv(bv, hh), fv(E4, hh), ALU.mult, ["a_f", "lw"], ["padpBH"])
                            self.tt(e1, pv(P["KH"], hh, hh), fv(k2, hh), fv(E4, hh), ALU.mult, ["k2", "lw"], ["padpKH"])
                        pr = lambda nm: "padp" + nm
                        identb = self.identb

                        def indep(q, sidx):
                            sl = SL[sidx]
                            bA, bB = ps[2 * sidx], ps[2 * sidx + 1]
                            rA, rB = "ps%d" % (2 * sidx), "ps%d" % (2 * sidx + 1)
                            sr = lambda nm: "sl%d_%s" % (sidx, nm)
                            ARq = P["AR"][:, q, :, :]
                            Aq = P["AR"][:, q, 0:2, :]
                            Rq_ = P["AR"][:, q, 2:4, :]
                            Bq = P["B"][:, q, :, :]
                            Kq = P["K"][:, q, :, :]
                            XQ, Xt = sl["XQ"], sl["Xt"]
                            tb0 = sidx * 256
                            psT = self.psT
                            XQE = "gpsimd" if _os.environ.get("RW_XQPOOL", "1") == "1" else "vector"
                            if "all" in _os.environ.get("RW_SKIP", ""):
                                return
                            yield

                            def tr(dst, src, res):
                                S.op("tensor", "transpose", dict(out=dst, in_=src, identity=identb[:, :]), reads=[res, "identb"], writes=["psT"])
                            if kind == "p":
                                self.mm(bA[:, 0:256], Bq, ARq, True, True, [pr("B"), pr("AR")], [rA])
                                self.mm(bB[:, 0:128], Aq, Bq, True, True, [pr("B"), pr("AR")], [rB])
                                tr(psT[:, tb0:tb0 + 128], P["BH"][:, q, :, :], pr("BH"))
                                tr(psT[:, tb0 + 128:tb0 + 256], P["KH"][:, q, :, :], pr("KH"))
                                yield
                                self.tt("vector", XQ[0][:, 0:128], bA[:, 0:128], MSU, ALU.mult, [rA, "cst"], [sr("XQ0")])
                                self.tt("vector", sl["MRB"][:, :], bA[:, 128:256], MIU, ALU.mult, [rA, "cst"], [sr("MRB")])
                                self.tt("vector", Xt[0][:, :], bB[:, 0:128], MSL, ALU.mult, [rB, "cst"], [sr("Xt0")])
                                self.act(sl["BhT"][:, :], psT[:, tb0:tb0 + 128], AF.Copy, ["psT"], [sr("BhT")])
                                self.act(sl["KhT"][:, :], psT[:, tb0 + 128:tb0 + 256], AF.Copy, ["psT"], [sr("KhT")])
                                yield
                                self.tt(XQE, XQ[0][:, 128:256], XQ[0][:, 0:128], IDN, ALU.add, [sr("XQ0"), "cst"], [sr("XQ0")])
                                self.mm(bA[:, 0:256], Kq, ARq, True, True, [pr("K"), pr("AR")], [rA])
                                self.mm(bB[:, 0:64], P["V"][:, q, :, :], i2b[:, :], True, True, [pr("V"), "i2b"], [rB])
                                tr(psT[:, tb0:tb0 + 128], Aq, pr("AR"))
                                yield
                                self.tt("vector", sl["MM2"][:, :], bA[:, 0:256], MSUIU, ALU.mult, [rA, "cst"], [sr("MM2")])
                                self.act(sl["Vst"][:, :], bB[:, 0:64], AF.Copy, [rB], [sr("Vst")])
                                self.act(sl["AT"][:, :], psT[:, tb0:tb0 + 128], AF.Copy, ["psT"], [sr("AT")])
                                yield
                                cur = 0
                                for lev in range(6):
                                    nxt = 1 - cur
                                    xc, xn = sr("XQ%d" % cur), sr("XQ%d" % nxt)
                                    tc, tn = sr("Xt%d" % cur), sr("Xt%d" % nxt)
                                    if lev == 0:
                                        self.mm(bA[:, 0:128], Xt[cur][:, :], XQ[cur][:, 0:128], True, True, [tc, xc], [rA])
                                        self.mm(bB[:, 0:128], XQ[cur][:, 0:128], Xt[cur][:, :], True, True, [tc, xc], [rB])
                                        yield
                                        self.act(XQ[nxt][:, 0:128], bA[:, 0:128], AF.Copy, [rA], [xn])
                                        self.cp(XQE, XQ[nxt][:, 128:256], XQ[cur][:, 128:256], [xc], [xn])
                                        self.cp("vector", Xt[nxt][:, :], bB[:, 0:128], [rB], [tn])
                                        yield
                                    elif lev < 5:
                                        self.mm(bA[:, 0:256], Xt[cur][:, :], XQ[cur][:, :], True, True, [tc, xc], [rA])
                                        self.mm(bB[:, 0:128], XQ[cur][:, 0:128], Xt[cur][:, :], True, True, [tc, xc], [rB])
                                        yield
                                        self.act(XQ[nxt][:, 0:128], bA[:, 0:128], AF.Copy, [rA], [xn])
                                        self.tt("vector", XQ[nxt][:, 128:256], bA[:, 128:256], XQ[cur][:, 128:256], ALU.add, [rA, xc], [xn])
                                        self.act(Xt[nxt][:, :], bB[:, 0:128], AF.Copy, [rB], [tn])
                                        yield
                                    else:
                                        self.mm(bA[:, 0:128], Xt[cur][:, :], XQ[cur][:, 128:256], True, True, [tc, xc], [rA])
                                        self.mm(bB[:, 0:64], sl["MM2"][:, 0:128], sl["Vst"][:, :], True, True, [sr("MM2"), sr("Vst")], [rB])
                                        yield
                                        self.tt("vector", XQ[nxt][:, 128:256], bA[:, 0:128], XQ[cur][:, 128:256], ALU.add, [rA, xc], [xn])
                                        self.act(sl["MakV"][:, :], bB[:, 0:64], AF.Copy, [rB], [sr("MakV")])
                                        yield
                                    cur = nxt
                                Q = XQ[cur][:, 128:256]
                                qres = sr("XQ%d" % cur)
                                self.mm(bA[:, 0:128], Q, sl["AT"][:, :], True, True, [qres, sr("AT")], [rA])
                                self.mm(bB[:, 0:64], Q, sl["MakV"][:, :], True, True, [qres, sr("MakV")], [rB])
                                yield
                                self.act(sl["AqT"][:, :], bA[:, 0:128], AF.Copy, [rA], [sr("AqT")])
                                self.act(sl["Vq"][:, :], bB[:, 0:64], AF.Copy, [rB], [sr("Vq")])
                                yield
                                AqT, aqres = sl["AqT"], sr("AqT")
                            else:
                                self.mm(bA[:, 0:128], Bq, Rq_, True, True, [pr("B"), pr("AR")], [rA])
                                self.mm(bB[:, 0:128], Kq, Rq_, True, True, [pr("K"), pr("AR")], [rB])
                                tr(psT[:, tb0:tb0 + 128], P["BH"][:, q, :, :], pr("BH"))
                                tr(psT[:, tb0 + 128:tb0 + 256], P["KH"][:, q, :, :], pr("KH"))
                                yield
                                self.tt("vector", sl["MRB"][:, :], bA[:, 0:128], MIU, ALU.mult, [rA, "cst"], [sr("MRB")])
                                self.tt("vector", sl["MM2"][:, 128:256], bB[:, 0:128], MIU, ALU.mult, [rB, "cst"], [sr("MM2")])
                                self.act(sl["BhT"][:, :], psT[:, tb0:tb0 + 128], AF.Copy, ["psT"], [sr("BhT")])
                                self.act(sl["KhT"][:, :], psT[:, tb0 + 128:tb0 + 256], AF.Copy, ["psT"], [sr("KhT")])
                                yield
                                self.mm(bB[:, 0:64], P["V"][:, q, :, :], i2b[:, :], True, True, [pr("V"), "i2b"], [rB])
                                tr(psT[:, tb0:tb0 + 128], Aq, pr("AR"))
                                yield
                                self.act(sl["Vst"][:, :], bB[:, 0:64], AF.Copy, [rB], [sr("Vst")])
                                self.cp("vector", sl["AT"][:, :], psT[:, tb0:tb0 + 128], ["psT"], [sr("AT")])
                                yield
                                AqT, aqres = sl["AT"], sr("AT")
                            self.mm(bA[:, 0:128], AqT[:, :], sl["BhT"][:, :], True, True, [aqres, sr("BhT")], [rA])
                            self.mm(bB[:, 0:128], AqT[:, :], sl["MRB"][:, :], True, True, [aqres, sr("MRB")], [rB])
                            yield
                            self.act(Gs[:, q, :], bA[:, 0:128], AF.Copy, [rA], ["rGs%d" % q])
                            self.tt("vector", Rqs[:, q, :], bB[:, 0:128], Rq_, ALU.add, [rB, pr("AR")], ["rRqs%d" % q])
                            yield
                            if kind == "p":
                                self.mm(bA[:, 0:64], sl["BhT"][:, :], sl["Vq"][:, :], True, False, [sr("BhT"), sr("Vq")], [rA])
                                self.mm(bA[:, 0:64], sl["KhT"][:, :], sl["Vst"][:, :], False, True, [sr("KhT"), sr("Vst")], [rA])
                                self.mm(bB[:, 0:64], sl["MRB"][:, :], sl["Vq"][:, :], True, False, [sr("MRB"), sr("Vq")], [rB])
                                self.mm(bB[:, 0:64], sl["MM2"][:, 128:256], sl["Vst"][:, :], False, True, [sr("MM2"), sr("Vst")], [rB])
                            else:
                                self.mm(bA[:, 0:64], sl["KhT"][:, :], sl["Vst"][:, :], True, True, [sr("KhT"), sr("Vst")], [rA])
                                self.mm(bB[:, 0:64], sl["MM2"][:, 128:256], sl["Vst"][:, :], True, True, [sr("MM2"), sr("Vst")], [rB])
                            yield
                            self.act(Zb[:, q, :], bA[:, 0:64], AF.Copy, [rA], ["rZb%d" % q])
                            self.act(Yz[:, q, :], bB[:, 0:64], AF.Copy, [rB], ["rYz%d" % q])
                            yield

                        def sidegen(q, yb, row):
                            bst, bag, ysb, ypad = bsts[yb], bags[yb], Ysb[yb], Ypad[yb]
                            yield
                            self.act(bag[:, 2:3], bag[:, 1:2], AF.Ln, ["rbag%d" % yb, "consts"], ["rbag%d" % yb], bias=self.eps_t[:, 3:4])
                            self.act(bag[:, 2:3], bag[:, 2:3], AF.Exp, ["rbag%d" % yb], ["rbag%d" % yb], scale=-0.5)
                            yield
                            for hh in range(2):
                                hs = slice(64 * hh, 64 * hh + 64)
                                self.ts("vector", ypad[hs, hh, :], ysb[hs, :], bag[hs, 0:1], ALU.subtract,
                                        ["rYsb%d" % yb, "rbag%d" % yb], ["rYpad%d" % yb], s2=bag[hs, 2:3], op1=ALU.mult)
                            yield
                            self.mm(ps[5][:, 64:128], ypad[:, :, :], i2b[:, :], True, True, ["rYpad%d" % yb, "i2b"], ["ps5"])
                            yield
                            if kind == "p":
                                self.act(yfm_sb[:, q * C:(q + 1) * C], ps[5][:, 64:128], AF.Copy, ["ps5"], ["yfm"])
                            else:
                                self.cp("vector", yfm_sb[:, row:row + 1], ps[5][:, 64:65], ["ps5"], ["yfm"])
                            yield

                        def seqgen(done):
                            if "seq" in _os.environ.get("RW_SKIP", ""):
                                return
                            sides = []

                            def adv_sides():
                                for sd in list(sides):
                                    try:
                                        next(sd)
                                    except StopIteration:
                                        sides.remove(sd)
                            for q in range(gn):
                                while q not in done:
                                    adv_sides()
                                    yield
                                row = g0 + q
                                if kind == "s":
                                    tb_i = q % 2
                                    S.dma("sync", "rTf%d" % tb_i, Tfs[tb_i][:, :], wkv_st[row, cc], writes=["rTf%d" % tb_i])
                                    self.cp("vector", Tbs[tb_i][:, :], Tfs[tb_i][:, :], ["rTf%d" % tb_i], ["rTb%d" % tb_i])
                                    gcol = gam[:, par, row:row + 1]
                                else:
                                    tb_i = 0
                                    gcol = gam[:, par, q:q + 1]
                                tf_, tb_ = Tfs[tb_i], Tbs[tb_i]
                                tfr, tbr = "rTf%d" % tb_i, "rTb%d" % tb_i
                                yb = q % NYB
                                self.mm(ps[4][:, 0:64], Gs[:, q, :], tb_[:, :], True, False, ["rGs%d" % q, tbr], ["ps4"])
                                self.mm(ps[4][:, 0:64], identb[:, :], Zb[:, q, :], False, True, ["identb", "rZb%d" % q], ["ps4"])
                                self.mm(ps[5][:, 0:64], Rqs[:, q, :], tb_[:, :], True, False, ["rRqs%d" % q, tbr], ["ps5"])
                                self.mm(ps[5][:, 0:64], identb[:, :], Yz[:, q, :], False, True, ["identb", "rYz%d" % q], ["ps5"])
                                adv_sides()
                                yield
                                self.stt(tb_[:, :], tf_[:, :], gcol, ps[4][:, 0:64], ALU.mult, ALU.add, [tfr, "gam%d" % par, "ps4"], [tbr])
                                self.act(Ysb[yb][:, :], ps[5][:, 0:64], AF.Copy, ["ps5"], ["rYsb%d" % yb])
                                self.stt(tf_[:, :], tf_[:, :], gcol, ps[4][:, 0:64], ALU.mult, ALU.add, [tfr, "gam%d" % par, "ps4"], [tfr])
                                if kind == "s":
                                    S.dma("sync", "rTfo%d" % tb_i, self.wkv_so[row, cc], tf_[:, :], reads=[tfr])
                                S.op("vector", "bn_stats", dict(out=bsts[yb][:, :], in_=Ysb[yb][:, :]), reads=["rYsb%d" % yb], writes=["rbst%d" % yb])
                                S.op("vector", "bn_aggr", dict(out=bags[yb][:, 0:2], in_=bsts[yb][:, :]), reads=["rbst%d" % yb], writes=["rbag%d" % yb])
                                sd = sidegen(q, yb, row)
                                next(sd)
                                sides.append(sd)
                                yield
                            while sides:
                                adv_sides()
                                yield

                        from collections import deque
                        pend = deque(range(gn))
                        slots = [None] * WSL
                        done = set()
                        sg = seqgen(done)
                        seq_alive = True
                        nx_alive = nxt_gen is not None and (kind == "p" or g0 > 0 or True)
                        while pend or any(x is not None for x in slots) or seq_alive or nx_alive:
                            if nx_alive:
                                try:
                                    next(nxt_gen)
                                except StopIteration:
                                    nx_alive = False
                            for sidx in range(WSL):
                                if slots[sidx] is None and pend:
                                    q_ = pend.popleft()
                                    slots[sidx] = (q_, indep(q_, sidx))
                                if slots[sidx] is not None:
                                    q_, g_ = slots[sidx]
                                    try:
                                        next(g_)
                                    except StopIteration:
                                        done.add(q_)
                                        slots[sidx] = None
                            if seq_alive:
                                try:
                                    next(sg)
                                except StopIteration:
                                    seq_alive = False
                    if not samp and t0 + n == L:
                        S.dma("sync", "rTfo0", self.wkv_po[cc], Tf[:, :], reads=["rTf0"])
                    self.stt(BB[par][:, :n], yfm_sb[:, :n], self.vcol("ln_w", cc), BB[par][:, :n], ALU.mult, ALU.add, ["yfm", "BB%d" % par, "vec"], ["BB%d" % par])
                    self.tt("vector", xot[:, :n], BB[par][:, :n], GF[par][:, :n], ALU.mult, ["BB%d" % par, "GF%d" % par], ["rxot"])
                    for d in range(NCH):
                        pb = ps[d % 2]
                        prr = "ps%d" % (d % 2)
                        self.mm(pb[:, :n], woc[:, d * 128:(d + 1) * 128], xot[:, :n], True, True, ["rwoc", "rxot"], [prr])
                        self.tt("vector", h[:, d, t0:t0 + n], h[:, d, t0:t0 + n], pb[:, :n], ALU.add, [prr, "h%d" % ti], ["h%d" % ti])
            S.barrier()

    def gla_layer(self, i):
        S = self.S
        L, NT = self.L, self.NT
        h, ub, ps = self.h, self.ubuf, self.ps
        CG = 128
        NCK = L // CG
        ntl = len(self.tiles)
        allu = ["u%d" % ti for ti in range(ntl)]
        self.mix_norm_all(i)
        w_in = self.gla_w_in
        with contextlib.ExitStack() as ph:
            self.cm128 = self.sb("cm128", [128, L], F32, ph)
            self.memset("gpsimd", self.cm128[:, :], 1.0, ["cm"])
            self.memset("gpsimd", self.cm128[:, :].rearrange("p (n k) -> p n k", k=128)[:, :, 0:1], 0.0, ["cm"])
            vtok = self.sb("vtok", [128, NCK, 256], BF16, ph)
            vtok_s = self.sb("vtok_s", [NS, 256], F32, ph)
            wvh = self.sb("gwvh", [128, NCH, 256], BF16, ph)
            glb = self.sb("glb", [16, NT], BF16, ph)
            wgl = self.sb("wgl", [128, NCH, 16], BF16, ph)
            negb = self.sb("negb", [128, 4], F32, ph)
            gb = VEC_COLS["gla_b_gk"]
            self.ts("vector", negb[:, :], self.vec[:, gb:gb + 4], -1.0, ALU.mult, ["vec"], ["negb"])
            self.load_w(wgl[:, :, :], w_in[:, 3072:3088], "wgl")
            if True:
                for ti, (t0, n) in enumerate(self.tiles):
                    for c in range(NCH):
                        self.mm(ps[2][:16, :n], wgl[:, c, :], ub[:, c, t0:t0 + n], c == 0, c == NCH - 1, ["wgl", "u%d" % ti], ["ps2"])
                    self.act(glb[:, t0:t0 + n], ps[2][:16, :n], AF.Copy, ["ps2"], ["glb"])
                S.barrier()
            bA = self.sb("gA", [128, NT], F32, ph)
            bB = self.sb("gB", [128, NT], F32, ph)
            qt = self.sb("gqt", [128, NT], BF16, ph)
            kt = self.sb("gkt", [128, NT], BF16, ph)
            kh = self.sb("gkh", [128, L], BF16, ph)
            oh = self.sb("goh", [128, 2, NT], BF16, ph)
            xoh = self.sb("gxo", [128, 2, 512], BF16, ph)
            wq = self.sb("gwq", [128, NCH, 128], BF16, ph)
            wk = self.sb("gwk", [128, NCH, 128], BF16, ph)
            wg = self.sb("gwg", [128, NCH, 256], BF16, ph)
            woh = self.sb("gwo", [128, 2, 1024], BF16, ph)
            wgk2 = self.sb("gwgk2", [16, 128], BF16, ph)
            Sf = self.sb("gSf", [128, 256], F32, ph)
            Sb = self.sb("gSb", [128, 256], BF16, ph)
            attb2 = [self.sb("gatt%d" % b_, [128, 128], BF16, ph) for b_ in range(2)]
            khT2 = [self.sb("gkhT%d" % b_, [128, 128], BF16, ph) for b_ in range(2)]
            ktok_s = self.sb("gktoks", [NS, 128], F32, ph)
            ksel2 = [self.sb("gksel%d" % b_, [NS, 128], F32, ph) for b_ in range(2)]
            qs = self.sb("gqs", [128, NS], F32, ph)
            S0 = [self.sb("gS0_%d" % b, [128, 256], F32, ph) for b in range(2)]
            S1 = [self.sb("gS1_%d" % b, [128, 256], F32, ph) for b in range(2)]
            tmp = self.sb("gtmp", [128, 512], F32, ph)
            tmp2 = self.sb("gtmp2", [128, 512], F32, ph)
            rs = self.rstd
            sqh = self.sq
            for hd in range(4):
                self.load_w(wq[:, :, :], w_in[:, hd * 128:(hd + 1) * 128], "gwq")
                self.load_w(wk[:, :, :], w_in[:, 512 + hd * 128:512 + (hd + 1) * 128], "gwk")
                self.load_w(wg[:, :, :], w_in[:, 2048 + hd * 256:2048 + (hd + 1) * 256], "gwg")
                self.load_w(woh[:, :, :], self.gla_wo[hd * 256:(hd + 1) * 256, :], "gwo")
                S.dma("gpsimd", "gwgk2", wgk2[:, :], self.gla_w_gk2[:, hd * 128:(hd + 1) * 128], writes=["gwgk2"])
                self.load_w(wvh[:, :, :], w_in[:, 1024 + hd * 256:1024 + (hd + 1) * 256], "gwvh")
                for n in range(NCK + 1):
                    pb = ps[n % 2]
                    if n < NCK:
                        lo, cnt = n * CG, CG
                    else:
                        lo, cnt = L, NS
                    for c in range(NCH):
                        self.mm(pb[:cnt, :256], ub[:, c, lo:lo + cnt], wvh[:, c, :], c == 0, c == NCH - 1, ["gwvh"] + allu, ["ps%d" % (n % 2)])
                    if n < NCK:
                        self.act(vtok[:, n, :], pb[:, :256], AF.Copy, ["ps%d" % (n % 2)], ["vtok"])
                    else:
                        self.act(vtok_s[:, :], pb[:NS, :256], AF.Copy, ["ps%d" % (n % 2)], ["vtok_s"])
                for ti, (t0, n) in enumerate(self.tiles):
                    self.mm(ps[0][:, :n], wgk2[:, :], glb[:, t0:t0 + n], True, True, ["gwgk2", "glb"], ["ps0"])
                    self.act(bA[:, t0:t0 + n], ps[0][:, :n], AF.Exp, ["ps0", "negb"], ["gA"], scale=-1.0, bias=negb[:, hd:hd + 1])
                self.act(bA[:, :], bA[:, :], AF.Ln, ["gA", "consts"], ["gA"], bias=self.eps_t[:, 1:2])
                S.op("vector", "tensor_tensor_scan", dict(out=bB[:, 0:L], data0=self.cm128[:, 0:L], data1=bA[:, 0:L], initial=0.0,
                                                          op0=ALU.mult, op1=ALU.add), reads=["gA", "cm"], writes=["gB"])
                self.cp("vector", bB[:, L:NT], bA[:, L:NT], ["gA"], ["gB"])
                self.act(bA[:, :], bB[:, :], AF.Exp, ["gB"], ["gA"], scale=-1.0 / 16)
                self.act(bB[:, :], bB[:, :], AF.Exp, ["gB"], ["gB"], scale=1.0 / 16)
                for ti, (t0, n) in enumerate(self.tiles):
                    for c in range(NCH):
                        self.mm(ps[0][:, :n], wq[:, c, :], ub[:, c, t0:t0 + n], c == 0, c == NCH - 1, ["gwq", "u%d" % ti], ["ps0"])
                    for c in range(NCH):
                        self.mm(ps[1][:, :n], wk[:, c, :], ub[:, c, t0:t0 + n], c == 0, c == NCH - 1, ["gwk", "u%d" % ti], ["ps1"])
                    self.stt(qt[:, t0:t0 + n], ps[0][:, :n], 128.0 ** -0.5, bA[:, t0:t0 + n], ALU.mult, ALU.mult, ["ps0", "gA"], ["gqt"])
                    self.tt("vector", kt[:, t0:t0 + n], ps[1][:, :n], bB[:, t0:t0 + n], ALU.mult, ["ps1", "gB"], ["gkt"])
                    if t0 >= L:
                        self.ts("vector", qs[:, :], ps[0][:, :NS], 128.0 ** -0.5, ALU.mult, ["ps0"], ["gqs"])
                        for c in range(NCH):
                            self.mm(ps[2][:NS, :128], ub[:, c, L:NT], wk[:, c, :], c == 0, c == NCH - 1, ["gwk", "u%d" % ti], ["ps2"])
                        self.act(ktok_s[:, :], ps[2][:NS, :128], AF.Copy, ["ps2"], ["gktoks"])
                for n in range(NCK):
                    last = n * CG + CG - 1
                    self.ts("vector", kh[:, n * CG:(n + 1) * CG], kt[:, n * CG:(n + 1) * CG], bA[:, last:last + 1], ALU.mult, ["gkt", "gA"], ["gkh"])
                self.memset("vector", Sf[:, :], 0.0, ["gSf"])
                self.memset("vector", Sb[:, :], 0.0, ["gSb"])
                for n in range(NCK):
                    sl = slice(n * CG, (n + 1) * CG)
                    pp = n % 2
                    pa, pra = ps[pp], "ps%d" % pp
                    tbo = pp * 128
                    self.mm(pa[:, :128], kt[:, sl], qt[:, sl], True, True, ["gkt", "gqt"], [pra])
                    self.tt("vector", attb2[pp][:, :], pa[:, :128], self.cst[:, self.C_TRIU:self.C_TRIU + 128], ALU.mult, [pra, "cst"], ["gatt%d" % pp])
                    S.op("tensor", "transpose", dict(out=self.psT[:, tbo:tbo + 128], in_=kh[:, sl], identity=self.identb[:, :]),
                         reads=["gkh", "identb"], writes=["psT"])
                    self.act(khT2[pp][:, :], self.psT[:, tbo:tbo + 128], AF.Copy, ["psT"], ["gkhT%d" % pp])
                    for m in range(2):
                        pb = ps[2 + 2 * pp + m]
                        pr = "ps%d" % (2 + 2 * pp + m)
                        self.mm(pb[:, :128], Sb[:, m * 128:(m + 1) * 128], qt[:, sl], True, False, ["gSb", "gqt"], [pr])
                        self.mm(pb[:, :128], vtok[:, n, m * 128:(m + 1) * 128], attb2[pp][:, :], False, True,
                                ["vtok", "gatt%d" % pp], [pr])
                        self.act(oh[:, m, sl], pb[:, :128], AF.Copy, [pr], ["goh"])
                    self.mm(ps[6][:, :256], khT2[pp][:, :], vtok[:, n, :], True, True, ["gkhT%d" % pp, "vtok"], ["ps6"])
                    last = n * CG + CG - 1
                    self.stt(Sf[:, :], Sf[:, :], bA[:, last:last + 1], ps[6][:, :256], ALU.mult, ALU.add, ["gSf", "gA", "ps6"], ["gSf"])
                    if n < NCK - 1:
                        self.cp("vector", Sb[:, :], Sf[:, :], ["gSf"], ["gSb"])
                S.dma("sync", "gSf", self.gla_po[hd], Sf[:, :], reads=["gSf"])
                for r in range(NS):
                    b = r % 2
                    S.dma("sync", "gS0_%d" % b, S0[b][:, :], self.gla_st[r, hd], writes=["gS0_%d" % b])
                    self.ts("vector", ksel2[b][:, :], ktok_s[:, :], self.cst[:NS, self.C_ID + r:self.C_ID + r + 1], ALU.mult, ["gktoks", "cst"], ["gksel%d" % b])
                    zb_, zr_ = ps[5 + b], "ps%d" % (5 + b)
                    self.mm(zb_[:, :256], ksel2[b][:, :], vtok_s[:, :], True, True, ["gksel%d" % b, "vtok_s"], [zr_])
                    self.stt(S1[b][:, :], S0[b][:, :], bA[:, L + r:L + r + 1], zb_[:, :256], ALU.mult, ALU.add,
                             ["gS0_%d" % b, "gA", zr_], ["gS1_%d" % b])
                    S.dma("sync", "gS1_%d" % b, self.gla_so[r, hd], S1[b][:, :], reads=["gS1_%d" % b])
                    ob_, or_ = ps[3 + b], "ps%d" % (3 + b)
                    self.mm(ob_[:, 0:1], S1[b][:, 0:128], qs[:, r:r + 1], True, True, ["gS1_%d" % b, "gqs"], [or_])
                    self.cp("vector", oh[:, 0, L + r:L + r + 1], ob_[:, 0:1], [or_], ["goh"])
                    self.mm(ob_[:, 0:1], S1[b][:, 128:256], qs[:, r:r + 1], True, True, ["gS1_%d" % b, "gqs"], [or_])
                    self.cp("vector", oh[:, 1, L + r:L + r + 1], ob_[:, 0:1], [or_], ["goh"])
                for ti, (t0, n) in enumerate(self.tiles):
                    self.act(sqh[:, 0:2, :n], oh[:, :, t0:t0 + n], AF.Square, ["goh"], ["sq"])
                    for m in range(2):
                        self.mm(ps[6][:, :n], self.ones_bf[:, :], sqh[:, m, :n], m == 0, m == 1, ["sq", "ones"], ["ps6"])
                    self.act(rs[:, :n], ps[6][:, :n], AF.Ln, ["ps6", "consts"], ["rstd"], scale=1.0 / 256, bias=self.eps_t[:, 2:3])
                    self.act(rs[:, :n], rs[:, :n], AF.Exp, ["rstd"], ["rstd"], scale=-0.5)
                    for m in range(2):
                        pb = ps[m]
                        pr = "ps%d" % m
                        for c in range(NCH):
                            self.mm(pb[:, :n], wg[:, c, m * 128:(m + 1) * 128], ub[:, c, t0:t0 + n], c == 0, c == NCH - 1, ["gwg", "u%d" % ti], [pr])
                        self.act(tmp[:, :n], pb[:, :n], AF.Exp, [pr], ["gtmp"], scale=-1.0)
                        self.ts("vector", tmp[:, :n], tmp[:, :n], 1.0, ALU.add, ["gtmp"], ["gtmp"])
                        S.op("vector", "reciprocal", dict(out=tmp[:, :n], in_=tmp[:, :n]), reads=["gtmp"], writes=["gtmp"])
                        self.tt("vector", tmp[:, :n], tmp[:, :n], pb[:, :n], ALU.mult, ["gtmp", pr], ["gtmp"])
                        self.stt(tmp2[:, :n], oh[:, m, t0:t0 + n], self.vec[:, VEC_COLS["gla_norm"] + m:VEC_COLS["gla_norm"] + m + 1], rs[:, :n],
                                 ALU.mult, ALU.mult, ["goh", "vec", "rstd"], ["gtmp2"])
                        self.tt("vector", xoh[:, m, :n], tmp2[:, :n], tmp[:, :n], ALU.mult, ["gtmp", "gtmp2"], ["gxo"])
                    for d in range(NCH):
                        pb = ps[4 + d % 2]
                        pr = "ps%d" % (4 + d % 2)
                        for m in range(2):
                            self.mm(pb[:, :n], woh[:, m, d * 128:(d + 1) * 128], xoh[:, m, :n], m == 0, m == 1, ["gwo", "gxo"], [pr])
                        self.tt("vector", h[:, d, t0:t0 + n], h[:, d, t0:t0 + n], pb[:, :n], ALU.add, [pr, "h%d" % ti], ["h%d" % ti])
            S.barrier()

    def conv_layer(self, i):
        S = self.S
        L, NT = self.L, self.NT
        self.mix_norm_all(i)
        ub = self.ubuf
        with contextlib.ExitStack() as ph:
            xo = self.sb("cxo", [128, NCH, NT], BF16, ph)
            wo = self.sb("cwo", [128, NCH, 1024], BF16, ph)
            ws = [[self.sb("cw%d_%d" % (k, b), [128, NCH, 128], BF16, ph) for k in range(3)] for b in range(2)]
            zcb = self.sb("zcb", [128, 2 + L + 3 * NS], F32, ph)
            gbt = self.sb("gbt", [128, NT], F32, ph)
            cv = self.sb("cv", [128, NT], F32, ph)
            tmp = self.sb("ctmp", [128, 512], F32, ph)
            cst = self.sb("cst", [128, NCH, 2 + 2 * NS], F32, ph)
            self.load_w(wo[:, :, :], self.conv_wo[:, :], "cwo", nsplit=2)
            SB = 2 + L
            zs = zcb[:, SB:SB + 3 * NS].rearrange("p (r k) -> p r k", k=3)
            stv = self.conv_st.rearrange("(c p) r k -> p c r k", p=128)
            self.memset("gpsimd", zcb[:, 0:2], 0.0, ["zcb_halo"])
            def conv_load(cc):
                b = cc % 2
                for k in range(3):
                    self.load_w(ws[b][k][:, :, :], self.conv_w_in[:, k * 1024 + cc * 128:k * 1024 + (cc + 1) * 128], "cw%d_%d" % (k, b))
            conv_load(0)
            for cc in range(NCH):
                b = cc % 2
                if cc + 1 < NCH:
                    conv_load(cc + 1)
                S.dma("sync", "zcb_st", zs[:, :, 0:2], stv[:, cc, :, :], writes=["zcb_st"])
                for ti, (t0, n) in enumerate(self.tiles):
                    pbs = []
                    for k in range(3):
                        pb = self.ps[k]
                        for c in range(NCH):
                            self.mm(pb[:, :n], ws[b][k][:, c, :], ub[:, c, t0:t0 + n], c == 0, c == NCH - 1,
                                    ["cw%d_%d" % (k, b), "u%d" % ti], ["ps%d" % k])
                    self.act(gbt[:, t0:t0 + n], self.ps[0][:, :n], AF.Copy, ["ps0"], ["gbt%d" % ti])
                    self.act(tmp[:, :n], self.ps[1][:, :n], AF.Copy, ["ps1"], ["ctmp"])
                    if t0 < L:
                        dst = zcb[:, 2 + t0:2 + t0 + n]
                    else:
                        dst = zs[:, :, 2]
                    self.tt("vector", dst, tmp[:, :n], self.ps[2][:, :n], ALU.mult, ["ctmp", "ps2"], ["zcb%d" % ti])
                allz = ["zcb%d" % ti for ti in range(len(self.tiles))] + ["zcb_halo", "zcb_st"]
                self.ts("vector", cv[:, 0:L], zcb[:, 2:2 + L], self.vcol("conv_w2", cc), ALU.mult, allz + ["vec"], ["cv"])
                self.stt(cv[:, 0:L], zcb[:, 1:1 + L], self.vcol("conv_w1", cc), cv[:, 0:L], ALU.mult, ALU.add, allz + ["cv", "vec"], ["cv"])
                self.stt(cv[:, 0:L], zcb[:, 0:L], self.vcol("conv_w0", cc), cv[:, 0:L], ALU.mult, ALU.add, allz + ["cv", "vec"], ["cv"])
                self.ts("vector", cv[:, L:NT], zs[:, :, 2], self.vcol("conv_w2", cc), ALU.mult, allz + ["vec", "cv"], ["cv"])
                self.stt(cv[:, L:NT], zs[:, :, 1], self.vcol("conv_w1", cc), cv[:, L:NT], ALU.mult, ALU.add, allz + ["cv", "vec"], ["cv"])
                self.stt(cv[:, L:NT], zs[:, :, 0], self.vcol("conv_w0", cc), cv[:, L:NT], ALU.mult, ALU.add, allz + ["cv", "vec"], ["cv"])
                gres = ["gbt%d" % ti for ti in range(len(self.tiles))]
                self.tt("vector", xo[:, cc, :], gbt[:, :], cv[:, :], ALU.mult, gres + ["cv"], ["cxo%d" % cc])
                self.cp("vector", cst[:, cc, 0:2], zcb[:, 2 + L - 2:2 + L], allz, ["cst"])
                self.cp("vector", cst[:, cc, 2:2 + 2 * NS].rearrange("p (r k) -> p r k", k=2), zs[:, :, 1:3], allz, ["cst"])
            S.dma("sync", "cst", self.conv_o.rearrange("(c p) n -> p c n", p=128), cst[:, :, :], reads=["cst"])
            self.out_proj(xo, wo, "cwo", lambda j, ti: ["cxo%d" % j])
            S.barrier()

    def pool_layer(self, i):
        S = self.S
        L, NT = self.L, self.NT
        h = self.h
        ub = self.ubuf
        NB = 15 + L + 16 * NS
        with contextlib.ExitStack() as ph:
            rall = self.sb("rall", [128, NT], F32, ph)
            pw = self.sb("pw", [128, 4, 2, 256], BF16, ph)
            bufs = [self.sb("pbuf%d" % b, [128, NB], F32, ph) for b in range(3)]
            pst = self.sb("pst", [128, NCH, 15 + 15 * NS], F32, ph)
            S.dma("gpsimd", "pw", pw[:, :, :, :], self.pool_w.rearrange("g (k p) n -> p g k n", p=128), writes=["pw"])
            for ti, (t0, n) in enumerate(self.tiles):
                self.rms_rstd(ti, rall[:, t0:t0 + n], "rall")
            stv = self.pool_st.rearrange("(c p) r k -> p c r k", p=128)
            SB = 15 + L
            for b in range(3):
                self.memset("gpsimd", bufs[b][:, 0:15], 0.0, ["pbuf%d" % b])
            for c in range(NCH):
                gi = c // 2
                w = 2 << gi
                A = bufs[0]
                As = A[:, SB:SB + 16 * NS].rearrange("p (r k) -> p r k", k=16)
                S.dma("sync", "pbuf0", As[:, :, 0:15], stv[:, c, :, :], writes=["pbuf0"])
                allh = ["h%d" % ti for ti in range(len(self.tiles))]
                self.stt(A[:, 15:15 + L], h[:, c, 0:L], self.vcol("norm_mix%d" % i, c), rall[:, 0:L], ALU.mult, ALU.mult,
                         allh + ["rall", "vec"], ["pbuf0"])
                self.stt(As[:, :, 15], h[:, c, L:NT], self.vcol("norm_mix%d" % i, c), rall[:, L:NT], ALU.mult, ALU.mult,
                         allh + ["rall", "vec"], ["pbuf0"])
                self.cp("gpsimd", pst[:, c, 0:15], A[:, L:L + 15], ["pbuf0"], ["pst"])
                self.cp("gpsimd", pst[:, c, 15:15 + 15 * NS].rearrange("p (r k) -> p r k", k=15), As[:, :, 1:16], ["pbuf0"], ["pst"])
                src, si = A, 0
                lo = 0
                for k in range(gi + 1):
                    sh = 1 << k
                    lo += sh
                    di = 1 if si != 1 else 2
                    dst = bufs[di]
                    ss = src[:, SB:SB + 16 * NS].rearrange("p (r k) -> p r k", k=16)
                    ds = dst[:, SB:SB + 16 * NS].rearrange("p (r k) -> p r k", k=16)
                    self.tt("vector", dst[:, lo:15 + L], src[:, lo:15 + L], src[:, lo - sh:15 + L - sh], ALU.add,
                            ["pbuf%d" % si], ["pbuf%d" % di])
                    self.tt("gpsimd", ds[:, :, lo:16], ss[:, :, lo:16], ss[:, :, lo - sh:16 - sh], ALU.add,
                            ["pbuf%d" % si], ["pbuf%d" % di])
                    src, si = dst, di
                ss = src[:, SB:SB + 16 * NS].rearrange("p (r k) -> p r k", k=16)
                self.stt(ub[:, c, 0:L], src[:, 15:15 + L], 1.0 / w, A[:, 15:15 + L], ALU.mult, ALU.subtract,
                         ["pbuf%d" % si, "pbuf0"], ["pd%d" % c])
                self.stt(ub[:, c, L:NT], ss[:, :, 15], 1.0 / w, As[:, :, 15], ALU.mult, ALU.subtract,
                         ["pbuf%d" % si, "pbuf0"], ["pd%d" % c])
                nf = w - 1
                ic = VEC_COLS["invc"]
                ftmp = self.rstd
                self.tt("vector", ftmp[:, 0:nf], src[:, 15:15 + nf], self.vec[:, ic:ic + nf], ALU.mult, ["pbuf%d" % si, "vec"], ["rstd"])
                self.tt("vector", ub[:, c, 0:nf], ftmp[:, 0:nf], A[:, 15:15 + nf], ALU.subtract, ["rstd", "pbuf0"], ["pd%d" % c])
            S.dma("sync", "pst", self.pool_o.rearrange("(c p) n -> p c n", p=128), pst[:, :, :], reads=["pst"])
            for ti, (t0, n) in enumerate(self.tiles):
                for gi in range(4):
                    for m in range(2):
                        d = 2 * gi + m
                        pb = self.ps[4 + d % 2]
                        pr = "ps%d" % (4 + d % 2)
                        for k in range(2):
                            self.mm(pb[:, :n], pw[:, gi, k, m * 128:(m + 1) * 128], ub[:, 2 * gi + k, t0:t0 + n], k == 0, k == 1,
                                    ["pw", "pd%d" % (2 * gi + k)], [pr])
                        self.stt(h[:, d, t0:t0 + n], pb[:, :n], self.vcol("pool_scale", d), h[:, d, t0:t0 + n], ALU.mult, ALU.add,
                                 [pr, "h%d" % ti, "vec"], ["h%d" % ti])
            S.barrier()

    def build(self):
        L, NT = self.L, self.NT
        nc = bass.Bass("TRN2", target_bir_lowering=False)
        self.nc = nc
        self.xT = self.dram_in("xT", [D, NT])
        self.vec_d = self.dram_in("vec", [128, NVEC])
        self.ffn_up = self.dram_in("ffn_up", [4, D, DFF])
        self.ffn_down = self.dram_in("ffn_down", [4, DFF, D])
        self.yT = self.dram_out("yT", [D, NT])
        self.conv_w_in = self.dram_in("conv_w_in", [D, 3 * D])
        self.conv_wo = self.dram_in("conv_wo", [D, D])
        self.conv_st = self.dram_in("conv_st", [D, NS, 2])
        self.conv_o = self.dram_out("conv_o", [D, 2 + 2 * NS])
        self.pool_w = self.dram_in("pool_w", [4, 256, 256])
        self.pool_st = self.dram_in("pool_st", [D, NS, 15])
        self.pool_o = self.dram_out("pool_o", [D, 15 + 15 * NS])
        self.cst_d = self.dram_in("cst", [128, self.NCST])
        self.gla_w_in = self.dram_in("gla_w_in", [D, 3088])
        self.gla_w_gk2 = self.dram_in("gla_w_gk2", [16, 512])
        self.gla_wo = self.dram_in("gla_wo", [D, D])
        self.gla_st = self.dram_in("gla_st", [NS, 4, 128, 256])
        self.gla_po = self.dram_out("gla_po", [4, 128, 256])
        self.gla_so = self.dram_out("gla_so", [NS, 4, 128, 256])
        self.rwkv_w_rkv = [self.dram_in("rwkv_w_%s" % k, [D, D]) for k in "rkv"]
        self.rwkv_w1 = self.dram_in("rwkv_w1", [D, 64])
        self.rwkv_w2 = self.dram_in("rwkv_w2", [64, D])
        self.rwkv_a1 = self.dram_in("rwkv_a1", [D, 64])
        self.rwkv_a2 = self.dram_in("rwkv_a2", [64, D])
        self.rwkv_g1 = self.dram_in("rwkv_g1", [D, 160])
        self.rwkv_g2 = self.dram_in("rwkv_g2", [160, D])
        self.rwkv_wo = self.dram_in("rwkv_wo", [D, D])
        self.shift_st = self.dram_in("shift_st", [D, NS])
        self.wkv_st = self.dram_in("wkv_st", [NS, 8, 128, 64])
        self.shift_o = self.dram_out("shift_o", [D, 1 + NS])
        self.wkv_po = self.dram_out("wkv_po", [8, 128, 64])
        self.wkv_so = self.dram_out("wkv_so", [NS, 8, 128, 64])
        with contextlib.ExitStack() as st:
            self.st = st
            S = Sched(nc, st)
            self.S = S
            self.h = self.sb("h", [128, NCH, NT], F32)
            self.ubuf = self.sb("ubuf", [128, NCH, NT + 1 + NS], BF16)
            self.vec = self.sb("vecs", [128, NVEC], F32)
            self.ones_bf = self.sb("ones_bf", [128, 128], BF16)
            self.eps_t = self.sb("eps_t", [128, 4], F32)
            self.sq = self.sb("sq", [128, NCH, 512], BF16)
            self.rstd = self.sb("rstd", [128, 512], F32)
            self.ps = [st.enter_context(nc.psum_tensor("psb%d" % i, [128, 512], F32)) for i in range(7)]
            self.psT = st.enter_context(nc.psum_tensor("psT", [128, 1024], BF16))
            self.cst = self.sb("cst", [128, self.NCST], F32)
            self.identb = self.sb("identb", [128, 128], BF16)

            self.memset("vector", self.ones_bf[:], 1.0, ["ones"])
            self.memset("vector", self.eps_t[:, 0:1], NORM_EPS, ["consts"])
            self.memset("vector", self.eps_t[:, 1:2], 1.0, ["consts"])
            self.memset("vector", self.eps_t[:, 2:3], 1e-5, ["consts"])
            self.memset("vector", self.eps_t[:, 3:4], 64e-5, ["consts"])
            S.dma("sync", "cst", self.cst[:], self.cst_d, writes=["cst"])
            self.cp("vector", self.identb[:, :], self.cst[:, self.C_ID:self.C_ID + 128], ["cst"], ["identb"])

            S.dma("sync", "vec", self.vec[:], self.vec_d, writes=["vec"])
            xv = self.xT.rearrange("(c p) t -> p c t", p=128)
            for ti, (t0, n) in enumerate(self.tiles):
                S.dma("sync", "h%d" % ti, self.h[:, :, t0:t0 + n], xv[:, :, t0:t0 + n], writes=["h%d" % ti])
            S.barrier()
            for i in range(4):
                if self.layers[i]:
                    getattr(self, ["rwkv_layer", "gla_layer", "conv_layer", "pool_layer"][i])(i)
                if self.ffn:
                    self.ffn_layer(i)
            yv = self.yT.rearrange("(c p) t -> p c t", p=128)
            with contextlib.ExitStack() as ph:
                yo = [self.sb("yo%d" % b, [128, NCH, 512], F32, ph) for b in range(2)]
                for ti, (t0, n) in enumerate(self.tiles):
                    b = ti % 2
                    self.norm_tile(ti, "norm_final", lambda c: yo[b][:, c, :n], "yo%d" % b)
                    S.dma("sync", "yo%d" % b, yv[:, :, t0:t0 + n], yo[b][:, :, :n], reads=["yo%d" % b])
                S.barrier()
            with nc.Block() as block:
                S.replay(block)
        return nc


def pack_vec(inp):
    v = np.zeros((128, NVEC), np.float32)

    def put(name, arr):
        a = np.asarray(arr, np.float32).reshape(-1)
        k = a.size // 128
        c0 = VEC_COLS[name]
        v[:, c0:c0 + k] = a.reshape(k, 128).T
    for i in range(4):
        put("norm_mix%d" % i, inp["norm_mix"][i])
        put("norm_ffn%d" % i, inp["norm_ffn"][i])
    put("norm_final", inp["norm_final"])
    for j in range(6):
        put("mu%d" % j, inp["rwkv_mu"][0, j])
    put("w0", inp["rwkv_w0"][0]); put("a0", inp["rwkv_a0"][0]); put("k_k", inp["rwkv_k_k"][0]); put("k_a", inp["rwkv_k_a"][0])
    put("r_k", inp["rwkv_r_k"][0]); put("ln_w", inp["rwkv_ln_w"][0]); put("ln_b", inp["rwkv_ln_b"][0])
    for j in range(3):
        put("conv_w%d" % j, inp["conv_w"][0, j])
    put("pool_scale", inp["pool_scale"][0])
    put("gla_b_gk", inp["gla_b_gk"][0]); put("gla_norm", inp["gla_norm"][0])
    v[:, VEC_COLS["invc"]:VEC_COLS["invc"] + 16] = (1.0 / np.arange(1, 17, dtype=np.float64)).astype(np.float32)[None, :]
    return v


def make_in_maps(inp, L, ncores=8):
    vec = pack_vec(inp)
    maps = []
    for core in range(ncores):
        xp = np.asarray(inp["x_prompt"][core, :L], np.float32)
        xs = np.asarray(inp["x_sample"][core * NS:(core + 1) * NS, 0], np.float32)
        xT = np.ascontiguousarray(np.concatenate([xp, xs], axis=0).T)
        rs = slice(core * NS, (core + 1) * NS)
        m = {"xT": xT, "vec": vec,
             "ffn_up": np.asarray(inp["ffn_up"], np.float32), "ffn_down": np.asarray(inp["ffn_down"], np.float32),
             "conv_w_in": np.asarray(inp["conv_w_in"][0], np.float32), "conv_wo": np.asarray(inp["conv_wo"][0], np.float32),
             "conv_st": np.ascontiguousarray(np.transpose(inp["state_conv"][0, rs], (2, 0, 1))),
             "pool_w": np.asarray(inp["pool_w"][0], np.float32), "cst": make_cst(),
             "gla_w_in": np.asarray(inp["gla_w_in"][0], np.float32), "gla_w_gk2": np.asarray(inp["gla_w_gk2"][0], np.float32),
             "gla_wo": np.asarray(inp["gla_wo"][0], np.float32), "gla_st": np.ascontiguousarray(inp["state_gla"][0, rs]),
             "rwkv_w_r": np.asarray(inp["rwkv_w_rkv"][0, 0], np.float32), "rwkv_w_k": np.asarray(inp["rwkv_w_rkv"][0, 1], np.float32),
             "rwkv_w_v": np.asarray(inp["rwkv_w_rkv"][0, 2], np.float32),
             "rwkv_w1": np.asarray(inp["rwkv_w1"][0], np.float32), "rwkv_w2": np.asarray(inp["rwkv_w2"][0], np.float32),
             "rwkv_a1": np.asarray(inp["rwkv_a1"][0], np.float32), "rwkv_a2": np.asarray(inp["rwkv_a2"][0], np.float32),
             "rwkv_g1": np.asarray(inp["rwkv_g1"][0], np.float32), "rwkv_g2": np.asarray(inp["rwkv_g2"][0], np.float32),
             "rwkv_wo": np.asarray(inp["rwkv_wo"][0], np.float32),
             "shift_st": np.ascontiguousarray(np.asarray(inp["state_rwkv_shift"][0, rs], np.float32).T),
             "wkv_st": np.ascontiguousarray(np.transpose(inp["state_rwkv_wkv"][0, rs], (0, 1, 3, 2)).reshape(NS, 8, 128, 64)),
             "pool_st": np.ascontiguousarray(np.transpose(inp["state_pool"][0, rs], (2, 0, 1)))}
        maps.append(m)
    return maps


_CACHE = {}


def run(inp, L, ncores=8, **kw):
    key = (L, tuple(sorted(kw.items())))
    if key not in _CACHE:
        _CACHE[key] = Builder(L, **kw).build()
    nc = _CACHE[key]
    maps = make_in_maps(inp, L, ncores)
    res = run_bass_kernel_spmd(nc, maps, core_ids=list(range(ncores)))
    return res.results


def gather(res, L):
    outs = {}
    outs["y"] = (np.stack([r["yT"][:, :L].T for r in res], 0),
                 np.concatenate([r["yT"][:, L:].T for r in res], 0)[:, None, :])
    outs["conv"] = (np.stack([r["conv_o"][:, 0:2].T for r in res], 0)[None],
                    np.concatenate([np.transpose(r["conv_o"][:, 2:].reshape(D, NS, 2), (1, 2, 0)) for r in res], 0)[None])
    outs["wkv"] = (np.stack([np.transpose(r["wkv_po"].reshape(16, 64, 64), (0, 2, 1)) for r in res], 0)[None],
                   np.concatenate([np.transpose(r["wkv_so"].reshape(NS, 16, 64, 64), (0, 1, 3, 2)) for r in res], 0)[None])
    outs["shift"] = (np.stack([r["shift_o"][:, 0] for r in res], 0)[None],
                     np.concatenate([r["shift_o"][:, 1:].T for r in res], 0)[None])
    outs["gla"] = (np.stack([r["gla_po"] for r in res], 0)[None], np.concatenate([r["gla_so"] for r in res], 0)[None])
    outs["pool"] = (np.stack([r["pool_o"][:, 0:15].T for r in res], 0)[None],
                    np.concatenate([np.transpose(r["pool_o"][:, 15:].reshape(D, NS, 15), (1, 2, 0)) for r in res], 0)[None])
    return outs


def kernel(**inputs):
    L = 2048
    res = run(inputs, L)
    o = gather(res, L)
    f = lambda a: np.ascontiguousarray(a, dtype=np.float32)
    return (f(o["y"][0]), f(o["y"][1]), f(o["wkv"][0]), f(o["wkv"][1]), f(o["shift"][0]), f(o["shift"][1]),
            f(o["gla"][0]), f(o["gla"][1]), f(o["conv"][0]), f(o["conv"][1]), f(o["pool"][0]), f(o["pool"][1]))
```
